# Optimizing a Trainium2 kernel written in Bass

```python
import numpy as np
import jax
import jax.numpy as jnp
from jax import lax

D_MODEL = 1024
BATCH = 4
SEQ = 4096
DEPTH = 1

NSA_HEADS = 8
NSA_KV_GROUPS = 2
NSA_GROUP = NSA_HEADS // NSA_KV_GROUPS
NSA_HEAD_DIM = 64
ROPE_DIM = NSA_HEAD_DIM // 4
ROPE_THETA = 500000.0
CMP_BLOCK = 32
CMP_STRIDE = 16
CMP_HIDDEN = 256
SLC_BLOCK = 64
SLC_TOPK = 16
WINDOW = 512
Q_BLOCK = 128

RET_HEADS = 4
RET_QK_DIM = 128
RET_V_DIM = 256
RET_CHUNK = 128
RET_ROPE_THETA = 10000.0

D_FF = 2816
CONV_WIDTH = 3

EPS = 1e-6
NEG = -1e30

NSA_Q_W = NSA_HEADS * NSA_HEAD_DIM
NSA_KV_W = NSA_KV_GROUPS * NSA_HEAD_DIM
RET_QK_W = RET_HEADS * RET_QK_DIM
RET_V_W = RET_HEADS * RET_V_DIM
IN_SPLITS = (NSA_Q_W, 6 * NSA_KV_W, 3 * NSA_HEADS, RET_QK_W, RET_QK_W, RET_V_W, RET_V_W, 2 * D_MODEL)
IN_PROJ_W = sum(IN_SPLITS)

kernel_name = 'hybrid_nsa_retention_convffn'


def rms_norm(x, w):
    xf = x.astype(jnp.float32)
    y = xf * lax.rsqrt(jnp.mean(xf * xf, axis=-1, keepdims=True) + EPS)
    return (y * w.astype(jnp.float32)).astype(x.dtype)


def rope(x, rot_dim, theta):
    S = x.shape[1]
    half = rot_dim // 2
    inv = jnp.power(jnp.float32(theta), -jnp.arange(half, dtype=jnp.float32) / half)
    ang = jnp.arange(S, dtype=jnp.float32)[:, None] * inv[None, :]
    cos = jnp.cos(ang)[None, :, None, :]
    sin = jnp.sin(ang)[None, :, None, :]
    xr = x[..., :rot_dim].astype(jnp.float32)
    x1, x2 = xr[..., :half], xr[..., half:]
    rot = jnp.concatenate([x1 * cos - x2 * sin, x1 * sin + x2 * cos], axis=-1).astype(x.dtype)
    return jnp.concatenate([rot, x[..., rot_dim:]], axis=-1)


def masked_softmax(s, mask):
    p = jax.nn.softmax(jnp.where(mask, s, NEG), axis=-1)
    return jnp.where(mask, p, 0.0)


def nsa_attention(q, k_cmp, v_cmp, k_slc, v_slc, k_win, v_win, gates,
                  pe_k, w1_k, w2_k, pe_v, w1_v, w2_v):
    B, S = q.shape[0], q.shape[1]
    H, G, dh = NSA_KV_GROUPS, NSA_GROUP, NSA_HEAD_DIM
    scale = dh ** -0.5
    pos = jnp.arange(S)
    qg = q.reshape(B, S, H, G, dh)

    n_cmp = (S - CMP_BLOCK) // CMP_STRIDE + 1
    cmp_start = np.arange(n_cmp) * CMP_STRIDE
    cmp_idx = cmp_start[:, None] + np.arange(CMP_BLOCK)[None, :]

    def compress(t, pe, w1, w2):
        blk = t[:, cmp_idx] + pe[:, None, :]
        blk = jnp.swapaxes(blk, 2, 3).reshape(B, n_cmp, H, CMP_BLOCK * dh)
        return jax.nn.gelu(blk @ w1) @ w2

    kc = compress(k_cmp, pe_k, w1_k, w2_k)
    vc = compress(v_cmp, pe_v, w1_v, w2_v)
    s_cmp = jnp.einsum('bshgd,bchd->bhgsc', qg, kc).astype(jnp.float32) * scale
    mask_cmp = jnp.asarray(cmp_start + CMP_BLOCK - 1)[None, :] <= pos[:, None]
    p_cmp = masked_softmax(s_cmp, mask_cmp)
    o_cmp = jnp.einsum('bhgsc,bchd->bshgd', p_cmp.astype(vc.dtype), vc)

    n_slc = S // SLC_BLOCK
    slc_start = np.arange(n_slc) * SLC_BLOCK
    overlap = np.clip(np.minimum(cmp_start[:, None] + CMP_BLOCK, slc_start[None, :] + SLC_BLOCK)
                      - np.maximum(cmp_start[:, None], slc_start[None, :]), 0, None) / CMP_BLOCK
    imp = jnp.einsum('bhgsc,cj->bhsj', p_cmp, jnp.asarray(overlap, dtype=jnp.float32))
    cur = (pos // SLC_BLOCK)[:, None]
    jb = jnp.arange(n_slc)[None, :]
    forced = (jb == 0) | (jb == cur) | (jb == cur - 1)
    valid = jb <= cur
    imp = jnp.where(forced, jnp.inf, jnp.where(valid, imp, -jnp.inf))
    top_n = min(SLC_TOPK, n_slc)
    _, sel = lax.top_k(imp, top_n)

    k_blocks = jnp.swapaxes(k_slc.reshape(B, n_slc, SLC_BLOCK, H, dh), 1, 3)
    k_blocks = jnp.swapaxes(k_blocks, 2, 3)
    v_blocks = jnp.swapaxes(jnp.swapaxes(v_slc.reshape(B, n_slc, SLC_BLOCK, H, dh), 1, 3), 2, 3)
    n_q = S // Q_BLOCK
    q_chunks = jnp.moveaxis(qg.reshape(B, n_q, Q_BLOCK, H, G, dh), 1, 0)
    sel_chunks = jnp.moveaxis(sel.reshape(B, H, n_q, Q_BLOCK, top_n), 2, 0)
    pos_chunks = pos.reshape(n_q, Q_BLOCK)
    gather = jax.vmap(jax.vmap(lambda blocks, ix: blocks[ix]))

    def selected_block(args):
        qb, ib, tb = args
        kg = gather(k_blocks, ib)
        vg = gather(v_blocks, ib)
        s = jnp.einsum('bqhgd,bhqnkd->bhgqnk', qb, kg).astype(jnp.float32) * scale
        kpos = ib[..., None] * SLC_BLOCK + jnp.arange(SLC_BLOCK)
        mask = (kpos <= tb[None, None, :, None, None]).reshape(B, H, 1, Q_BLOCK, top_n * SLC_BLOCK)
        p = masked_softmax(s.reshape(B, H, G, Q_BLOCK, top_n * SLC_BLOCK), mask)
        return jnp.einsum('bhgqm,bhqmd->bqhgd', p.astype(vg.dtype),
                          vg.reshape(B, H, Q_BLOCK, top_n * SLC_BLOCK, dh))

    o_slc = lax.map(selected_block, (q_chunks, sel_chunks, pos_chunks))
    o_slc = jnp.moveaxis(o_slc, 0, 1).reshape(B, S, H, G, dh)

    kw = jnp.pad(k_win, ((0, 0), (WINDOW, 0), (0, 0), (0, 0)))
    vw = jnp.pad(v_win, ((0, 0), (WINDOW, 0), (0, 0), (0, 0)))
    band_len = Q_BLOCK + WINDOW
    band = np.arange(n_q)[:, None] * Q_BLOCK + np.arange(band_len)[None, :]
    kb = kw[:, band]
    vb = vw[:, band]
    qb = qg.reshape(B, n_q, Q_BLOCK, H, G, dh)
    s_win = jnp.einsum('bnqhgd,bnkhd->bnhgqk', qb, kb).astype(jnp.float32) * scale
    qi = np.arange(Q_BLOCK)[:, None]
    ki = np.arange(band_len)[None, :]
    k_abs = np.arange(n_q)[:, None, None] * Q_BLOCK + ki[None] - WINDOW
    win_mask = ((ki > qi) & (ki <= qi + WINDOW))[None] & (k_abs >= 0)
    p_win = masked_softmax(s_win, jnp.asarray(win_mask)[None, :, None, None])
    o_win = jnp.einsum('bnhgqk,bnkhd->bnqhgd', p_win.astype(vb.dtype), vb).reshape(B, S, H, G, dh)

    g = gates.reshape(B, S, H, G, 3)
    o = g[..., 0:1] * o_cmp + g[..., 1:2] * o_slc + g[..., 2:3] * o_win
    return o.reshape(B, S, NSA_HEADS * dh)


def retention(q, k, v):
    B, S, H, dk = q.shape
    dv = v.shape[-1]
    C = RET_CHUNK
    n = S // C
    f32 = jnp.float32
    q = rope(q, dk, RET_ROPE_THETA).astype(f32)
    k = rope(k, dk, RET_ROPE_THETA).astype(f32) * (dk ** -0.5)
    v = v.astype(f32)
    log_gamma = jnp.log1p(-jnp.exp2(-5.0 - jnp.arange(H, dtype=f32)))
    idx = jnp.arange(C, dtype=f32)
    diff = idx[:, None] - idx[None, :]
    decay_in = jnp.where(diff >= 0, jnp.exp(log_gamma[:, None, None] * jnp.maximum(diff, 0.0)), 0.0)
    q_decay = jnp.exp(log_gamma[None, :] * (idx[:, None] + 1.0))
    k_decay = jnp.exp(log_gamma[None, :] * (C - 1.0 - idx[:, None]))
    chunk_decay = jnp.exp(log_gamma * C)
    qc = q.reshape(B, n, C, H, dk)
    kc = k.reshape(B, n, C, H, dk)
    vc = v.reshape(B, n, C, H, dv)
    scores = jnp.einsum('bnihd,bnjhd->bnhij', qc, kc) * decay_in
    o_inner = jnp.einsum('bnhij,bnjhe->bnihe', scores, vc)

    def step(state, xs):
        qn, kn, vn = xs
        cross = jnp.einsum('bihd,bhde->bihe', qn, state) * q_decay[None, :, :, None]
        state = state * chunk_decay[None, :, None, None] + jnp.einsum(
            'bjhd,bjhe->bhde', kn * k_decay[None, :, :, None], vn)
        return state, cross

    state0 = jnp.zeros((B, H, dk, dv), f32)
    _, o_cross = lax.scan(step, state0, (jnp.moveaxis(qc, 1, 0), jnp.moveaxis(kc, 1, 0), jnp.moveaxis(vc, 1, 0)))
    o = o_inner + jnp.moveaxis(o_cross, 0, 1)
    return o.reshape(B, S, H, dv)


def head_group_norm(o, w):
    mean = jnp.mean(o, axis=-1, keepdims=True)
    var = jnp.mean(jnp.square(o - mean), axis=-1, keepdims=True)
    return (o - mean) * lax.rsqrt(var + EPS) * w.astype(jnp.float32)


def causal_dwconv(u, w, b):
    S = u.shape[1]
    up = jnp.pad(u, ((0, 0), (CONV_WIDTH - 1, 0), (0, 0)))
    y = b
    for j in range(CONV_WIDTH):
        y = y + w[j] * up[:, j:j + S]
    return y


def setup_inputs(seed: int = 0) -> dict:
    key = jax.random.key(seed)
    ks = jax.random.split(key, 20)
    f32 = jnp.float32
    L = DEPTH

    def nrm(k, shape, fan_in):
        return jax.random.normal(k, shape, f32) * (fan_in ** -0.5)

    def gain(k, shape):
        return 1.0 + 0.02 * jax.random.normal(k, shape, f32)

    return {
        'x': jax.random.normal(ks[0], (BATCH, SEQ, D_MODEL), f32),
        'norm_mix_w': gain(ks[1], (L, D_MODEL)),
        'w_in': nrm(ks[2], (L, D_MODEL, IN_PROJ_W), D_MODEL),
        'cmp_pe_k': 0.02 * jax.random.normal(ks[3], (L, CMP_BLOCK, NSA_HEAD_DIM), f32),
        'cmp_w1_k': nrm(ks[4], (L, CMP_BLOCK * NSA_HEAD_DIM, CMP_HIDDEN), CMP_BLOCK * NSA_HEAD_DIM),
        'cmp_w2_k': nrm(ks[5], (L, CMP_HIDDEN, NSA_HEAD_DIM), CMP_HIDDEN),
        'cmp_pe_v': 0.02 * jax.random.normal(ks[6], (L, CMP_BLOCK, NSA_HEAD_DIM), f32),
        'cmp_w1_v': nrm(ks[7], (L, CMP_BLOCK * NSA_HEAD_DIM, CMP_HIDDEN), CMP_BLOCK * NSA_HEAD_DIM),
        'cmp_w2_v': nrm(ks[8], (L, CMP_HIDDEN, NSA_HEAD_DIM), CMP_HIDDEN),
        'w_nsa_branch': nrm(ks[9], (L, NSA_Q_W, D_MODEL), NSA_Q_W),
        'ret_gn_w': gain(ks[10], (L, RET_HEADS, RET_V_DIM)),
        'w_ret_branch': nrm(ks[11], (L, RET_V_W, D_MODEL), RET_V_W),
        'w_mix_out': nrm(ks[12], (L, D_MODEL, D_MODEL), D_MODEL),
        'norm_ffn_w': gain(ks[13], (L, D_MODEL)),
        'w_ffn_up': nrm(ks[14], (L, D_MODEL, 2 * D_FF), D_MODEL),
        'ffn_conv_w': nrm(ks[15], (L, CONV_WIDTH, 2 * D_FF), CONV_WIDTH),
        'ffn_conv_b': 0.01 * jax.random.normal(ks[16], (L, 2 * D_FF), f32),
        'w_ffn_down': nrm(ks[17], (L, D_FF, D_MODEL), D_FF),
        'norm_final_w': gain(ks[18], (D_MODEL,)),
    }


def reference(x, norm_mix_w, w_in, cmp_pe_k, cmp_w1_k, cmp_w2_k, cmp_pe_v, cmp_w1_v, cmp_w2_v,
              w_nsa_branch, ret_gn_w, w_ret_branch, w_mix_out, norm_ffn_w, w_ffn_up, ffn_conv_w,
              ffn_conv_b, w_ffn_down, norm_final_w):
    B, S, _ = x.shape
    split_at = [int(v) for v in np.cumsum(IN_SPLITS)[:-1]]
    for layer in range(DEPTH):
        h = rms_norm(x, norm_mix_w[layer])
        proj = h @ w_in[layer]
        q_nsa, kv_nsa, g_nsa, q_ret, k_ret, v_ret, g_ret, g_merge = jnp.split(proj, split_at, axis=-1)

        q_nsa = rope(q_nsa.reshape(B, S, NSA_HEADS, NSA_HEAD_DIM), ROPE_DIM, ROPE_THETA)
        kv = kv_nsa.reshape(B, S, 6, NSA_KV_GROUPS, NSA_HEAD_DIM)
        k_cmp, v_cmp = kv[:, :, 0], kv[:, :, 1]
        k_slc = rope(kv[:, :, 2], ROPE_DIM, ROPE_THETA)
        v_slc = kv[:, :, 3]
        k_win = rope(kv[:, :, 4], ROPE_DIM, ROPE_THETA)
        v_win = kv[:, :, 5]
        nsa_gates = jax.nn.sigmoid(g_nsa.reshape(B, S, NSA_HEADS, 3))
        o_a = nsa_attention(q_nsa, k_cmp, v_cmp, k_slc, v_slc, k_win, v_win, nsa_gates,
                            cmp_pe_k[layer], cmp_w1_k[layer], cmp_w2_k[layer],
                            cmp_pe_v[layer], cmp_w1_v[layer], cmp_w2_v[layer])
        y_a = o_a @ w_nsa_branch[layer]

        o_r = retention(q_ret.reshape(B, S, RET_HEADS, RET_QK_DIM),
                        k_ret.reshape(B, S, RET_HEADS, RET_QK_DIM),
                        v_ret.reshape(B, S, RET_HEADS, RET_V_DIM))
        o_r = head_group_norm(o_r, ret_gn_w[layer]) * jax.nn.silu(
            g_ret.reshape(B, S, RET_HEADS, RET_V_DIM).astype(jnp.float32))
        y_b = o_r.astype(x.dtype).reshape(B, S, RET_V_W) @ w_ret_branch[layer]

        gate_a, gate_b = jnp.split(jax.nn.sigmoid(g_merge), 2, axis=-1)
        x = x + (gate_a * y_a + gate_b * y_b) @ w_mix_out[layer]

        h = rms_norm(x, norm_ffn_w[layer])
        u = causal_dwconv(h @ w_ffn_up[layer], ffn_conv_w[layer], ffn_conv_b[layer])
        u_gate, u_val = jnp.split(u, 2, axis=-1)
        x = x + (jax.nn.silu(u_gate) * u_val) @ w_ffn_down[layer]
    return rms_norm(x, norm_final_w)
```

```python
import numpy as np
from contextlib import ExitStack
import concourse.bass as bass
import concourse.mybir as mybir
from concourse.bass_utils import run_bass_kernel_spmd

F32 = mybir.dt.float32
BF = mybir.dt.bfloat16
AF = mybir.ActivationFunctionType
ALU = mybir.AluOpType
AX = mybir.AxisListType

NB = 32
OB0 = 15
NO = 17
EPS = 1e-6
SEM_MAX = 12000
import os
_VAR = os.environ.get('KVAR', '')


class Eng:
    def __init__(self, nc, e, name, is_pe=False):
        self.nc, self.e, self.name, self.is_pe = nc, e, name, is_pe
        self.k = 0
        self.own = set()
        self.waited = {}
        self.new_sem()

    def new_sem(self):
        self.sem = self.nc.alloc_semaphore(f"{self.name}_c{self.k}")
        self.k += 1
        self.count = 0
        self.own.add(self.sem)


class T:
    def __init__(self, t, psum=False):
        self.t = t
        self.w = None
        self.r = {}
        self.psum = psum

    def __getitem__(self, k):
        return self.t[k]


class K:
    def __init__(self):
        nc = bass.Bass("TRN2", target_bir_lowering=False)
        self.nc = nc
        self.PE = Eng(nc, nc.tensor, "pe", True)
        self.ACT = Eng(nc, nc.scalar, "act")
        self.DVE = Eng(nc, nc.vector, "dve")
        self.POOL = Eng(nc, nc.gpsimd, "pool")
        self.SP = Eng(nc, nc.sync, "sp")
        self.dsems = [nc.alloc_semaphore(f"dma{i}") for i in range(20)]
        self.dvals = [0] * 20
        self.di = 0
        self.uid = 0

    def sb(self, stack, shape, dt, name=None):
        self.uid += 1
        t = stack.enter_context(self.nc.sbuf_tensor(f"{name or 't'}_{self.uid}", list(shape), dt))
        return T(t)

    def ps(self, stack, shape, dt, name=None):
        self.uid += 1
        t = stack.enter_context(self.nc.psum_tensor(f"{name or 'p'}_{self.uid}", list(shape), dt))
        return T(t, psum=True)

    def _waits(self, E, outs, ins):
        need = {}

        def add(sem, v):
            if need.get(sem, 0) < v:
                need[sem] = v
        for b in ins:
            if b.w is not None:
                add(*b.w)
            if b.psum:
                for sem, v in b.r.items():
                    if sem not in E.own:
                        add(sem, v)
        for b in outs:
            if b.w is not None:
                add(*b.w)
            for sem, v in b.r.items():
                add(sem, v)
        for sem, v in need.items():
            if E.is_pe and sem in E.own:
                continue
            if E.waited.get(sem, 0) >= v:
                continue
            E.e.wait_ge(sem, v)
            E.waited[sem] = v

    def _mark(self, d, outs, ins):
        sem, v = d
        for b in ins:
            if b.r.get(sem, 0) < v:
                b.r[sem] = v
        for b in outs:
            b.w = d
            b.r = {}

    def op(self, E, fn, outs=(), ins=()):
        self._waits(E, outs, ins)
        if E.count >= SEM_MAX:
            E.new_sem()
        i = fn(E.e)
        E.count += 1
        i.then_inc(E.sem, 1)
        self._mark((E.sem, E.count), outs, ins)

    def dma(self, Q, out_ap, in_ap, outs=(), ins=(), **kw):
        self._waits(Q, outs, ins)
        i = self.di
        self.di = (self.di + 1) % len(self.dsems)
        if self.dvals[i] >= SEM_MAX:
            if Q.waited.get(self.dsems[i], 0) < self.dvals[i]:
                Q.e.wait_ge(self.dsems[i], self.dvals[i])
            self.uid += 1
            self.dsems[i] = self.nc.alloc_semaphore(f"dmax{self.uid}")
            self.dvals[i] = 0
        sem, prev = self.dsems[i], self.dvals[i]
        if prev > 0 and Q.waited.get(sem, 0) < prev:
            Q.e.wait_ge(sem, prev)
            Q.waited[sem] = prev
        Q.e.dma_start(out=out_ap, in_=in_ap, **kw).then_inc(sem, 16)
        self.dvals[i] = prev + 16
        self._mark((sem, prev + 16), outs, ins)
        return (sem, prev + 16)

    def barrier(self):
        engs = [self.PE, self.ACT, self.DVE, self.POOL, self.SP]
        for E in engs:
            for P in engs:
                if P is E or P.count == 0:
                    continue
                if E.waited.get(P.sem, 0) < P.count:
                    E.e.wait_ge(P.sem, P.count)
                    E.waited[P.sem] = P.count
            for sem, v in zip(self.dsems, self.dvals):
                if v > 0 and E.waited.get(sem, 0) < v:
                    E.e.wait_ge(sem, v)
                    E.waited[sem] = v

    def mm(self, out_t, out_ap, lhsT_t, lhsT_ap, rhs_t, rhs_ap, start, stop, skip=False):
        self.op(self.PE, lambda e: e.matmul(out_ap, lhsT=lhsT_ap, rhs=rhs_ap, start=start, stop=stop,
                                            skip_group_check=skip),
                outs=[out_t], ins=[lhsT_t, rhs_t])

    def tr(self, out_t, out_ap, in_t, in_ap, ident_t, ident_ap):
        self.op(self.PE, lambda e: e.transpose(out_ap, in_ap, ident_ap), outs=[out_t], ins=[in_t, ident_t])

    def V(self, fn, outs, ins):
        self.op(self.DVE, fn, outs, ins)

    def A(self, fn, outs, ins):
        self.op(self.ACT, fn, outs, ins)

    def G(self, fn, outs, ins):
        self.op(self.POOL, fn, outs, ins)


class Rot:
    def __init__(self, items):
        self.items = items
        self.i = 0

    def next(self):
        t = self.items[self.i]
        self.i = (self.i + 1) % len(self.items)
        return t


CONST_SPECS = {
    "ident": [128, 128], "nmw": [128, 8], "nfw": [128, 8], "nlw": [128, 1024],
    "ropeN_c": [128, 32, 8], "ropeN_s": [128, 32, 8], "ropeR_c": [128, 32, 64], "ropeR_s": [128, 32, 64],
    "expand": [64, 4096], "ov": [128, 2, 64], "cmpmask": [128, 2, 17, 128],
    "selvalid": [128, 17, 64], "seladd": [128, 17, 64], "selkeep": [128, 17, 64], "lowmask": [128, 128], "upmask": [128, 128],
    "pbias": [128, 1], "decT": [128, 4, 128], "qdecrow": [128, 4, 128], "kdec": [128, 4],
    "gnw": [128, 1024], "convw": [128, 3, 44], "convb": [128, 44], "peTk": [128, 32], "peTv": [128, 32],
}
WEIGHT_SPECS = {
    "w_in": [1024, 6424], "cmp_w1_k": [2048, 256], "cmp_w2_k": [256, 64], "cmp_w1_v": [2048, 256],
    "cmp_w2_v": [256, 64], "w_nsa": [512, 1024], "w_ret": [1024, 1024], "w_mix": [1024, 1024],
    "w_up": [1024, 5632], "w_dn": [2816, 1024],
}
GAMMAS = [1.0 - 2.0 ** (-5.0 - h) for h in range(4)]


class _Stop(Exception):
    pass


def build(stop_after=None, debug=()):
    kb = K()
    live = []
    top = ExitStack()
    try:
        dbg_out = _build_body(kb, top, live, stop_after, debug)
    except _Stop as e:
        dbg_out = e.args[0]
    return finish(kb, top, list(reversed(live)), dbg_out)


def _build_body(kb, top, live, stop_after, debug):
    nc = kb.nc
    PE, ACT, DVE, POOL, SP = kb.PE, kb.ACT, kb.DVE, kb.POOL, kb.SP
    D = {}
    D["xl"] = nc.dram_tensor("xl", [4096, 1024], F32, kind="ExternalInput").ap()
    for n, s in list(CONST_SPECS.items()) + list(WEIGHT_SPECS.items()):
        D[n] = nc.dram_tensor(n, s, F32, kind="ExternalInput").ap()
    out_d = nc.dram_tensor("out", [2048, 1024], F32, kind="ExternalOutput").ap()
    x1d = nc.dram_tensor("x1d", [NO * 128, 1024], F32, kind="Internal").ap()
    dbg_out = {}

    def ck(tag):
        if stop_after == tag:
            kb.barrier()
            for st_ in reversed(live):
                st_.close()
            del live[:]
            raise _Stop(dbg_out)

    def cload(stack, name, dt=F32, eng=None):
        shape = CONST_SPECS[name]
        t = kb.sb(stack, shape, F32, name)
        kb.dma(SP, t[:], D[name], outs=[t])
        if dt == F32:
            return t
        tb = kb.sb(stack, shape, dt, name + "b")
        kb.V(lambda e: e.tensor_copy(out=tb[:], in_=t[:]), [tb], [t])
        return tb

    identf = cload(top, "ident")
    identb = kb.sb(top, [128, 128], BF, "identb")
    kb.V(lambda e: e.tensor_copy(out=identb[:], in_=identf[:]), [identb], [identf])
    nmw = cload(top, "nmw")
    nfw = cload(top, "nfw")
    lowmask = cload(top, "lowmask", BF)
    upmask = cload(top, "upmask", BF)
    pbias = cload(top, "pbias")
    zbias = kb.sb(top, [128, 1], F32, "zbias")
    kb.V(lambda e: e.memset(zbias[:], 0.0), [zbias], [])

    psf = [kb.ps(top, [128, 512], F32, f"psf{i}") for i in range(7)]
    psb = kb.ps(top, [128, 1024], BF, "psb")
    PSS = Rot(psf[0:4])
    PSA = Rot(psf[4:7])
    PSALL = Rot(psf)

    xts = Rot([kb.sb(top, [128, 1024], F32, "xt") for _ in range(2)])
    xns = Rot([kb.sb(top, [128, 1024], BF, "xn") for _ in range(2)])
    hTs = Rot([kb.sb(top, [128, 8, 128], BF, "hT") for _ in range(2)])
    sq_junk = kb.sb(top, [128, 1024], F32, "sqj")
    stat = Rot([kb.sb(top, [128, 4], F32, "stat") for _ in range(4)])
    stg = Rot([kb.sb(top, [128, 2048], F32, "stg") for _ in range(2)])

    def norm_T(src_ap, wcol, want_xt=False):
        xt = xts.next()
        kb.dma(SP, xt[:], src_ap, outs=[xt])
        st = stat.next()
        kb.V(lambda e: e.scalar_tensor_tensor(out=sq_junk[:], in0=xt[:], scalar=1.0, in1=xt[:], op0=ALU.mult,
                                              op1=ALU.mult, accum_out=st[:, 0:1]), [sq_junk, st], [xt])
        kb.V(lambda e: e.tensor_scalar(out=st[:, 1:2], in0=st[:, 0:1], scalar1=1.0 / 1024, scalar2=EPS,
                                       op0=ALU.mult, op1=ALU.add), [st], [st])
        kb.A(lambda e: e.activation(out=st[:, 3:4], in_=st[:, 1:2], func=AF.Sqrt), [st], [st])
        kb.V(lambda e: e.reciprocal(out=st[:, 2:3], in_=st[:, 3:4]), [st], [st])
        xn = xns.next()
        kb.A(lambda e: e.activation(out=xn[:], in_=xt[:], func=AF.Copy, scale=st[:, 2:3]), [xn], [xt, st])
        for c in range(8):
            kb.tr(psb, psb[:, c * 128:(c + 1) * 128], xn, xn[:, c * 128:(c + 1) * 128], identb, identb[:])
        hT = hTs.next()
        kb.V(lambda e: e.tensor_tensor(out=hT[:], in0=psb[:].rearrange("p (c t) -> p c t", c=8),
                                       in1=wcol[:].unsqueeze(2).to_broadcast([128, 8, 128]), op=ALU.mult),
             [hT], [psb, wcol])
        if want_xt:
            return hT, xt, st
        return hT

    def load_w(dst, dst_ap_fn, src3d, kc, ncols, cast_eng=None):
        step = 1 << ((2048 // kc).bit_length() - 1)
        for c0 in range(0, ncols, step):
            n = min(step, ncols - c0)
            s = stg.next()
            sv = s[:, 0:kc * n].rearrange("p (c n) -> p c n", c=kc)
            kb.dma(SP, sv, src3d[:, :, c0:c0 + n], outs=[s])
            kb.G(lambda e: e.tensor_copy(out=dst_ap_fn(c0, n), in_=sv), [dst], [s])

    def dump(name, t, ap, shape):
        if name in debug:
            d = nc.dram_tensor("dbg_" + name, list(shape), ap.dtype, kind="ExternalOutput").ap()
            kb.dma(SP, d, ap, ins=[t])
            dbg_out[name] = d

    w_in3 = D["w_in"].rearrange("(c p) n -> p c n", p=128)
    xl_blk = D["xl"].rearrange("(b p) f -> b p f", p=128)

    mixs = ExitStack()
    live.append(mixs)
    OAT = kb.sb(mixs, [128, 4, NO * 128], BF, "OAT")
    nsa = ExitStack()
    live.append(nsa)
    KE = [kb.sb(nsa, [128, 4096], BF, f"KE{g}") for g in range(2)]
    KW = kb.sb(nsa, [128, 4096], BF, "KW")
    VS = kb.sb(nsa, [128, 32, 2, 66], BF, "VS")
    VW = kb.sb(nsa, [128, 32, 2, 66], BF, "VW")
    KCT = kb.sb(nsa, [128, 256], BF, "KCT")
    VCA = kb.sb(nsa, [128, 2, 2, 128], BF, "VCA")

    kb.V(lambda e: e.memset(VS[:], 1.0), [VS], [])
    kb.V(lambda e: e.memset(VW[:], 1.0), [VW], [])
    kb.V(lambda e: e.memset(KCT[:], 0.0), [KCT], [])
    for g, rows in ((0, slice(64, 128)), (1, slice(0, 64))):
        for c0 in range(0, 4096, 2048):
            s = stg.next()
            kb.dma(SP, s[rows, 0:2048], D["expand"][:, c0:c0 + 2048], outs=[s])
            kb.G(lambda e: e.tensor_copy(out=KE[g][rows, c0:c0 + 2048], in_=s[rows, 0:2048]), [KE[g]], [s])

    ck(0.3)
    with ExitStack() as p1:
        live.append(p1)
        Wkv = kb.sb(p1, [128, 8, 768], BF, "Wkv")
        load_w(Wkv, lambda c0, n: Wkv[:, :, c0:c0 + n], w_in3[:, :, 512:1280], 8, 768)
        W1 = {}
        for nm in ("k", "v"):
            W1[nm] = kb.sb(p1, [128, 32, 256], BF, "W1" + nm)
            src = D["cmp_w1_" + nm].rearrange("(l d) m -> d l m", d=64)
            for half in range(2):
                rows = slice(64 * half, 64 * half + 64)
                for l0 in range(0, 32, 8):
                    s = stg.next()
                    sv = s[rows, :].rearrange("p (l m) -> p l m", l=8)
                    kb.dma(SP, sv, src[:, l0:l0 + 8, :], outs=[s])
                    kb.G(lambda e: e.tensor_copy(out=W1[nm][rows, l0:l0 + 8, :], in_=sv), [W1[nm]], [s])
        W2k = kb.sb(p1, [128, 2, 128], BF, "W2k")
        W2v = kb.sb(p1, [128, 2, 64], BF, "W2v")
        s = stg.next()
        kb.dma(SP, s[:, 0:128].rearrange("p (c d) -> p c d", c=2),
               D["cmp_w2_k"].rearrange("(c p) d -> p c d", p=128), outs=[s])
        kb.dma(SP, s[:, 128:256].rearrange("p (c d) -> p c d", c=2),
               D["cmp_w2_v"].rearrange("(c p) d -> p c d", p=128), outs=[s])
        sk = s[:, 0:128].rearrange("p (c d) -> p c d", c=2)
        kb.G(lambda e: e.tensor_copy(out=W2k[:, :, 0:64], in_=sk), [W2k], [s])
        kb.G(lambda e: e.tensor_copy(out=W2k[:, :, 64:128], in_=sk), [W2k], [s])
        kb.G(lambda e: e.tensor_copy(out=W2v[:], in_=s[:, 128:256].rearrange("p (c d) -> p c d", c=2)), [W2v], [s])
        peT = {"k": cload(p1, "peTk", BF), "v": cload(p1, "peTv", BF)}
        ropeN_c = cload(p1, "ropeN_c")
        ropeN_s = cload(p1, "ropeN_s")
        ovb = cload(p1, "ov", BF)
        for g in range(2):
            for ct in range(2):
                kb.V(lambda e: e.tensor_copy(out=VCA[:, g, ct, 64:128], in_=ovb[:, ct, :]), [VCA], [ovb])
        KC = kb.sb(p1, [128, 4096], BF, "KC")
        VC = kb.sb(p1, [128, 4096], BF, "VC")
        ktin = Rot([kb.sb(p1, [128, 4, 128], BF, "ktin") for _ in range(2)])
        rtmp = Rot([kb.sb(p1, [128, 2, 8], F32, "rtmp") for _ in range(6)])

        def rope_small(dst_t, dst3, src_t, src3, cs, sn, nh):
            cb = cs.unsqueeze(1).to_broadcast([128, nh, 8])
            sb_ = sn.unsqueeze(1).to_broadcast([128, nh, 8])
            x1, x2 = src3[:, :, 0:8], src3[:, :, 8:16]
            return cb, sb_, x1, x2

        ck(0.5)
        if stop_after == 0.55:
            hT = norm_T(xl_blk[17], nmw)
            dump("hT", hT, hT[:], [128, 8, 128])
            ck(0.55)
        for blk in range(NB if not (isinstance(stop_after, float) and 0.6 <= stop_after < 0.7) else 1):
            hT = norm_T(xl_blk[blk], nmw)
            pa, pb = PSS.next(), PSS.next()
            for c in range(8):
                kb.mm(pa, pa[:, 0:512], hT, hT[:, c, :], Wkv, Wkv[:, c, 0:512], c == 0, c == 7)
            for c in range(8):
                kb.mm(pb, pb[:, 0:256], hT, hT[:, c, :], Wkv, Wkv[:, c, 512:768], c == 0, c == 7)
            ck(0.61)
            kt = ktin.next()
            kb.A(lambda e: e.copy(out=kt[:, 0:2, :], in_=pa[:, 0:256].rearrange("p (a d) -> p a d", a=2)), [kt], [pa])
            kb.A(lambda e: e.copy(out=VS[:, blk, :, 0:64], in_=pa[:, 384:512].rearrange("p (g d) -> p g d", g=2)),
                 [VS], [pa])
            kb.A(lambda e: e.copy(out=VW[:, blk, :, 0:64], in_=pb[:, 128:256].rearrange("p (g d) -> p g d", g=2)),
                 [VW], [pb])
            ck(0.62)
            for slot, (pt, lo) in ((2, (pa, 256)), (3, (pb, 0))):
                src3 = pt[:, lo:lo + 128].rearrange("p (g d) -> p g d", g=2)
                dst3 = kt[:, slot, :].rearrange("p (g d) -> p g d", g=2)
                cb = ropeN_c[:, blk, :].unsqueeze(1).to_broadcast([128, 2, 8])
                sb_ = ropeN_s[:, blk, :].unsqueeze(1).to_broadcast([128, 2, 8])
                x1, x2 = src3[:, :, 0:8], src3[:, :, 8:16]
                t1, t2, t3, t4 = rtmp.next(), rtmp.next(), rtmp.next(), rtmp.next()
                kb.V(lambda e: e.tensor_tensor(out=t1[:], in0=x1, in1=cb, op=ALU.mult), [t1], [pt, ropeN_c])
                kb.V(lambda e: e.tensor_tensor(out=t2[:], in0=x2, in1=sb_, op=ALU.mult), [t2], [pt, ropeN_s])
                kb.V(lambda e: e.tensor_tensor(out=t3[:], in0=x1, in1=sb_, op=ALU.mult), [t3], [pt, ropeN_s])
                kb.V(lambda e: e.tensor_tensor(out=t4[:], in0=x2, in1=cb, op=ALU.mult), [t4], [pt, ropeN_c])
                kb.V(lambda e: e.tensor_tensor(out=dst3[:, :, 0:8], in0=t1[:], in1=t2[:], op=ALU.subtract), [kt], [t1, t2])
                kb.V(lambda e: e.tensor_tensor(out=dst3[:, :, 8:16], in0=t3[:], in1=t4[:], op=ALU.add), [kt], [t3, t4])
                if "noactcopy" in _VAR:
                    kb.V(lambda e: e.tensor_copy(out=dst3[:, :, 16:64], in_=src3[:, :, 16:64]), [kt], [pt])
                else:
                    kb.A(lambda e: e.copy(out=dst3[:, :, 16:64], in_=src3[:, :, 16:64]), [kt], [pt])
            ck(0.63)
            for slot in range(4):
                kb.tr(psb, psb[:, slot * 128:(slot + 1) * 128], kt, kt[:, slot, :], identb, identb[:])
            ck(0.64)
            cs = slice(blk * 128, (blk + 1) * 128)
            kb.V(lambda e: e.tensor_copy(out=KC[:, cs], in_=psb[:, 0:128]), [KC], [psb])
            kb.V(lambda e: e.tensor_copy(out=VC[:, cs], in_=psb[:, 128:256]), [VC], [psb])
            kb.V(lambda e: e.tensor_copy(out=KE[0][0:64, cs], in_=psb[0:64, 256:384]), [KE[0]], [psb])
            kb.V(lambda e: e.tensor_copy(out=KE[1][64:128, cs], in_=psb[64:128, 256:384]), [KE[1]], [psb])
            kb.V(lambda e: e.tensor_copy(out=KW[:, cs], in_=psb[:, 384:512]), [KW], [psb])

        dump("KE0", KE[0], KE[0][:], [128, 4096])
        dump("KE1", KE[1], KE[1][:], [128, 4096])
        dump("KW", KW, KW[:], [128, 4096])
        dump("KC", KC, KC[:], [128, 4096])
        dump("VS", VS, VS[:], [128, 32, 2, 66])

        ck(0.6)
        ck(0.7)
        GT = kb.sb(p1, [128, 2, 256], BF, "GT")
        hb = kb.sb(p1, [128, 1], F32, "hb")
        gx = [kb.sb(p1, [128, 255], F32, f"gx{i}") for i in range(4)]
        for nm, src in (("k", KC), ("v", VC)):
            for g in range(2):
                rows = slice(64 * g, 64 * g + 64)
                kb.V(lambda e: e.memset(GT[:], 0.0), [GT], [])
                for mc in range(2):
                    ph, pbi = PSS.next(), PSS.next()
                    for l in range(32):
                        kb.mm(ph, ph[:, 0:255], W1[nm], W1[nm][rows, l, mc * 128:(mc + 1) * 128],
                              src, src[rows, l:l + 16 * 254 + 1:16], l == 0, l == 31)
                    for l in range(32):
                        kb.mm(pbi, pbi[:, 0:1], W1[nm], W1[nm][rows, l, mc * 128:(mc + 1) * 128],
                              peT[nm], peT[nm][rows, l:l + 1], l == 0, l == 31)
                    kb.V(lambda e: e.tensor_copy(out=hb[:], in_=pbi[:, 0:1]), [hb], [pbi])
                    x0, x2_, x3_, sg = gx
                    kb.V(lambda e: e.tensor_scalar(out=x0[:], in0=ph[:, 0:255], scalar1=hb[:, 0:1], scalar2=None,
                                                   op0=ALU.add), [x0], [ph, hb])
                    kb.V(lambda e: e.tensor_tensor(out=x2_[:], in0=x0[:], in1=x0[:], op=ALU.mult), [x2_], [x0])
                    kb.V(lambda e: e.tensor_scalar(out=x2_[:], in0=x2_[:], scalar1=0.044715, scalar2=1.0,
                                                   op0=ALU.mult, op1=ALU.add), [x2_], [x2_])
                    kb.V(lambda e: e.tensor_tensor(out=x3_[:], in0=x2_[:], in1=x0[:], op=ALU.mult), [x3_], [x2_, x0])
                    kb.A(lambda e: e.activation(out=sg[:], in_=x3_[:], func=AF.Sigmoid, scale=1.5957691216057308),
                         [sg], [x3_])
                    kb.V(lambda e: e.tensor_tensor(out=GT[:, mc, 0:255], in0=x0[:], in1=sg[:], op=ALU.mult),
                         [GT], [x0, sg])
                if nm == "k":
                    pk = PSS.next()
                    for mc in range(2):
                        kb.mm(pk, pk[:, 0:255], W2k, W2k[:, mc, :], GT, GT[:, mc, 0:255], mc == 0, mc == 1)
                    kb.V(lambda e: e.tensor_copy(out=KCT[rows, 0:255], in_=pk[rows, 0:255]), [KCT], [pk])
                else:
                    for ct in range(2):
                        pv = PSS.next()
                        for mc in range(2):
                            kb.mm(pv, pv[:, 0:64], GT, GT[:, mc, ct * 128:(ct + 1) * 128], W2v, W2v[:, mc, :],
                                  mc == 0, mc == 1)
                        kb.V(lambda e: e.tensor_copy(out=VCA[:, g, ct, 0:64], in_=pv[:, 0:64]), [VCA], [pv])
        dump("KCT", KCT, KCT[:], [128, 256])
        dump("VCA", VCA, VCA[:], [128, 2, 2, 128])
        kb.barrier()
        live.remove(p1)
    ck(1)

    with ExitStack() as p2:
        live.append(p2)
        Wq = kb.sb(p2, [128, 8, 536], BF, "Wq")
        load_w(Wq, lambda c0, n: Wq[:, :, c0:c0 + n], w_in3[:, :, 0:512], 8, 512)
        load_w(Wq, lambda c0, n: Wq[:, :, 512 + c0:512 + c0 + n], w_in3[:, :, 1280:1304], 8, 24)
        ropeN_c = cload(p2, "ropeN_c")
        ropeN_s = cload(p2, "ropeN_s")
        cmpmask = cload(p2, "cmpmask", BF)
        selvalid = cload(p2, "selvalid")
        seladd = cload(p2, "seladd")
        selkeep = cload(p2, "selkeep")
        QB = [Rot([kb.sb(p2, [128, 512], BF, f"QB{g}") for _ in range(2)]) for g in range(2)]
        qr = Rot([kb.sb(p2, [128, 4, 128], BF, "qr") for _ in range(2)])
        gat = Rot([kb.sb(p2, [128, 24], F32, "gat") for _ in range(2)])
        rt = Rot([kb.sb(p2, [128, 8, 8], F32, "rt") for _ in range(6)])
        PTs = Rot([kb.sb(p2, [128, 512], BF, "PT") for _ in range(4)])
        sm = Rot([kb.sb(p2, [128, 16], F32, "sm") for _ in range(12)])
        impt = Rot([kb.sb(p2, [128, 64], F32, "imp") for _ in range(8)])
        m8 = Rot([kb.sb(p2, [128, 8], F32, "m8") for _ in range(4)])
        ZS = [kb.sb(p2, [128, 128], BF, f"ZS{g}") for g in range(2)]
        for g in range(2):
            kb.V(lambda e: e.memset(ZS[g][:], 0.0), [ZS[g]], [])
        oacc = Rot([kb.sb(p2, [128, 8, 64], F32, "oacc") for _ in range(2)])
        otmp = Rot([kb.sb(p2, [128, 4, 64], F32, "otmp") for _ in range(3)])
        oab = Rot([kb.sb(p2, [128, 512], BF, "oab") for _ in range(2)])

        def coef_from(rs_ap, rs_t, gate_ap, gate_t):
            c = sm.next()
            kb.V(lambda e: e.tensor_scalar(out=c[:, 0:4], in0=rs_ap, scalar1=1e-30, scalar2=None, op0=ALU.max),
                 [c], [rs_t])
            kb.V(lambda e: e.reciprocal(out=c[:, 4:8], in_=c[:, 0:4]), [c], [c])
            kb.V(lambda e: e.tensor_tensor(out=c[:, 8:12], in0=c[:, 4:8], in1=gate_ap, op=ALU.mult), [c], [c, gate_t])
            return c

        def accumulate(oa, g, first, ps_t, o_view, coef):
            cb = coef[:, 8:12].unsqueeze(2).to_broadcast([128, 4, 64])
            if first:
                kb.V(lambda e: e.tensor_tensor(out=oa[:, 4 * g:4 * g + 4, :], in0=o_view, in1=cb, op=ALU.mult),
                     [oa], [ps_t, coef])
            else:
                tm = otmp.next()
                kb.V(lambda e: e.tensor_tensor(out=tm[:], in0=o_view, in1=cb, op=ALU.mult), [tm], [ps_t, coef])
                kb.V(lambda e: e.tensor_tensor(out=oa[:, 4 * g:4 * g + 4, :], in0=oa[:, 4 * g:4 * g + 4, :],
                                               in1=tm[:], op=ALU.add), [oa], [oa, tm])

        for o in range(NO):
            blk = OB0 + o
            hT = norm_T(xl_blk[blk], nmw)
            pq, pg = PSS.next(), PSS.next()
            for c in range(8):
                kb.mm(pq, pq[:, 0:512], hT, hT[:, c, :], Wq, Wq[:, c, 0:512], c == 0, c == 7)
            for c in range(8):
                kb.mm(pg, pg[:, 0:24], hT, hT[:, c, :], Wq, Wq[:, c, 512:536], c == 0, c == 7)
            ga = gat.next()
            kb.A(lambda e: e.activation(out=ga[:], in_=pg[:, 0:24], func=AF.Sigmoid), [ga], [pg])
            gav = ga[:].rearrange("p (h b) -> p h b", b=3)
            q_ = qr.next()
            src4 = pq[:, 0:512].rearrange("p (j hp d) -> p j hp d", j=2, hp=4)
            dst4 = q_[:].rearrange("p hp (j d) -> p j hp d", j=2)
            cb = ropeN_c[:, blk, :].unsqueeze(1).unsqueeze(1).to_broadcast([128, 2, 4, 8])
            sb_ = ropeN_s[:, blk, :].unsqueeze(1).unsqueeze(1).to_broadcast([128, 2, 4, 8])
            x1, x2 = src4[:, :, :, 0:8], src4[:, :, :, 8:16]
            t1, t2, t3, t4 = rt.next(), rt.next(), rt.next(), rt.next()
            tv = lambda t: t[:].rearrange("p (j hp) d -> p j hp d", j=2)
            kb.V(lambda e: e.tensor_tensor(out=tv(t1), in0=x1, in1=cb, op=ALU.mult), [t1], [pq, ropeN_c])
            kb.V(lambda e: e.tensor_tensor(out=tv(t2), in0=x2, in1=sb_, op=ALU.mult), [t2], [pq, ropeN_s])
            kb.V(lambda e: e.tensor_tensor(out=tv(t3), in0=x1, in1=sb_, op=ALU.mult), [t3], [pq, ropeN_s])
            kb.V(lambda e: e.tensor_tensor(out=tv(t4), in0=x2, in1=cb, op=ALU.mult), [t4], [pq, ropeN_c])
            kb.V(lambda e: e.tensor_tensor(out=dst4[:, :, :, 0:8], in0=tv(t1), in1=tv(t2), op=ALU.subtract), [q_], [t1, t2])
            kb.V(lambda e: e.tensor_tensor(out=dst4[:, :, :, 8:16], in0=tv(t3), in1=tv(t4), op=ALU.add), [q_], [t3, t4])
            kb.A(lambda e: e.copy(out=dst4[:, :, :, 16:64], in_=src4[:, :, :, 16:64]), [q_], [pq])
            for h in range(4):
                kb.tr(psb, psb[:, h * 128:(h + 1) * 128], q_, q_[:, h, :], identb, identb[:])
            qb = [QB[0].next(), QB[1].next()]
            kb.V(lambda e: e.tensor_copy(out=qb[0][0:64, :], in_=psb[0:64, 0:512]), [qb[0]], [psb])
            kb.V(lambda e: e.tensor_copy(out=qb[1][64:128, :], in_=psb[64:128, 0:512]), [qb[1]], [psb])
            oa = oacc.next()

            for g in range(2):
                rows = slice(64 * g, 64 * g + 64)
                pso = PSA.next()
                for ct in range(2):
                    pS = PSS.next()
                    kb.mm(pS, pS[:, :], KCT, KCT[rows, ct * 128:(ct + 1) * 128], qb[g], qb[g][rows, :], True, True)
                    pt = PTs.next()
                    kb.A(lambda e: e.activation(out=pt[:], in_=pS[:], func=AF.Exp, scale=0.125), [pt], [pS])
                    kb.V(lambda e: e.tensor_tensor(
                        out=pt[:].rearrange("p (h q) -> p h q", h=4), in0=pt[:].rearrange("p (h q) -> p h q", h=4),
                        in1=cmpmask[:, ct, o, :].unsqueeze(1).to_broadcast([128, 4, 128]), op=ALU.mult),
                        [pt], [pt, cmpmask])
                    for h in range(4):
                        kb.mm(pso, pso[:, h * 128:(h + 1) * 128], pt, pt[:, h * 128:(h + 1) * 128],
                              VCA, VCA[:, g, ct, :], ct == 0 and h == 0, ct == 1 and h == 3, skip=True)
                pv4 = pso[:].rearrange("p (h x) -> p h x", h=4)
                rs = sm.next()
                kb.V(lambda e: e.tensor_reduce(out=rs[:, 0:4], in_=pv4[:, :, 64:128], axis=AX.X, op=ALU.add),
                     [rs], [pso])
                cf = coef_from(rs[:, 0:4], rs, gav[:, 4 * g:4 * g + 4, 0], ga)
                accumulate(oa, g, True, pso, pv4[:, :, 0:64], cf)
                im = impt.next()
                kb.V(lambda e: e.tensor_scalar(out=im[:], in0=pv4[:, 0, 64:128], scalar1=cf[:, 4:5], scalar2=None,
                                               op0=ALU.mult), [im], [pso, cf])
                for h in range(1, 4):
                    kb.V(lambda e: e.scalar_tensor_tensor(out=im[:], in0=pv4[:, h, 64:128], scalar=cf[:, 4 + h:5 + h],
                                                          in1=im[:], op0=ALU.mult, op1=ALU.add), [im], [pso, cf, im])
                imf = impt.next()
                kb.V(lambda e: e.tensor_tensor(out=imf[:], in0=im[:], in1=selvalid[:, o, :], op=ALU.mult),
                     [imf], [im, selvalid])
                kb.V(lambda e: e.tensor_tensor(out=imf[:], in0=imf[:], in1=seladd[:, o, :], op=ALU.add),
                     [imf], [imf, seladd])
                ma, mb = m8.next(), m8.next()
                im2 = impt.next()
                kb.V(lambda e: e.max(out=ma[:], in_=imf[:]), [ma], [imf])
                kb.V(lambda e: e.match_replace(out=im2[:], in_to_replace=ma[:], in_values=imf[:], imm_value=-1e9),
                     [im2], [ma, imf])
                kb.V(lambda e: e.max(out=mb[:], in_=im2[:]), [mb], [im2])
                sel = impt.next()
                kb.V(lambda e: e.tensor_scalar(out=sel[:], in0=imf[:], scalar1=mb[:, 7:8], scalar2=None,
                                               op0=ALU.is_ge), [sel], [imf, mb])
                kb.V(lambda e: e.tensor_tensor(out=sel[:], in0=sel[:], in1=selkeep[:, o, :], op=ALU.mult),
                     [sel], [sel, selkeep])
                zc = slice(64, 128) if g == 0 else slice(0, 64)
                kb.V(lambda e: e.tensor_scalar(out=ZS[g][:, zc], in0=sel[:], scalar1=1.0, scalar2=1e5,
                                               op0=ALU.subtract, op1=ALU.mult), [ZS[g]], [sel])
                if g == 0 and o == 1:
                    dump("imf", imf, imf[:], [128, 64])
                    dump("sel", sel, sel[:], [128, 64])
                    dump("oa_cmp", oa, oa[:], [128, 8, 64])
                kb.tr(psb, psb[:, 512 + g * 128:512 + (g + 1) * 128], ZS[g], ZS[g][:], identb, identb[:])
                kb.V(lambda e: e.tensor_copy(
                    out=qb[g][zc, :].rearrange("p (h q) -> p h q", h=4),
                    in_=psb[zc, 512 + g * 128:512 + (g + 1) * 128].unsqueeze(1).to_broadcast([64, 4, 128])),
                    [qb[g]], [psb])

            for g in range(2):
                rows = slice(64 * g, 64 * g + 64)
                psw = PSA.next()
                for i, kt in enumerate(range(blk - 4, blk + 1)):
                    pS = PSS.next()
                    kb.mm(pS, pS[:, :], KW, KW[rows, kt * 128:(kt + 1) * 128], qb[g], qb[g][rows, :], True, True)
                    pt = PTs.next()
                    bt = pbias if kt < 16 else zbias
                    kb.A(lambda e: e.activation(out=pt[:], in_=pS[:], func=AF.Exp, scale=0.125, bias=bt[:, 0:1]),
                         [pt], [pS, bt])
                    if i in (0, 4):
                        mk = upmask if i == 0 else lowmask
                        kb.V(lambda e: e.tensor_tensor(
                            out=pt[:].rearrange("p (h q) -> p h q", h=4),
                            in0=pt[:].rearrange("p (h q) -> p h q", h=4),
                            in1=mk[:].unsqueeze(1).to_broadcast([128, 4, 128]), op=ALU.mult), [pt], [pt, mk])
                    for h in range(4):
                        kb.mm(psw, psw[:, h * 66:(h + 1) * 66], pt, pt[:, h * 128:(h + 1) * 128],
                              VW, VW[:, kt, g, :], i == 0 and h == 0, i == 4 and h == 3, skip=True)
                pv4 = psw[:, 0:264].rearrange("p (h x) -> p h x", h=4)
                cf = coef_from(pv4[:, :, 64], psw, gav[:, 4 * g:4 * g + 4, 2], ga)
                accumulate(oa, g, False, psw, pv4[:, :, 0:64], cf)

            for g in range(2):
                pss = PSA.next()
                for kt in range(blk + 1):
                    pS = PSS.next()
                    kb.mm(pS, pS[:, :], KE[g], KE[g][:, kt * 128:(kt + 1) * 128], qb[g], qb[g][:, :], True, True)
                    pt = PTs.next()
                    kb.A(lambda e: e.activation(out=pt[:], in_=pS[:], func=AF.Exp, scale=0.125), [pt], [pS])
                    if kt == blk:
                        kb.V(lambda e: e.tensor_tensor(
                            out=pt[:].rearrange("p (h q) -> p h q", h=4),
                            in0=pt[:].rearrange("p (h q) -> p h q", h=4),
                            in1=lowmask[:].unsqueeze(1).to_broadcast([128, 4, 128]), op=ALU.mult), [pt], [pt, lowmask])
                    for h in range(4):
                        kb.mm(pss, pss[:, h * 66:(h + 1) * 66], pt, pt[:, h * 128:(h + 1) * 128],
                              VS, VS[:, kt, g, :], kt == 0 and h == 0, kt == blk and h == 3, skip=True)
                pv4 = pss[:, 0:264].rearrange("p (h x) -> p h x", h=4)
                cf = coef_from(pv4[:, :, 64], pss, gav[:, 4 * g:4 * g + 4, 1], ga)
                accumulate(oa, g, False, pss, pv4[:, :, 0:64], cf)

            ob = oab.next()
            kb.V(lambda e: e.tensor_copy(out=ob[:], in_=oa[:].rearrange("p h d -> p (h d)")), [ob], [oa])
            if o == 1:
                dump("oa", oa, oa[:], [128, 8, 64])
            for c in range(4):
                kb.tr(psb, psb[:, c * 128:(c + 1) * 128], ob, ob[:, c * 128:(c + 1) * 128], identb, identb[:])
            kb.V(lambda e: e.tensor_copy(out=OAT[:, :, o * 128:(o + 1) * 128],
                                         in_=psb[:, 0:512].rearrange("p (c t) -> p c t", c=4)), [OAT], [psb])
        kb.barrier()
        live.remove(p2)
    nsa.close()
    live.remove(nsa)
    ck(2)

    ORT = kb.sb(mixs, [128, 8, NO * 128], BF, "ORT")
    with ExitStack() as p3:
        live.append(p3)
        Wr = kb.sb(p3, [128, 8, 3072], BF, "Wr")
        load_w(Wr, lambda c0, n: Wr[:, :, c0:c0 + n], w_in3[:, :, 1304:4376], 8, 3072)
        rtc = Rot([kb.sb(p3, [128, 64], F32, "rtc") for _ in range(2)])
        rts = Rot([kb.sb(p3, [128, 64], F32, "rts") for _ in range(2)])
        cur_rt = {}
        decT = cload(p3, "decT")
        qdecrow = cload(p3, "qdecrow")
        kdec = cload(p3, "kdec")
        gnw = cload(p3, "gnw")
        ST = kb.sb(p3, [128, 4, 256], F32, "ST")
        STb = kb.sb(p3, [128, 4, 256], BF, "STb")
        kb.V(lambda e: e.memset(ST[:], 0.0), [ST], [])
        kb.V(lambda e: e.memset(STb[:], 0.0), [STb], [])
        rr = Rot([kb.sb(p3, [128, 4, 64], F32, "rr") for _ in range(4)])
        krs = Rot([kb.sb(p3, [128, 4, 128], F32, "kr") for _ in range(3)])
        KDs = Rot([kb.sb(p3, [128, 512], BF, "KD") for _ in range(2)])
        VBs = Rot([kb.sb(p3, [128, 1024], BF, "VB") for _ in range(2)])
        qkb = Rot([kb.sb(p3, [128, 2, 512], BF, "qkb") for _ in range(2)])
        QKT = Rot([kb.sb(p3, [128, 8, 128], BF, "QKT") for _ in range(2)])
        QTd = Rot([kb.sb(p3, [128, 4, 128], BF, "QTd") for _ in range(2)])
        SCs = Rot([kb.sb(p3, [128, 4, 128], BF, "SC") for _ in range(2)])
        sgs = Rot([kb.sb(p3, [128, 1024], F32, "sg") for _ in range(1)])
        ys = Rot([kb.sb(p3, [128, 1024], F32, "y") for _ in range(1)])
        orb = Rot([kb.sb(p3, [128, 1024], BF, "orb") for _ in range(2)])
        bst = Rot([kb.sb(p3, [128, 4, 6], F32, "bst") for _ in range(2)])
        mv = Rot([kb.sb(p3, [128, 4, 4], F32, "mv") for _ in range(2)])

        def rope_full(dst3, dst_t, pt, blk):
            src3 = pt[:, 0:512].rearrange("p (h d) -> p h d", h=4)
            if cur_rt.get("blk") != blk:
                ropeR_c, ropeR_s = rtc.next(), rts.next()
                kb.dma(SP, ropeR_c[:], D["ropeR_c"][:, blk, :], outs=[ropeR_c])
                kb.dma(SP, ropeR_s[:], D["ropeR_s"][:, blk, :], outs=[ropeR_s])
                cur_rt.update(blk=blk, c=ropeR_c, s=ropeR_s)
            ropeR_c, ropeR_s = cur_rt["c"], cur_rt["s"]
            cb = ropeR_c[:].unsqueeze(1).to_broadcast([128, 4, 64])
            sb_ = ropeR_s[:].unsqueeze(1).to_broadcast([128, 4, 64])
            x1, x2 = src3[:, :, 0:64], src3[:, :, 64:128]
            t1, t2 = rr.next(), rr.next()
            kb.V(lambda e: e.tensor_tensor(out=t1[:], in0=x1, in1=cb, op=ALU.mult), [t1], [pt, ropeR_c])
            kb.V(lambda e: e.tensor_tensor(out=t2[:], in0=x2, in1=sb_, op=ALU.mult), [t2], [pt, ropeR_s])
            kb.V(lambda e: e.tensor_tensor(out=dst3[:, :, 0:64], in0=t1[:], in1=t2[:], op=ALU.subtract),
                 [dst_t], [t1, t2])
            t3, t4 = rr.next(), rr.next()
            kb.V(lambda e: e.tensor_tensor(out=t3[:], in0=x1, in1=sb_, op=ALU.mult), [t3], [pt, ropeR_s])
            kb.V(lambda e: e.tensor_tensor(out=t4[:], in0=x2, in1=cb, op=ALU.mult), [t4], [pt, ropeR_c])
            kb.V(lambda e: e.tensor_tensor(out=dst3[:, :, 64:128], in0=t3[:], in1=t4[:], op=ALU.add),
                 [dst_t], [t3, t4])

        for blk in range(NB):
            o = blk - OB0
            hT = norm_T(xl_blk[blk], nmw)
            pk = PSALL.next()
            for c in range(8):
                kb.mm(pk, pk[:, :], hT, hT[:, c, :], Wr, Wr[:, c, 512:1024], c == 0, c == 7)
            kr = krs.next()
            rope_full(kr[:], kr, pk, blk)
            KD = KDs.next()
            kb.V(lambda e: e.tensor_tensor(out=KD[:].rearrange("p (h d) -> p h d", h=4), in0=kr[:],
                                           in1=kdec[:].unsqueeze(2).to_broadcast([128, 4, 128]), op=ALU.mult),
                 [KD], [kr, kdec])
            VB = VBs.next()
            for n in range(2):
                pvv = PSALL.next()
                for c in range(8):
                    kb.mm(pvv, pvv[:, :], hT, hT[:, c, :], Wr, Wr[:, c, 1024 + n * 512:1536 + n * 512], c == 0, c == 7)
                kb.A(lambda e: e.copy(out=VB[:, n * 512:(n + 1) * 512], in_=pvv[:]), [VB], [pvv])
            if o >= 0:
                pq = PSALL.next()
                for c in range(8):
                    kb.mm(pq, pq[:, :], hT, hT[:, c, :], Wr, Wr[:, c, 0:512], c == 0, c == 7)
                qrf = krs.next()
                rope_full(qrf[:], qrf, pq, blk)
                qk = qkb.next()
                kb.V(lambda e: e.tensor_copy(out=qk[:, 0, :].rearrange("p (h d) -> p h d", h=4), in_=qrf[:]), [qk], [qrf])
                kb.V(lambda e: e.tensor_copy(out=qk[:, 1, :].rearrange("p (h d) -> p h d", h=4), in_=kr[:]), [qk], [kr])
                sg = sgs.next()
                for n in range(2):
                    pgg = PSALL.next()
                    for c in range(8):
                        kb.mm(pgg, pgg[:, :], hT, hT[:, c, :], Wr, Wr[:, c, 2048 + n * 512:2560 + n * 512],
                              c == 0, c == 7)
                    kb.A(lambda e: e.activation(out=sg[:, n * 512:(n + 1) * 512], in_=pgg[:], func=AF.Silu), [sg], [pgg])
                for j in range(8):
                    kb.tr(psb, psb[:, j * 128:(j + 1) * 128], qk, qk[:, j // 4, (j % 4) * 128:(j % 4 + 1) * 128],
                          identb, identb[:])
                qkt = QKT.next()
                kb.V(lambda e: e.tensor_copy(out=qkt[:], in_=psb[:].rearrange("p (j t) -> p j t", j=8)), [qkt], [psb])
                qtd = QTd.next()
                kb.V(lambda e: e.tensor_tensor(out=qtd[:], in0=qkt[:, 0:4, :], in1=qdecrow[:], op=ALU.mult),
                     [qtd], [qkt, qdecrow])
                psc = PSALL.next()
                for h in range(4):
                    kb.mm(psc, psc[:, h * 128:(h + 1) * 128], qkt, qkt[:, 4 + h, :], qkt, qkt[:, h, :], True, True)
                sc = SCs.next()
                kb.V(lambda e: e.tensor_tensor(out=sc[:], in0=psc[:].rearrange("p (h i) -> p h i", h=4), in1=decT[:],
                                               op=ALU.mult), [sc], [psc, decT])
                y = ys.next()
                bs, mvv = bst.next(), mv.next()
                for n in range(2):
                    po = PSALL.next()
                    for hh in range(2):
                        h = 2 * n + hh
                        kb.mm(po, po[:, hh * 256:(hh + 1) * 256], sc, sc[:, h, :], VB, VB[:, h * 256:(h + 1) * 256],
                              True, False)
                        kb.mm(po, po[:, hh * 256:(hh + 1) * 256], qtd, qtd[:, h, :], STb, STb[:, h, :], False, True)
                    for hh in range(2):
                        h = 2 * n + hh
                        kb.V(lambda e: e.bn_stats(out=bs[:, h, :], in_=po[:, hh * 256:(hh + 1) * 256]), [bs], [po])
                        kb.V(lambda e: e.bn_aggr(out=mvv[:, h, 0:2], in_=bs[:, h, :]), [mvv], [bs])
                        kb.V(lambda e: e.tensor_scalar(out=mvv[:, h, 3:4], in0=mvv[:, h, 1:2], scalar1=EPS, scalar2=None,
                                                       op0=ALU.add), [mvv], [mvv])
                        kb.A(lambda e: e.activation(out=mvv[:, h, 3:4], in_=mvv[:, h, 3:4], func=AF.Sqrt), [mvv], [mvv])
                        kb.V(lambda e: e.reciprocal(out=mvv[:, h, 2:3], in_=mvv[:, h, 3:4]), [mvv], [mvv])
                        kb.V(lambda e: e.tensor_scalar(out=y[:, h * 256:(h + 1) * 256], in0=po[:, hh * 256:(hh + 1) * 256],
                                                       scalar1=mvv[:, h, 0:1], scalar2=mvv[:, h, 2:3],
                                                       op0=ALU.subtract, op1=ALU.mult), [y], [po, mvv])
                if o == 1:
                    dump("ret_y", y, y[:], [128, 1024])
                kb.V(lambda e: e.tensor_tensor(out=y[:], in0=y[:], in1=gnw[:], op=ALU.mult), [y], [y, gnw])
                ob = orb.next()
                kb.V(lambda e: e.tensor_tensor(out=ob[:], in0=y[:], in1=sg[:], op=ALU.mult), [ob], [y, sg])
                for c in range(8):
                    kb.tr(psb, psb[:, c * 128:(c + 1) * 128], ob, ob[:, c * 128:(c + 1) * 128], identb, identb[:])
                kb.V(lambda e: e.tensor_copy(out=ORT[:, :, o * 128:(o + 1) * 128],
                                             in_=psb[:].rearrange("p (c t) -> p c t", c=8)), [ORT], [psb])
            if blk < NB - 1:
                for n in range(2):
                    pu = PSALL.next()
                    for hh in range(2):
                        h = 2 * n + hh
                        kb.mm(pu, pu[:, hh * 256:(hh + 1) * 256], KD, KD[:, h * 128:(h + 1) * 128],
                              VB, VB[:, h * 256:(h + 1) * 256], True, True)
                    for hh in range(2):
                        h = 2 * n + hh
                        kb.V(lambda e: e.scalar_tensor_tensor(out=ST[:, h, :], in0=ST[:, h, :], scalar=GAMMAS[h] ** 128,
                                                              in1=pu[:, hh * 256:(hh + 1) * 256], op0=ALU.mult,
                                                              op1=ALU.add), [ST], [ST, pu])
                kb.V(lambda e: e.tensor_copy(out=STb[:], in_=ST[:]), [STb], [ST])
        dump("ORT", ORT, ORT[:], [128, 8, NO * 128])
        kb.barrier()
        live.remove(p3)
    ck(3)

    x1_blk = x1d.rearrange("(b p) f -> b p f", p=128)
    with ExitStack() as p4:
        live.append(p4)
        Wgm = kb.sb(p4, [128, 8, 2048], BF, "Wgm")
        load_w(Wgm, lambda c0, n: Wgm[:, :, c0:c0 + n], w_in3[:, :, 4376:6424], 8, 2048)
        Wnsa = kb.sb(p4, [128, 4, 1024], BF, "Wnsa")
        load_w(Wnsa, lambda c0, n: Wnsa[:, :, c0:c0 + n], D["w_nsa"].rearrange("(c p) n -> p c n", p=128), 4, 1024)
        Wret = kb.sb(p4, [128, 8, 1024], BF, "Wret")
        load_w(Wret, lambda c0, n: Wret[:, :, c0:c0 + n], D["w_ret"].rearrange("(c p) n -> p c n", p=128), 8, 1024)
        Wmix = kb.sb(p4, [128, 8, 1024], BF, "Wmix")
        load_w(Wmix, lambda c0, n: Wmix[:, :, c0:c0 + n], D["w_mix"].rearrange("(c p) n -> p c n", p=128), 8, 1024)
        GMs = Rot([kb.sb(p4, [128, 16, 128], F32, "GM") for _ in range(2)])
        MTs = Rot([kb.sb(p4, [128, 8, 128], BF, "MT") for _ in range(2)])
        mt1 = Rot([kb.sb(p4, [128, 512], F32, "mt1") for _ in range(2)])
        mt2 = Rot([kb.sb(p4, [128, 512], F32, "mt2") for _ in range(2)])
        x1s = Rot([kb.sb(p4, [128, 1024], F32, "x1") for _ in range(2)])
        for o in range(NO):
            blk = OB0 + o
            hT, xt, _ = norm_T(xl_blk[blk], nmw, want_xt=True)
            gm = GMs.next()
            for q4 in range(4):
                pgm = PSALL.next()
                for j in range(4):
                    mc = q4 * 4 + j
                    for c in range(8):
                        kb.mm(pgm, pgm[:, j * 128:(j + 1) * 128], Wgm, Wgm[:, c, mc * 128:(mc + 1) * 128],
                              hT, hT[:, c, :], c == 0, c == 7)
                kb.A(lambda e: e.activation(out=gm[:, q4 * 4:q4 * 4 + 4, :].rearrange("p a t -> p (a t)"),
                                            in_=pgm[:], func=AF.Sigmoid), [gm], [pgm])
            mt = MTs.next()
            for half in range(2):
                pya, pyb = PSALL.next(), PSALL.next()
                for j in range(4):
                    fc = half * 4 + j
                    for c in range(4):
                        kb.mm(pya, pya[:, j * 128:(j + 1) * 128], Wnsa, Wnsa[:, c, fc * 128:(fc + 1) * 128],
                              OAT, OAT[:, c, o * 128:(o + 1) * 128], c == 0, c == 3)
                    for c in range(8):
                        kb.mm(pyb, pyb[:, j * 128:(j + 1) * 128], Wret, Wret[:, c, fc * 128:(fc + 1) * 128],
                              ORT, ORT[:, c, o * 128:(o + 1) * 128], c == 0, c == 7)
                a1, a2 = mt1.next(), mt2.next()
                kb.V(lambda e: e.tensor_tensor(out=a1[:], in0=pya[:],
                                               in1=gm[:, half * 4:half * 4 + 4, :].rearrange("p a t -> p (a t)"),
                                               op=ALU.mult), [a1], [pya, gm])
                kb.V(lambda e: e.tensor_tensor(out=a2[:], in0=pyb[:],
                                               in1=gm[:, 8 + half * 4:12 + half * 4, :].rearrange("p a t -> p (a t)"),
                                               op=ALU.mult), [a2], [pyb, gm])
                kb.V(lambda e: e.tensor_tensor(out=mt[:, half * 4:half * 4 + 4, :].rearrange("p a t -> p (a t)"),
                                               in0=a1[:], in1=a2[:], op=ALU.add), [mt], [a1, a2])
            x1 = x1s.next()
            for n in range(2):
                pm = PSALL.next()
                for fc in range(8):
                    kb.mm(pm, pm[:, :], mt, mt[:, fc, :], Wmix, Wmix[:, fc, n * 512:(n + 1) * 512], fc == 0, fc == 7)
                kb.V(lambda e: e.tensor_tensor(out=x1[:, n * 512:(n + 1) * 512], in0=pm[:],
                                               in1=xt[:, n * 512:(n + 1) * 512], op=ALU.add), [x1], [pm, xt])
            kb.dma(SP, x1_blk[o], x1[:], ins=[x1])
            if o == 1:
                dump("x1", x1, x1[:], [128, 1024])
        kb.barrier()
        live.remove(p4)
    mixs.close()
    live.remove(mixs)
    ck(4)

    with ExitStack() as p5:
        live.append(p5)
        Wup = kb.sb(p5, [128, 8, 5632], BF, "Wup")
        load_w(Wup, lambda c0, n: Wup[:, :, c0:c0 + n], D["w_up"].rearrange("(c p) n -> p c n", p=128), 8, 5632)
        Wdn = kb.sb(p5, [128, 22, 1024], BF, "Wdn")
        load_w(Wdn, lambda c0, n: Wdn[:, :, c0:c0 + n], D["w_dn"].rearrange("(c p) n -> p c n", p=128), 22, 1024)
        convw = cload(p5, "convw")
        convb = cload(p5, "convb")
        nlw = cload(p5, "nlw")
        uh = kb.sb(p5, [128, 44, 2], F32, "uh")
        kb.V(lambda e: e.memset(uh[:], 0.0), [uh], [])
        uts = Rot([kb.sb(p5, [128, 130], F32, "ut") for _ in range(4)])
        cvs = Rot([kb.sb(p5, [128, 128], F32, "cv") for _ in range(4)])
        cvg = Rot([kb.sb(p5, [128, 128], F32, "cvg") for _ in range(3)])
        ATs = Rot([kb.sb(p5, [128, 22, 128], BF, "AT") for _ in range(2)])
        x2s = Rot([kb.sb(p5, [128, 1024], F32, "x2") for _ in range(1)])
        ys5 = Rot([kb.sb(p5, [128, 1024], F32, "y5") for _ in range(2)])
        out_blk = out_d.rearrange("(b p) f -> b p f", p=128)
        for o in range(NO):
            hT, xt, _ = norm_T(x1_blk[o], nfw, want_xt=True)
            at = ATs.next()

            def conv_chunk(ch):
                pu = PSS.next()
                for c in range(8):
                    kb.mm(pu, pu[:, 0:128], Wup, Wup[:, c, ch * 128:(ch + 1) * 128], hT, hT[:, c, :], c == 0, c == 7)
                ut = uts.next()
                kb.A(lambda e: e.copy(out=ut[:, 2:130], in_=pu[:, 0:128]), [ut], [pu])
                kb.V(lambda e: e.tensor_copy(out=ut[:, 0:2], in_=uh[:, ch, :]), [ut], [uh])
                kb.V(lambda e: e.tensor_copy(out=uh[:, ch, :], in_=ut[:, 128:130]), [uh], [ut])
                cv = cvs.next()
                kb.V(lambda e: e.tensor_scalar(out=cv[:], in0=ut[:, 2:130], scalar1=convw[:, 2, ch:ch + 1],
                                               scalar2=convb[:, ch:ch + 1], op0=ALU.mult, op1=ALU.add),
                     [cv], [ut, convw, convb])
                kb.V(lambda e: e.scalar_tensor_tensor(out=cv[:], in0=ut[:, 1:129], scalar=convw[:, 1, ch:ch + 1],
                                                      in1=cv[:], op0=ALU.mult, op1=ALU.add), [cv], [ut, convw, cv])
                kb.V(lambda e: e.scalar_tensor_tensor(out=cv[:], in0=ut[:, 0:128], scalar=convw[:, 0, ch:ch + 1],
                                                      in1=cv[:], op0=ALU.mult, op1=ALU.add), [cv], [ut, convw, cv])
                return cv

            for c22 in range(22):
                cg_ = conv_chunk(c22)
                sgt = cvg.next()
                kb.A(lambda e: e.activation(out=sgt[:], in_=cg_[:], func=AF.Silu), [sgt], [cg_])
                cvv = conv_chunk(22 + c22)
                kb.V(lambda e: e.tensor_tensor(out=at[:, c22, :], in0=sgt[:], in1=cvv[:], op=ALU.mult), [at], [sgt, cvv])
            if o == 0:
                continue
            x2 = x2s.next()
            for n in range(2):
                pd = PSA.next()
                for c in range(22):
                    kb.mm(pd, pd[:, :], at, at[:, c, :], Wdn, Wdn[:, c, n * 512:(n + 1) * 512], c == 0, c == 21)
                kb.V(lambda e: e.tensor_tensor(out=x2[:, n * 512:(n + 1) * 512], in0=pd[:],
                                               in1=xt[:, n * 512:(n + 1) * 512], op=ALU.add), [x2], [pd, xt])
            st = stat.next()
            kb.V(lambda e: e.scalar_tensor_tensor(out=sq_junk[:], in0=x2[:], scalar=1.0, in1=x2[:], op0=ALU.mult,
                                                  op1=ALU.mult, accum_out=st[:, 0:1]), [sq_junk, st], [x2])
            kb.V(lambda e: e.tensor_scalar(out=st[:, 1:2], in0=st[:, 0:1], scalar1=1.0 / 1024, scalar2=EPS,
                                           op0=ALU.mult, op1=ALU.add), [st], [st])
            kb.A(lambda e: e.activation(out=st[:, 3:4], in_=st[:, 1:2], func=AF.Sqrt), [st], [st])
            kb.V(lambda e: e.reciprocal(out=st[:, 2:3], in_=st[:, 3:4]), [st], [st])
            y5 = ys5.next()
            kb.V(lambda e: e.scalar_tensor_tensor(out=y5[:], in0=x2[:], scalar=st[:, 2:3], in1=nlw[:], op0=ALU.mult,
                                                  op1=ALU.mult), [y5], [x2, st, nlw])
            kb.dma(SP, out_blk[o - 1], y5[:], ins=[y5])
        kb.barrier()
        live.remove(p5)
    return dbg_out


def finish(kb, top, stacks, dbg_out):
    kb.barrier()
    for s in stacks:
        s.close()
    top.close()
    return kb.nc, dbg_out


def _consts(s):
    f = np.float32
    c = {}
    c["ident"] = np.eye(128, dtype=f)
    tloc = np.arange(4096)
    pos = np.where(tloc < 2048, tloc, 2048 * s + tloc - 2048).astype(np.float64)
    if s == 0:
        pos[:2048] = 0.0

    def rope_tab(theta, half):
        inv = np.power(np.float32(theta), -np.arange(half, dtype=np.float32) / half).astype(np.float32)
        ang = pos.astype(np.float32)[:, None] * inv[None, :]
        cs, sn = np.cos(ang).astype(f), np.sin(ang).astype(f)
        to = lambda a: np.ascontiguousarray(a.reshape(32, 128, half).transpose(1, 0, 2))
        return to(cs), to(sn)
    c["ropeN_c"], c["ropeN_s"] = rope_tab(500000.0, 8)
    c["ropeR_c"], c["ropeR_s"] = rope_tab(10000.0, 64)
    c["expand"] = (np.arange(4096)[None, :] // 64 == np.arange(64)[:, None]).astype(f)
    cst = np.arange(256) * 16
    sst = np.arange(64) * 64
    ov = np.clip(np.minimum(cst[:, None] + 32, sst[None, :] + 64) - np.maximum(cst[:, None], sst[None, :]), 0, None) / 32.0
    ov[255] = 0.0
    c["ov"] = np.ascontiguousarray(ov.reshape(2, 128, 64).transpose(1, 0, 2)).astype(f)
    lo_c = 0 if s == 1 else 128
    lo_j = 0 if s == 1 else 32
    tq = (OB0 * 128 + np.arange(NO * 128))
    cm = (16 * np.arange(256)[:, None] + 31 <= tq[None, :]) & (np.arange(256)[:, None] >= lo_c) & (np.arange(256)[:, None] < 255)
    c["cmpmask"] = np.ascontiguousarray(cm.reshape(2, 128, NO, 128).transpose(1, 0, 2, 3)).astype(f)
    cur = tq // 64
    jb = np.arange(64)[None, :]
    valid = (jb <= cur[:, None]) & (jb >= lo_j)
    forced = ((jb == lo_j) | (jb == cur[:, None]) | (jb == cur[:, None] - 1)) & valid
    add = np.where(forced, 1e4 + jb, np.where(valid, 0.0, -1e4 - jb))
    vm = (valid & ~forced)
    c["selvalid"] = np.ascontiguousarray(vm.reshape(NO, 128, 64).transpose(1, 0, 2)).astype(f)
    c["seladd"] = np.ascontiguousarray(add.reshape(NO, 128, 64).transpose(1, 0, 2)).astype(f)
    c["selkeep"] = np.ascontiguousarray(valid.reshape(NO, 128, 64).transpose(1, 0, 2)).astype(f)
    k_, q_ = np.arange(128)[:, None], np.arange(128)[None, :]
    c["lowmask"] = (k_ <= q_).astype(f)
    c["upmask"] = (k_ > q_).astype(f)
    c["pbias"] = np.full((128, 1), 0.0 if s == 1 else -30000.0, dtype=f)
    g = np.array(GAMMAS, dtype=np.float64)
    i_ = np.arange(128)
    dk = 128 ** -0.5
    dec = np.where(i_[None, :] >= i_[:, None], g[:, None, None] ** np.maximum(i_[None, :] - i_[:, None], 0)[None], 0.0)
    c["decT"] = np.ascontiguousarray((dec * dk).transpose(1, 0, 2)).astype(f)
    c["qdecrow"] = np.ascontiguousarray(np.broadcast_to((g[:, None] ** (i_[None, :] + 1.0))[None], (128, 4, 128))).astype(f)
    c["kdec"] = np.ascontiguousarray((g[None, :] ** (127.0 - i_[:, None])) * dk).astype(f)
    return c


def kernel(x, norm_mix_w, w_in, cmp_pe_k, cmp_w1_k, cmp_w2_k, cmp_pe_v, cmp_w1_v, cmp_w2_v,
           w_nsa_branch, ret_gn_w, w_ret_branch, w_mix_out, norm_ffn_w, w_ffn_up, ffn_conv_w,
           ffn_conv_b, w_ffn_down, norm_final_w, _build_kwargs=None, _return_all=False):
    f = np.float32
    A = lambda a: np.ascontiguousarray(np.asarray(a, dtype=f))
    x = A(x)
    shared = {
        "w_in": A(w_in)[0], "cmp_w1_k": A(cmp_w1_k)[0], "cmp_w2_k": A(cmp_w2_k)[0], "cmp_w1_v": A(cmp_w1_v)[0],
        "cmp_w2_v": A(cmp_w2_v)[0], "w_nsa": A(w_nsa_branch)[0], "w_ret": A(w_ret_branch)[0], "w_mix": A(w_mix_out)[0],
        "w_up": A(w_ffn_up)[0], "w_dn": A(w_ffn_down)[0],
        "nmw": np.ascontiguousarray(A(norm_mix_w)[0].reshape(8, 128).T),
        "nfw": np.ascontiguousarray(A(norm_ffn_w)[0].reshape(8, 128).T),
        "nlw": np.ascontiguousarray(np.broadcast_to(A(norm_final_w)[None, :], (128, 1024))),
        "gnw": np.ascontiguousarray(np.broadcast_to(A(ret_gn_w)[0].reshape(1, 1024), (128, 1024))),
        "convw": np.ascontiguousarray(A(ffn_conv_w)[0].reshape(3, 44, 128).transpose(2, 0, 1)),
        "convb": np.ascontiguousarray(A(ffn_conv_b)[0].reshape(44, 128).T),
        "peTk": np.ascontiguousarray(np.concatenate([A(cmp_pe_k)[0].T] * 2, axis=0)),
        "peTv": np.ascontiguousarray(np.concatenate([A(cmp_pe_v)[0].T] * 2, axis=0)),
    }
    consts = [_consts(0), _consts(1)]
    nc, dbg = build(**(_build_kwargs or {}))
    in_maps = []
    for core in range(8):
        b, s = core // 2, core % 2
        xl = np.zeros((4096, 1024), dtype=f)
        if s == 1:
            xl[:] = x[b]
        else:
            xl[2048:] = x[b, :2048]
        m = dict(shared)
        m["xl"] = xl
        cs = consts[s]
        for k_, v in cs.items():
            if not k_.startswith("_"):
                m[k_] = v
        in_maps.append(m)
    res = run_bass_kernel_spmd(nc, in_maps, core_ids=list(range(8)))
    if _return_all:
        return res
    out = np.zeros((4, 4096, 1024), dtype=f)
    for core in range(8):
        b, s = core // 2, core % 2
        out[b, 2048 * s:2048 * (s + 1)] = res.results[core]["out"]
    return out
```

```python
import numpy as np
from contextlib import ExitStack
import concourse.bass as bass
import concourse.mybir as mybir
from concourse.bass_utils import run_bass_kernel_spmd

F32 = mybir.dt.float32
BF = mybir.dt.bfloat16
AF = mybir.ActivationFunctionType
ALU = mybir.AluOpType
AX = mybir.AxisListType

NB = 32
OB0 = 15
NO = 17
EPS = 1e-6
SEM_MAX = 12000
import os
_VAR = os.environ.get('KVAR', '')


class Eng:
    def __init__(self, nc, e, name, is_pe=False):
        self.nc, self.e, self.name, self.is_pe = nc, e, name, is_pe
        self.k = 0
        self.own = set()
        self.waited = {}
        self.new_sem()

    def new_sem(self):
        self.sem = self.nc.alloc_semaphore(f"{self.name}_c{self.k}")
        self.k += 1
        self.count = 0
        self.own.add(self.sem)


class T:
    def __init__(self, t, psum=False):
        self.t = t
        self.w = None
        self.r = {}
        self.psum = psum

    def __getitem__(self, k):
        return self.t[k]


class K:
    def __init__(self):
        nc = bass.Bass("TRN2", target_bir_lowering=False)
        self.nc = nc
        self.PE = Eng(nc, nc.tensor, "pe", True)
        self.ACT = Eng(nc, nc.scalar, "act")
        self.DVE = Eng(nc, nc.vector, "dve")
        self.POOL = Eng(nc, nc.gpsimd, "pool")
        self.SP = Eng(nc, nc.sync, "sp")
        self.dsems = [nc.alloc_semaphore(f"dma{i}") for i in range(20)]
        self.dvals = [0] * 20
        self.di = 0
        self.uid = 0

    def sb(self, stack, shape, dt, name=None):
        self.uid += 1
        t = stack.enter_context(self.nc.sbuf_tensor(f"{name or 't'}_{self.uid}", list(shape), dt))
        return T(t)

    def ps(self, stack, shape, dt, name=None):
        self.uid += 1
        t = stack.enter_context(self.nc.psum_tensor(f"{name or 'p'}_{self.uid}", list(shape), dt))
        return T(t, psum=True)

    def _waits(self, E, outs, ins):
        need = {}

        def add(sem, v):
            if need.get(sem, 0) < v:
                need[sem] = v
        for b in ins:
            if b.w is not None:
                add(*b.w)
            if b.psum:
                for sem, v in b.r.items():
                    if sem not in E.own:
                        add(sem, v)
        for b in outs:
            if b.w is not None:
                add(*b.w)
            for sem, v in b.r.items():
                add(sem, v)
        for sem, v in need.items():
            if E.is_pe and sem in E.own:
                continue
            if E.waited.get(sem, 0) >= v:
                continue
            E.e.wait_ge(sem, v)
            E.waited[sem] = v

    def _mark(self, d, outs, ins):
        sem, v = d
        for b in ins:
            if b.r.get(sem, 0) < v:
                b.r[sem] = v
        for b in outs:
            b.w = d
            b.r = {}

    def op(self, E, fn, outs=(), ins=()):
        self._waits(E, outs, ins)
        if E.count >= SEM_MAX:
            E.new_sem()
        i = fn(E.e)
        E.count += 1
        i.then_inc(E.sem, 1)
        self._mark((E.sem, E.count), outs, ins)

    def dma(self, Q, out_ap, in_ap, outs=(), ins=(), **kw):
        self._waits(Q, outs, ins)
        i = self.di
        self.di = (self.di + 1) % len(self.dsems)
        if self.dvals[i] >= SEM_MAX:
            if Q.waited.get(self.dsems[i], 0) < self.dvals[i]:
                Q.e.wait_ge(self.dsems[i], self.dvals[i])
            self.uid += 1
            self.dsems[i] = self.nc.alloc_semaphore(f"dmax{self.uid}")
            self.dvals[i] = 0
        sem, prev = self.dsems[i], self.dvals[i]
        if prev > 0 and Q.waited.get(sem, 0) < prev:
            Q.e.wait_ge(sem, prev)
            Q.waited[sem] = prev
        Q.e.dma_start(out=out_ap, in_=in_ap, **kw).then_inc(sem, 16)
        self.dvals[i] = prev + 16
        self._mark((sem, prev + 16), outs, ins)
        return (sem, prev + 16)

    def barrier(self):
        engs = [self.PE, self.ACT, self.DVE, self.POOL, self.SP]
        for E in engs:
            for P in engs:
                if P is E or P.count == 0:
                    continue
                if E.waited.get(P.sem, 0) < P.count:
                    E.e.wait_ge(P.sem, P.count)
                    E.waited[P.sem] = P.count
            for sem, v in zip(self.dsems, self.dvals):
                if v > 0 and E.waited.get(sem, 0) < v:
                    E.e.wait_ge(sem, v)
                    E.waited[sem] = v

    def mm(self, out_t, out_ap, lhsT_t, lhsT_ap, rhs_t, rhs_ap, start, stop, skip=False, extra=()):
        self.op(self.PE, lambda e: e.matmul(out_ap, lhsT=lhsT_ap, rhs=rhs_ap, start=start, stop=stop,
                                            skip_group_check=skip),
                outs=[out_t], ins=[lhsT_t, rhs_t] + list(extra))

    def tr(self, out_t, out_ap, in_t, in_ap, ident_t, ident_ap):
        self.op(self.PE, lambda e: e.transpose(out_ap, in_ap, ident_ap), outs=[out_t], ins=[in_t, ident_t])

    def V(self, fn, outs, ins):
        self.op(self.DVE, fn, outs, ins)

    def A(self, fn, outs, ins):
        self.op(self.ACT, fn, outs, ins)

    def G(self, fn, outs, ins):
        self.op(self.POOL, fn, outs, ins)


class Rot:
    def __init__(self, items):
        self.items = items
        self.i = 0

    def next(self):
        t = self.items[self.i]
        self.i = (self.i + 1) % len(self.items)
        return t


CONST_SPECS = {
    "ident": [128, 128], "nmw": [128, 8], "nfw": [128, 8], "nlw": [128, 1024],
    "ropeN_c": [128, 32, 8], "ropeN_s": [128, 32, 8], "ropeR_c": [128, 32, 64], "ropeR_s": [128, 32, 64],
    "expand": [64, 4096], "ov": [128, 2, 64], "cmpmask": [128, 2, 17, 128],
    "selvalid": [128, 17, 64], "seladd": [128, 17, 64], "selkeep": [128, 17, 64], "lowmask": [128, 128], "upmask": [128, 128],
    "pbias": [128, 1], "decT": [128, 4, 128], "qdecrow": [128, 4, 128], "kdec": [128, 4],
    "gnw": [128, 1024], "convw": [128, 3, 44], "convb": [128, 44], "peTk": [128, 32], "peTv": [128, 32],
}
WEIGHT_SPECS = {
    "w_in": [1024, 6424], "cmp_w1_k": [2048, 256], "cmp_w2_k": [256, 64], "cmp_w1_v": [2048, 256],
    "cmp_w2_v": [256, 64], "w_nsa": [512, 1024], "w_ret": [1024, 1024], "w_mix": [1024, 1024],
    "w_up": [1024, 5632], "w_dn": [2816, 1024],
}
GAMMAS = [1.0 - 2.0 ** (-5.0 - h) for h in range(4)]


class _Stop(Exception):
    pass


def build(stop_after=None, debug=()):
    kb = K()
    live = []
    top = ExitStack()
    try:
        dbg_out = _build_body(kb, top, live, stop_after, debug)
    except _Stop as e:
        dbg_out = e.args[0]
    return finish(kb, top, list(reversed(live)), dbg_out)


def _build_body(kb, top, live, stop_after, debug):
    nc = kb.nc
    PE, ACT, DVE, POOL, SP = kb.PE, kb.ACT, kb.DVE, kb.POOL, kb.SP
    D = {}
    D["xl"] = nc.dram_tensor("xl", [4096, 1024], F32, kind="ExternalInput").ap()
    for n, s in list(CONST_SPECS.items()) + list(WEIGHT_SPECS.items()):
        D[n] = nc.dram_tensor(n, s, F32, kind="ExternalInput").ap()
    out_d = nc.dram_tensor("out", [2048, 1024], F32, kind="ExternalOutput").ap()
    x1d = nc.dram_tensor("x1d", [NO * 128, 1024], F32, kind="Internal").ap()
    dbg_out = {}

    def ck(tag):
        if stop_after == tag:
            kb.barrier()
            for st_ in reversed(live):
                st_.close()
            del live[:]
            raise _Stop(dbg_out)

    def cload(stack, name, dt=F32, eng=None):
        shape = CONST_SPECS[name]
        t = kb.sb(stack, shape, F32, name)
        kb.dma(SP, t[:], D[name], outs=[t])
        if dt == F32:
            return t
        tb = kb.sb(stack, shape, dt, name + "b")
        kb.V(lambda e: e.tensor_copy(out=tb[:], in_=t[:]), [tb], [t])
        return tb

    identf = cload(top, "ident")
    identb = kb.sb(top, [128, 128], BF, "identb")
    kb.V(lambda e: e.tensor_copy(out=identb[:], in_=identf[:]), [identb], [identf])
    nmw = cload(top, "nmw")
    nfw = cload(top, "nfw")
    lowmask = cload(top, "lowmask", BF)
    upmask = cload(top, "upmask", BF)
    pbias = cload(top, "pbias")
    zbias = kb.sb(top, [128, 1], F32, "zbias")
    kb.V(lambda e: e.memset(zbias[:], 0.0), [zbias], [])

    psf = [kb.ps(top, [128, 512], F32, f"psf{i}") for i in range(7)]
    psb = kb.ps(top, [128, 1024], BF, "psb")
    PSS = Rot(psf[0:4])
    PSA = Rot(psf[4:7])
    PSALL = Rot(psf)

    xts = Rot([kb.sb(top, [128, 1024], F32, "xt") for _ in range(2)])
    xns = Rot([kb.sb(top, [128, 1024], BF, "xn") for _ in range(2)])
    hTs = Rot([kb.sb(top, [128, 8, 128], BF, "hT") for _ in range(2)])
    sq_junk = kb.sb(top, [128, 1024], BF, "sqj")
    stat = Rot([kb.sb(top, [128, 4], F32, "stat") for _ in range(4)])

    def norm_T(src_ap, wcol, want_xt=False, dst=None):
        xt = xts.next()
        kb.dma(SP, xt[:], src_ap, outs=[xt])
        st = stat.next()
        kb.V(lambda e: e.scalar_tensor_tensor(out=sq_junk[:], in0=xt[:], scalar=1.0, in1=xt[:], op0=ALU.mult,
                                              op1=ALU.mult, accum_out=st[:, 0:1]), [sq_junk, st], [xt])
        kb.V(lambda e: e.tensor_scalar(out=st[:, 1:2], in0=st[:, 0:1], scalar1=1.0 / 1024, scalar2=EPS,
                                       op0=ALU.mult, op1=ALU.add), [st], [st])
        kb.A(lambda e: e.activation(out=st[:, 3:4], in_=st[:, 1:2], func=AF.Sqrt), [st], [st])
        kb.V(lambda e: e.reciprocal(out=st[:, 2:3], in_=st[:, 3:4]), [st], [st])
        xn = xns.next()
        kb.A(lambda e: e.activation(out=xn[:], in_=xt[:], func=AF.Copy, scale=st[:, 2:3]), [xn], [xt, st])
        for c in range(8):
            kb.tr(psb, psb[:, c * 128:(c + 1) * 128], xn, xn[:, c * 128:(c + 1) * 128], identb, identb[:])
        if dst is None:
            hT = hTs.next()
            hT_ap = hT[:]
        else:
            hT, hT_ap = dst
        kb.V(lambda e: e.tensor_tensor(out=hT_ap, in0=psb[:].rearrange("p (c t) -> p c t", c=8),
                                       in1=wcol[:].unsqueeze(2).to_broadcast([128, 8, 128]), op=ALU.mult),
             [hT], [psb, wcol])
        if want_xt:
            return hT, xt, st
        return hT

    class Staging:
        def __enter__(self):
            self.ws = ExitStack()
            live.append(self.ws)
            self.rot = Rot([kb.sb(self.ws, [128, 2048], F32, "stg") for _ in range(2)])
            return self.rot

        def __exit__(self, *a):
            kb.barrier()
            live.remove(self.ws)
            self.ws.close()
            return False

    def load_w(stg, dst, dst_ap_fn, src3d, kc, ncols):
        step = 1 << ((2048 // kc).bit_length() - 1)
        for c0 in range(0, ncols, step):
            n = min(step, ncols - c0)
            s = stg.next()
            sv = s[:, 0:kc * n].rearrange("p (c n) -> p c n", c=kc)
            kb.dma(SP, sv, src3d[:, :, c0:c0 + n], outs=[s])
            kb.G(lambda e: e.tensor_copy(out=dst_ap_fn(c0, n), in_=sv), [dst], [s])

    def dump(name, t, ap, shape):
        if name in debug:
            d = nc.dram_tensor("dbg_" + name, list(shape), ap.dtype, kind="ExternalOutput").ap()
            kb.dma(SP, d, ap, ins=[t])
            dbg_out[name] = d

    w_in3 = D["w_in"].rearrange("(c p) n -> p c n", p=128)
    xl_blk = D["xl"].rearrange("(b p) f -> b p f", p=128)

    mixs = ExitStack()
    live.append(mixs)
    OAT = kb.sb(mixs, [128, 4, NO * 128], BF, "OAT")
    nsa = ExitStack()
    live.append(nsa)
    KE = [kb.sb(nsa, [128, 4096], BF, f"KE{g}") for g in range(2)]
    KW = kb.sb(nsa, [128, 4096], BF, "KW")
    VS = kb.sb(nsa, [128, 32, 2, 66], BF, "VS")
    VW = kb.sb(nsa, [128, 32, 2, 66], BF, "VW")
    KCT = kb.sb(nsa, [128, 256], BF, "KCT")
    VCA = kb.sb(nsa, [128, 2, 2, 128], BF, "VCA")

    kb.V(lambda e: e.memset(VS[:], 1.0), [VS], [])
    kb.V(lambda e: e.memset(VW[:], 1.0), [VW], [])
    kb.V(lambda e: e.memset(KCT[:], 0.0), [KCT], [])
    with Staging() as stg:
        for g, rows in ((0, slice(64, 128)), (1, slice(0, 64))):
            for c0 in range(0, 4096, 2048):
                s = stg.next()
                kb.dma(SP, s[rows, 0:2048], D["expand"][:, c0:c0 + 2048], outs=[s])
                kb.G(lambda e: e.tensor_copy(out=KE[g][rows, c0:c0 + 2048], in_=s[rows, 0:2048]), [KE[g]], [s])

    ck(0.3)
    with ExitStack() as p1:
        live.append(p1)
        Wkv = kb.sb(p1, [128, 8, 768], BF, "Wkv")
        W1 = {nm: kb.sb(p1, [128, 32, 256], BF, "W1" + nm) for nm in ("k", "v")}
        W2k = kb.sb(p1, [128, 2, 128], BF, "W2k")
        W2v = kb.sb(p1, [128, 2, 64], BF, "W2v")
        peT = {"k": cload(p1, "peTk", BF), "v": cload(p1, "peTv", BF)}
        ropeN_c = cload(p1, "ropeN_c")
        ropeN_s = cload(p1, "ropeN_s")
        ovb = cload(p1, "ov", BF)
        for g in range(2):
            for ct in range(2):
                kb.V(lambda e: e.tensor_copy(out=VCA[:, g, ct, 64:128], in_=ovb[:, ct, :]), [VCA], [ovb])
        with Staging() as stg:
            load_w(stg, Wkv, lambda c0, n: Wkv[:, :, c0:c0 + n], w_in3[:, :, 512:1280], 8, 768)
            for nm in ("k", "v"):
                src = D["cmp_w1_" + nm].rearrange("(l d) m -> d l m", d=64)
                for half in range(2):
                    rows = slice(64 * half, 64 * half + 64)
                    for l0 in range(0, 32, 8):
                        s = stg.next()
                        sv = s[rows, :].rearrange("p (l m) -> p l m", l=8)
                        kb.dma(SP, sv, src[:, l0:l0 + 8, :], outs=[s])
                        kb.G(lambda e: e.tensor_copy(out=W1[nm][rows, l0:l0 + 8, :], in_=sv), [W1[nm]], [s])
            s = stg.next()
            kb.dma(SP, s[:, 0:128].rearrange("p (c d) -> p c d", c=2),
                   D["cmp_w2_k"].rearrange("(c p) d -> p c d", p=128), outs=[s])
            kb.dma(SP, s[:, 128:256].rearrange("p (c d) -> p c d", c=2),
                   D["cmp_w2_v"].rearrange("(c p) d -> p c d", p=128), outs=[s])
            sk = s[:, 0:128].rearrange("p (c d) -> p c d", c=2)
            kb.G(lambda e: e.tensor_copy(out=W2k[:, :, 0:64], in_=sk), [W2k], [s])
            kb.G(lambda e: e.tensor_copy(out=W2k[:, :, 64:128], in_=sk), [W2k], [s])
            kb.G(lambda e: e.tensor_copy(out=W2v[:], in_=s[:, 128:256].rearrange("p (c d) -> p c d", c=2)), [W2v], [s])
        KC = kb.sb(p1, [128, 4096], BF, "KC")
        VC = kb.sb(p1, [128, 4096], BF, "VC")
        ktin = Rot([kb.sb(p1, [128, 4, 128], BF, "ktin") for _ in range(2)])
        rtmp = Rot([kb.sb(p1, [128, 2, 8], F32, "rtmp") for _ in range(16)])

        ck(0.5)
        if stop_after == 0.55:
            hT = norm_T(xl_blk[17], nmw)
            dump("hT", hT, hT[:], [128, 8, 128])
            ck(0.55)
        for blk in range(NB if not (isinstance(stop_after, float) and 0.6 <= stop_after < 0.7) else 1):
            hT = norm_T(xl_blk[blk], nmw)
            pa, pb = PSS.next(), PSS.next()
            for c in range(8):
                kb.mm(pa, pa[:, 0:512], hT, hT[:, c, :], Wkv, Wkv[:, c, 0:512], c == 0, c == 7)
            for c in range(8):
                kb.mm(pb, pb[:, 0:256], hT, hT[:, c, :], Wkv, Wkv[:, c, 512:768], c == 0, c == 7)
            ck(0.61)
            kt = ktin.next()
            kb.A(lambda e: e.copy(out=kt[:, 0:2, :], in_=pa[:, 0:256].rearrange("p (a d) -> p a d", a=2)), [kt], [pa])
            kb.A(lambda e: e.copy(out=VS[:, blk, :, 0:64], in_=pa[:, 384:512].rearrange("p (g d) -> p g d", g=2)),
                 [VS], [pa])
            kb.A(lambda e: e.copy(out=VW[:, blk, :, 0:64], in_=pb[:, 128:256].rearrange("p (g d) -> p g d", g=2)),
                 [VW], [pb])
            ck(0.62)
            for slot, (pt, lo) in ((2, (pa, 256)), (3, (pb, 0))):
                src3 = pt[:, lo:lo + 128].rearrange("p (g d) -> p g d", g=2)
                dst3 = kt[:, slot, :].rearrange("p (g d) -> p g d", g=2)
                cb = ropeN_c[:, blk, :].unsqueeze(1).to_broadcast([128, 2, 8])
                sb_ = ropeN_s[:, blk, :].unsqueeze(1).to_broadcast([128, 2, 8])
                x1, x2 = src3[:, :, 0:8], src3[:, :, 8:16]
                t1, t2, t3, t4 = rtmp.next(), rtmp.next(), rtmp.next(), rtmp.next()
                kb.V(lambda e: e.tensor_tensor(out=t1[:], in0=x1, in1=cb, op=ALU.mult), [t1], [pt, ropeN_c])
                kb.V(lambda e: e.tensor_tensor(out=t2[:], in0=x2, in1=sb_, op=ALU.mult), [t2], [pt, ropeN_s])
                kb.V(lambda e: e.tensor_tensor(out=t3[:], in0=x1, in1=sb_, op=ALU.mult), [t3], [pt, ropeN_s])
                kb.V(lambda e: e.tensor_tensor(out=t4[:], in0=x2, in1=cb, op=ALU.mult), [t4], [pt, ropeN_c])
                kb.V(lambda e: e.tensor_tensor(out=dst3[:, :, 0:8], in0=t1[:], in1=t2[:], op=ALU.subtract), [kt], [t1, t2])
                kb.V(lambda e: e.tensor_tensor(out=dst3[:, :, 8:16], in0=t3[:], in1=t4[:], op=ALU.add), [kt], [t3, t4])
                if "noactcopy" in _VAR:
                    kb.V(lambda e: e.tensor_copy(out=dst3[:, :, 16:64], in_=src3[:, :, 16:64]), [kt], [pt])
                else:
                    kb.A(lambda e: e.copy(out=dst3[:, :, 16:64], in_=src3[:, :, 16:64]), [kt], [pt])
            ck(0.63)
            for slot in range(4):
                kb.tr(psb, psb[:, slot * 128:(slot + 1) * 128], kt, kt[:, slot, :], identb, identb[:])
            ck(0.64)
            cs = slice(blk * 128, (blk + 1) * 128)
            kb.V(lambda e: e.tensor_copy(out=KC[:, cs], in_=psb[:, 0:128]), [KC], [psb])
            kb.V(lambda e: e.tensor_copy(out=VC[:, cs], in_=psb[:, 128:256]), [VC], [psb])
            kb.V(lambda e: e.tensor_copy(out=KE[0][0:64, cs], in_=psb[0:64, 256:384]), [KE[0]], [psb])
            kb.V(lambda e: e.tensor_copy(out=KE[1][64:128, cs], in_=psb[64:128, 256:384]), [KE[1]], [psb])
            kb.V(lambda e: e.tensor_copy(out=KW[:, cs], in_=psb[:, 384:512]), [KW], [psb])

        dump("KE0", KE[0], KE[0][:], [128, 4096])
        dump("KE1", KE[1], KE[1][:], [128, 4096])
        dump("KW", KW, KW[:], [128, 4096])
        dump("KC", KC, KC[:], [128, 4096])
        dump("VS", VS, VS[:], [128, 32, 2, 66])

        ck(0.6)
        ck(0.7)
        GT = kb.sb(p1, [128, 2, 256], BF, "GT")
        hb = kb.sb(p1, [128, 1], F32, "hb")
        gx = [kb.sb(p1, [128, 255], F32, f"gx{i}") for i in range(4)]
        for nm, src in (("k", KC), ("v", VC)):
            for g in range(2):
                rows = slice(64 * g, 64 * g + 64)
                kb.V(lambda e: e.memset(GT[:], 0.0), [GT], [])
                for mc in range(2):
                    ph, pbi = PSS.next(), PSS.next()
                    for l in range(32):
                        kb.mm(ph, ph[:, 0:255], W1[nm], W1[nm][rows, l, mc * 128:(mc + 1) * 128],
                              src, src[rows, l:l + 16 * 254 + 1:16], l == 0, l == 31)
                    for l in range(32):
                        kb.mm(pbi, pbi[:, 0:1], W1[nm], W1[nm][rows, l, mc * 128:(mc + 1) * 128],
                              peT[nm], peT[nm][rows, l:l + 1], l == 0, l == 31)
                    kb.V(lambda e: e.tensor_copy(out=hb[:], in_=pbi[:, 0:1]), [hb], [pbi])
                    x0, x2_, x3_, sg = gx
                    kb.V(lambda e: e.tensor_scalar(out=x0[:], in0=ph[:, 0:255], scalar1=hb[:, 0:1], scalar2=None,
                                                   op0=ALU.add), [x0], [ph, hb])
                    kb.V(lambda e: e.tensor_tensor(out=x2_[:], in0=x0[:], in1=x0[:], op=ALU.mult), [x2_], [x0])
                    kb.V(lambda e: e.tensor_scalar(out=x2_[:], in0=x2_[:], scalar1=0.044715, scalar2=1.0,
                                                   op0=ALU.mult, op1=ALU.add), [x2_], [x2_])
                    kb.V(lambda e: e.tensor_tensor(out=x3_[:], in0=x2_[:], in1=x0[:], op=ALU.mult), [x3_], [x2_, x0])
                    kb.A(lambda e: e.activation(out=sg[:], in_=x3_[:], func=AF.Sigmoid, scale=1.5957691216057308),
                         [sg], [x3_])
                    kb.V(lambda e: e.tensor_tensor(out=GT[:, mc, 0:255], in0=x0[:], in1=sg[:], op=ALU.mult),
                         [GT], [x0, sg])
                if nm == "k":
                    pk = PSS.next()
                    for mc in range(2):
                        kb.mm(pk, pk[:, 0:255], W2k, W2k[:, mc, :], GT, GT[:, mc, 0:255], mc == 0, mc == 1)
                    kb.V(lambda e: e.tensor_copy(out=KCT[rows, 0:255], in_=pk[rows, 0:255]), [KCT], [pk])
                else:
                    for ct in range(2):
                        pv = PSS.next()
                        for mc in range(2):
                            kb.mm(pv, pv[:, 0:64], GT, GT[:, mc, ct * 128:(ct + 1) * 128], W2v, W2v[:, mc, :],
                                  mc == 0, mc == 1)
                        kb.V(lambda e: e.tensor_copy(out=VCA[:, g, ct, 0:64], in_=pv[:, 0:64]), [VCA], [pv])
        dump("KCT", KCT, KCT[:], [128, 256])
        dump("VCA", VCA, VCA[:], [128, 2, 2, 128])
        kb.barrier()
        live.remove(p1)
    ck(1)

    with ExitStack() as p2:
        live.append(p2)
        Wq = kb.sb(p2, [128, 8, 536], BF, "Wq")
        with Staging() as stg:
            load_w(stg, Wq, lambda c0, n: Wq[:, :, c0:c0 + n], w_in3[:, :, 0:512], 8, 512)
            load_w(stg, Wq, lambda c0, n: Wq[:, :, 512 + c0:512 + c0 + n], w_in3[:, :, 1280:1304], 8, 24)
        ropeN_c = cload(p2, "ropeN_c")
        ropeN_s = cload(p2, "ropeN_s")
        cmpmask = cload(p2, "cmpmask", BF)
        selvalid = cload(p2, "selvalid")
        seladd = cload(p2, "seladd")
        selkeep = cload(p2, "selkeep")
        def qb_pair(g):
            t = kb.sb(p2, [128, 512], BF, f"QB{g}")
            return (t, T(t.t))
        QB = [Rot([qb_pair(g) for _ in range(2)]) for g in range(2)]
        qr = Rot([kb.sb(p2, [128, 4, 128], BF, "qr") for _ in range(2)])
        gat = Rot([kb.sb(p2, [128, 24], F32, "gat") for _ in range(2)])
        rt = Rot([kb.sb(p2, [128, 8, 8], F32, "rt") for _ in range(8)])
        PTs = Rot([kb.sb(p2, [128, 512], BF, "PT") for _ in range(4)])
        sm = Rot([kb.sb(p2, [128, 16], F32, "sm") for _ in range(12)])
        impt = Rot([kb.sb(p2, [128, 64], F32, "imp") for _ in range(8)])
        m8 = Rot([kb.sb(p2, [128, 8], F32, "m8") for _ in range(4)])
        ZS = [kb.sb(p2, [128, 128], BF, f"ZS{g}") for g in range(2)]
        for g in range(2):
            kb.V(lambda e: e.memset(ZS[g][:], 0.0), [ZS[g]], [])
        oacc = Rot([kb.sb(p2, [128, 8, 64], F32, "oacc") for _ in range(2)])
        otmp = Rot([kb.sb(p2, [128, 4, 64], F32, "otmp") for _ in range(3)])
        oab = Rot([kb.sb(p2, [128, 512], BF, "oab") for _ in range(2)])

        def coef_from(rs_ap, rs_t, gate_ap, gate_t):
            c = sm.next()
            kb.V(lambda e: e.tensor_scalar(out=c[:, 0:4], in0=rs_ap, scalar1=1e-30, scalar2=None, op0=ALU.max),
                 [c], [rs_t])
            kb.V(lambda e: e.reciprocal(out=c[:, 4:8], in_=c[:, 0:4]), [c], [c])
            kb.V(lambda e: e.tensor_tensor(out=c[:, 8:12], in0=c[:, 4:8], in1=gate_ap, op=ALU.mult), [c], [c, gate_t])
            return c

        def accumulate(oa, g, first, ps_t, o_view, coef):
            cb = coef[:, 8:12].unsqueeze(2).to_broadcast([128, 4, 64])
            if first:
                kb.V(lambda e: e.tensor_tensor(out=oa[:, 4 * g:4 * g + 4, :], in0=o_view, in1=cb, op=ALU.mult),
                     [oa], [ps_t, coef])
            else:
                tm = otmp.next()
                kb.V(lambda e: e.tensor_tensor(out=tm[:], in0=o_view, in1=cb, op=ALU.mult), [tm], [ps_t, coef])
                kb.V(lambda e: e.tensor_tensor(out=oa[:, 4 * g:4 * g + 4, :], in0=oa[:, 4 * g:4 * g + 4, :],
                                               in1=tm[:], op=ALU.add), [oa], [oa, tm])

        for o in range(NO):
            blk = OB0 + o
            hT = norm_T(xl_blk[blk], nmw)
            pq, pg = PSS.next(), PSS.next()
            for c in range(8):
                kb.mm(pq, pq[:, 0:512], hT, hT[:, c, :], Wq, Wq[:, c, 0:512], c == 0, c == 7)
            for c in range(8):
                kb.mm(pg, pg[:, 0:24], hT, hT[:, c, :], Wq, Wq[:, c, 512:536], c == 0, c == 7)
            ga = gat.next()
            kb.A(lambda e: e.activation(out=ga[:], in_=pg[:, 0:24], func=AF.Sigmoid), [ga], [pg])
            gav = ga[:].rearrange("p (h b) -> p h b", b=3)
            q_ = qr.next()
            src4 = pq[:, 0:512].rearrange("p (j hp d) -> p j hp d", j=2, hp=4)
            dst4 = q_[:].rearrange("p hp (j d) -> p j hp d", j=2)
            cb = ropeN_c[:, blk, :].unsqueeze(1).unsqueeze(1).to_broadcast([128, 2, 4, 8])
            sb_ = ropeN_s[:, blk, :].unsqueeze(1).unsqueeze(1).to_broadcast([128, 2, 4, 8])
            x1, x2 = src4[:, :, :, 0:8], src4[:, :, :, 8:16]
            t1, t2, t3, t4 = rt.next(), rt.next(), rt.next(), rt.next()
            tv = lambda t: t[:].rearrange("p (j hp) d -> p j hp d", j=2)
            kb.V(lambda e: e.tensor_tensor(out=tv(t1), in0=x1, in1=cb, op=ALU.mult), [t1], [pq, ropeN_c])
            kb.V(lambda e: e.tensor_tensor(out=tv(t2), in0=x2, in1=sb_, op=ALU.mult), [t2], [pq, ropeN_s])
            kb.V(lambda e: e.tensor_tensor(out=tv(t3), in0=x1, in1=sb_, op=ALU.mult), [t3], [pq, ropeN_s])
            kb.V(lambda e: e.tensor_tensor(out=tv(t4), in0=x2, in1=cb, op=ALU.mult), [t4], [pq, ropeN_c])
            kb.V(lambda e: e.tensor_tensor(out=dst4[:, :, :, 0:8], in0=tv(t1), in1=tv(t2), op=ALU.subtract), [q_], [t1, t2])
            kb.V(lambda e: e.tensor_tensor(out=dst4[:, :, :, 8:16], in0=tv(t3), in1=tv(t4), op=ALU.add), [q_], [t3, t4])
            kb.A(lambda e: e.copy(out=dst4[:, :, :, 16:64], in_=src4[:, :, :, 16:64]), [q_], [pq])
            for h in range(4):
                kb.tr(psb, psb[:, h * 128:(h + 1) * 128], q_, q_[:, h, :], identb, identb[:])
            qpair = [QB[0].next(), QB[1].next()]
            qq = [qpair[0][0], qpair[1][0]]
            qbias = [qpair[0][1], qpair[1][1]]
            kb.V(lambda e: e.tensor_copy(out=qq[0][0:64, :], in_=psb[0:64, 0:512]), [qq[0]], [psb])
            kb.V(lambda e: e.tensor_copy(out=qq[1][64:128, :], in_=psb[64:128, 0:512]), [qq[1]], [psb])
            oa = oacc.next()
            P4V = lambda t: t[:].rearrange("p (h q) -> p h q", h=4)

            def fin_cmp(g, pso):
                pv4 = pso[:].rearrange("p (h x) -> p h x", h=4)
                rs = sm.next()
                kb.V(lambda e: e.tensor_reduce(out=rs[:, 0:4], in_=pv4[:, :, 64:128], axis=AX.X, op=ALU.add),
                     [rs], [pso])
                cf = coef_from(rs[:, 0:4], rs, gav[:, 4 * g:4 * g + 4, 0], ga)
                accumulate(oa, g, True, pso, pv4[:, :, 0:64], cf)
                im = impt.next()
                kb.V(lambda e: e.tensor_scalar(out=im[:], in0=pv4[:, 0, 64:128], scalar1=cf[:, 4:5], scalar2=None,
                                               op0=ALU.mult), [im], [pso, cf])
                for h in range(1, 4):
                    kb.V(lambda e: e.scalar_tensor_tensor(out=im[:], in0=pv4[:, h, 64:128], scalar=cf[:, 4 + h:5 + h],
                                                          in1=im[:], op0=ALU.mult, op1=ALU.add), [im], [pso, cf, im])
                imf = impt.next()
                kb.V(lambda e: e.tensor_tensor(out=imf[:], in0=im[:], in1=selvalid[:, o, :], op=ALU.mult),
                     [imf], [im, selvalid])
                kb.V(lambda e: e.tensor_tensor(out=imf[:], in0=imf[:], in1=seladd[:, o, :], op=ALU.add),
                     [imf], [imf, seladd])
                ma, mb = m8.next(), m8.next()
                im2 = impt.next()
                kb.V(lambda e: e.max(out=ma[:], in_=imf[:]), [ma], [imf])
                kb.V(lambda e: e.match_replace(out=im2[:], in_to_replace=ma[:], in_values=imf[:], imm_value=-1e9),
                     [im2], [ma, imf])
                kb.V(lambda e: e.max(out=mb[:], in_=im2[:]), [mb], [im2])
                sel = impt.next()
                kb.V(lambda e: e.tensor_scalar(out=sel[:], in0=imf[:], scalar1=mb[:, 7:8], scalar2=None,
                                               op0=ALU.is_ge), [sel], [imf, mb])
                kb.V(lambda e: e.tensor_tensor(out=sel[:], in0=sel[:], in1=selkeep[:, o, :], op=ALU.mult),
                     [sel], [sel, selkeep])
                zc = slice(64, 128) if g == 0 else slice(0, 64)
                kb.V(lambda e: e.tensor_scalar(out=ZS[g][:, zc], in0=sel[:], scalar1=1.0, scalar2=1e5,
                                               op0=ALU.subtract, op1=ALU.mult), [ZS[g]], [sel])

            def put_bias(g):
                zc = slice(64, 128) if g == 0 else slice(0, 64)
                kb.tr(psb, psb[:, 512 + g * 128:512 + (g + 1) * 128], ZS[g], ZS[g][:], identb, identb[:])
                kb.V(lambda e: e.tensor_copy(
                    out=qbias[g][zc, :].rearrange("p (h q) -> p h q", h=4),
                    in_=psb[zc, 512 + g * 128:512 + (g + 1) * 128].unsqueeze(1).to_broadcast([64, 4, 128])),
                    [qbias[g]], [psb])

            def fin_soft(g, ps_t, br):
                pv4 = ps_t[:, 0:264].rearrange("p (h x) -> p h x", h=4)
                cf = coef_from(pv4[:, :, 64], ps_t, gav[:, 4 * g:4 * g + 4, br], ga)
                accumulate(oa, g, False, ps_t, pv4[:, :, 0:64], cf)

            jobs = []
            for g in range(2):
                rows = slice(64 * g, 64 * g + 64)
                for ct in range(2):
                    jobs.append(dict(kind="cmp", g=g, first=ct == 0, last=ct == 1, w=128,
                                     lhsT=(KCT, KCT[rows, ct * 128:(ct + 1) * 128]), rhs=(qq[g], qq[g][rows, :]), extra=(),
                                     bias=None, mask=(cmpmask, cmpmask[:, ct, o, :]), v=(VCA, VCA[:, g, ct, :])))
            for g in range(2):
                rows = slice(64 * g, 64 * g + 64)
                for i, kt in enumerate(range(blk - 4, blk + 1)):
                    mk = (upmask, upmask[:]) if i == 0 else ((lowmask, lowmask[:]) if i == 4 else None)
                    jobs.append(dict(kind="win", g=g, first=i == 0, last=i == 4, w=66,
                                     lhsT=(KW, KW[rows, kt * 128:(kt + 1) * 128]), rhs=(qq[g], qq[g][rows, :]), extra=(),
                                     bias=pbias if kt < 16 else None, mask=mk, v=(VW, VW[:, kt, g, :]),
                                     bias_after=(i == 2)))
            for g in range(2):
                for kt in range(blk + 1):
                    mk = (lowmask, lowmask[:]) if kt == blk else None
                    jobs.append(dict(kind="slc", g=g, first=kt == 0, last=kt == blk, w=66,
                                     lhsT=(KE[g], KE[g][:, kt * 128:(kt + 1) * 128]), rhs=(qq[g], qpair[g][0][:, :]),
                                     extra=(qbias[g],), bias=None, mask=mk, v=(VS, VS[:, kt, g, :])))

            def stageA(j):
                pS = PSS.next()
                kb.mm(pS, pS[:, :], j["lhsT"][0], j["lhsT"][1], j["rhs"][0], j["rhs"][1], True, True, extra=j["extra"])
                pt = PTs.next()
                if j["bias"] is None:
                    kb.A(lambda e: e.activation(out=pt[:], in_=pS[:], func=AF.Exp, scale=0.125), [pt], [pS])
                else:
                    bt = j["bias"]
                    kb.A(lambda e: e.activation(out=pt[:], in_=pS[:], func=AF.Exp, scale=0.125, bias=bt[:, 0:1]),
                         [pt], [pS, bt])
                if j["mask"] is not None:
                    mt_, map_ = j["mask"]
                    kb.V(lambda e: e.tensor_tensor(out=P4V(pt), in0=P4V(pt),
                                                   in1=map_.unsqueeze(1).to_broadcast([128, 4, 128]), op=ALU.mult),
                         [pt], [pt, mt_])
                j["pt"] = pt

            acc = {}

            def stageC(j):
                key = (j["kind"], j["g"])
                if j["first"]:
                    acc[key] = PSA.next()
                pa_ = acc[key]
                w, pt = j["w"], j["pt"]
                for h in range(4):
                    kb.mm(pa_, pa_[:, h * w:(h + 1) * w], pt, pt[:, h * 128:(h + 1) * 128], j["v"][0], j["v"][1],
                          j["first"] and h == 0, j["last"] and h == 3, skip=True)
                if j.get("bias_after"):
                    put_bias(j["g"])
                if j["last"]:
                    if j["kind"] == "cmp":
                        fin_cmp(j["g"], pa_)
                    else:
                        fin_soft(j["g"], pa_, 2 if j["kind"] == "win" else 1)

            LA = 2
            for i in range(len(jobs) + LA):
                if i < len(jobs):
                    stageA(jobs[i])
                if i >= LA:
                    stageC(jobs[i - LA])

            ob = oab.next()
            kb.V(lambda e: e.tensor_copy(out=ob[:], in_=oa[:].rearrange("p h d -> p (h d)")), [ob], [oa])
            if o == 1:
                dump("oa", oa, oa[:], [128, 8, 64])
            for c in range(4):
                kb.tr(psb, psb[:, c * 128:(c + 1) * 128], ob, ob[:, c * 128:(c + 1) * 128], identb, identb[:])
            kb.V(lambda e: e.tensor_copy(out=OAT[:, :, o * 128:(o + 1) * 128],
                                         in_=psb[:, 0:512].rearrange("p (c t) -> p c t", c=4)), [OAT], [psb])
        kb.barrier()
        live.remove(p2)
    nsa.close()
    live.remove(nsa)
    ck(2)

    ORT = kb.sb(mixs, [128, 8, NO * 128], BF, "ORT")
    with ExitStack() as p3:
        live.append(p3)
        Wr = kb.sb(p3, [128, 8, 3072], BF, "Wr")
        with Staging() as stg:
            load_w(stg, Wr, lambda c0, n: Wr[:, :, c0:c0 + n], w_in3[:, :, 1304:4376], 8, 3072)
        rtc = Rot([kb.sb(p3, [128, 64], F32, "rtc") for _ in range(2)])
        rts = Rot([kb.sb(p3, [128, 64], F32, "rts") for _ in range(2)])
        cur_rt = {}
        decT = cload(p3, "decT")
        qdecrow = cload(p3, "qdecrow")
        kdec = cload(p3, "kdec")
        gnw = cload(p3, "gnw")
        ST = kb.sb(p3, [128, 4, 256], F32, "ST")
        STb = kb.sb(p3, [128, 4, 256], BF, "STb")
        kb.V(lambda e: e.memset(ST[:], 0.0), [ST], [])
        kb.V(lambda e: e.memset(STb[:], 0.0), [STb], [])
        rr = Rot([kb.sb(p3, [128, 4, 64], F32, "rr") for _ in range(4)])
        krs = Rot([kb.sb(p3, [128, 4, 128], F32, "kr") for _ in range(3)])
        KDs = Rot([kb.sb(p3, [128, 512], BF, "KD") for _ in range(2)])
        VBs = Rot([kb.sb(p3, [128, 1024], BF, "VB") for _ in range(2)])
        qkb = Rot([kb.sb(p3, [128, 2, 512], BF, "qkb") for _ in range(2)])
        QKT = Rot([kb.sb(p3, [128, 8, 128], BF, "QKT") for _ in range(2)])
        QTd = Rot([kb.sb(p3, [128, 4, 128], BF, "QTd") for _ in range(2)])
        SCs = Rot([kb.sb(p3, [128, 4, 128], BF, "SC") for _ in range(2)])
        sgs = Rot([kb.sb(p3, [128, 1024], F32, "sg") for _ in range(1)])
        ys = Rot([kb.sb(p3, [128, 1024], F32, "y") for _ in range(1)])
        orb = Rot([kb.sb(p3, [128, 1024], BF, "orb") for _ in range(2)])
        bst = Rot([kb.sb(p3, [128, 4, 6], F32, "bst") for _ in range(2)])
        mv = Rot([kb.sb(p3, [128, 4, 4], F32, "mv") for _ in range(2)])

        def rope_full(dst3, dst_t, pt, blk):
            src3 = pt[:, 0:512].rearrange("p (h d) -> p h d", h=4)
            if cur_rt.get("blk") != blk:
                ropeR_c, ropeR_s = rtc.next(), rts.next()
                kb.dma(SP, ropeR_c[:], D["ropeR_c"][:, blk, :], outs=[ropeR_c])
                kb.dma(SP, ropeR_s[:], D["ropeR_s"][:, blk, :], outs=[ropeR_s])
                cur_rt.update(blk=blk, c=ropeR_c, s=ropeR_s)
            ropeR_c, ropeR_s = cur_rt["c"], cur_rt["s"]
            cb = ropeR_c[:].unsqueeze(1).to_broadcast([128, 4, 64])
            sb_ = ropeR_s[:].unsqueeze(1).to_broadcast([128, 4, 64])
            x1, x2 = src3[:, :, 0:64], src3[:, :, 64:128]
            t1, t2 = rr.next(), rr.next()
            kb.V(lambda e: e.tensor_tensor(out=t1[:], in0=x1, in1=cb, op=ALU.mult), [t1], [pt, ropeR_c])
            kb.V(lambda e: e.tensor_tensor(out=t2[:], in0=x2, in1=sb_, op=ALU.mult), [t2], [pt, ropeR_s])
            kb.V(lambda e: e.tensor_tensor(out=dst3[:, :, 0:64], in0=t1[:], in1=t2[:], op=ALU.subtract),
                 [dst_t], [t1, t2])
            t3, t4 = rr.next(), rr.next()
            kb.V(lambda e: e.tensor_tensor(out=t3[:], in0=x1, in1=sb_, op=ALU.mult), [t3], [pt, ropeR_s])
            kb.V(lambda e: e.tensor_tensor(out=t4[:], in0=x2, in1=cb, op=ALU.mult), [t4], [pt, ropeR_c])
            kb.V(lambda e: e.tensor_tensor(out=dst3[:, :, 64:128], in0=t3[:], in1=t4[:], op=ALU.add),
                 [dst_t], [t3, t4])

        for blk in range(NB):
            o = blk - OB0
            hT = norm_T(xl_blk[blk], nmw)
            pk = PSALL.next()
            for c in range(8):
                kb.mm(pk, pk[:, :], hT, hT[:, c, :], Wr, Wr[:, c, 512:1024], c == 0, c == 7)
            kr = krs.next()
            rope_full(kr[:], kr, pk, blk)
            KD = KDs.next()
            kb.V(lambda e: e.tensor_tensor(out=KD[:].rearrange("p (h d) -> p h d", h=4), in0=kr[:],
                                           in1=kdec[:].unsqueeze(2).to_broadcast([128, 4, 128]), op=ALU.mult),
                 [KD], [kr, kdec])
            VB = VBs.next()
            for n in range(2):
                pvv = PSALL.next()
                for c in range(8):
                    kb.mm(pvv, pvv[:, :], hT, hT[:, c, :], Wr, Wr[:, c, 1024 + n * 512:1536 + n * 512], c == 0, c == 7)
                kb.A(lambda e: e.copy(out=VB[:, n * 512:(n + 1) * 512], in_=pvv[:]), [VB], [pvv])
            if o >= 0:
                pq = PSALL.next()
                for c in range(8):
                    kb.mm(pq, pq[:, :], hT, hT[:, c, :], Wr, Wr[:, c, 0:512], c == 0, c == 7)
                qrf = krs.next()
                rope_full(qrf[:], qrf, pq, blk)
                qk = qkb.next()
                kb.V(lambda e: e.tensor_copy(out=qk[:, 0, :].rearrange("p (h d) -> p h d", h=4), in_=qrf[:]), [qk], [qrf])
                kb.V(lambda e: e.tensor_copy(out=qk[:, 1, :].rearrange("p (h d) -> p h d", h=4), in_=kr[:]), [qk], [kr])
                sg = sgs.next()
                for n in range(2):
                    pgg = PSALL.next()
                    for c in range(8):
                        kb.mm(pgg, pgg[:, :], hT, hT[:, c, :], Wr, Wr[:, c, 2048 + n * 512:2560 + n * 512],
                              c == 0, c == 7)
                    kb.A(lambda e: e.activation(out=sg[:, n * 512:(n + 1) * 512], in_=pgg[:], func=AF.Silu), [sg], [pgg])
                for j in range(8):
                    kb.tr(psb, psb[:, j * 128:(j + 1) * 128], qk, qk[:, j // 4, (j % 4) * 128:(j % 4 + 1) * 128],
                          identb, identb[:])
                qkt = QKT.next()
                kb.V(lambda e: e.tensor_copy(out=qkt[:], in_=psb[:].rearrange("p (j t) -> p j t", j=8)), [qkt], [psb])
                qtd = QTd.next()
                kb.V(lambda e: e.tensor_tensor(out=qtd[:], in0=qkt[:, 0:4, :], in1=qdecrow[:], op=ALU.mult),
                     [qtd], [qkt, qdecrow])
                psc = PSALL.next()
                for h in range(4):
                    kb.mm(psc, psc[:, h * 128:(h + 1) * 128], qkt, qkt[:, 4 + h, :], qkt, qkt[:, h, :], True, True)
                sc = SCs.next()
                kb.V(lambda e: e.tensor_tensor(out=sc[:], in0=psc[:].rearrange("p (h i) -> p h i", h=4), in1=decT[:],
                                               op=ALU.mult), [sc], [psc, decT])
                y = ys.next()
                bs, mvv = bst.next(), mv.next()
                for n in range(2):
                    po = PSALL.next()
                    for hh in range(2):
                        h = 2 * n + hh
                        kb.mm(po, po[:, hh * 256:(hh + 1) * 256], sc, sc[:, h, :], VB, VB[:, h * 256:(h + 1) * 256],
                              True, False)
                        kb.mm(po, po[:, hh * 256:(hh + 1) * 256], qtd, qtd[:, h, :], STb, STb[:, h, :], False, True)
                    for hh in range(2):
                        h = 2 * n + hh
                        kb.V(lambda e: e.bn_stats(out=bs[:, h, :], in_=po[:, hh * 256:(hh + 1) * 256]), [bs], [po])
                        kb.V(lambda e: e.bn_aggr(out=mvv[:, h, 0:2], in_=bs[:, h, :]), [mvv], [bs])
                        kb.V(lambda e: e.tensor_scalar(out=mvv[:, h, 3:4], in0=mvv[:, h, 1:2], scalar1=EPS, scalar2=None,
                                                       op0=ALU.add), [mvv], [mvv])
                        kb.A(lambda e: e.activation(out=mvv[:, h, 3:4], in_=mvv[:, h, 3:4], func=AF.Sqrt), [mvv], [mvv])
                        kb.V(lambda e: e.reciprocal(out=mvv[:, h, 2:3], in_=mvv[:, h, 3:4]), [mvv], [mvv])
                        kb.V(lambda e: e.tensor_scalar(out=y[:, h * 256:(h + 1) * 256], in0=po[:, hh * 256:(hh + 1) * 256],
                                                       scalar1=mvv[:, h, 0:1], scalar2=mvv[:, h, 2:3],
                                                       op0=ALU.subtract, op1=ALU.mult), [y], [po, mvv])
                if o == 1:
                    dump("ret_y", y, y[:], [128, 1024])
                kb.V(lambda e: e.tensor_tensor(out=y[:], in0=y[:], in1=gnw[:], op=ALU.mult), [y], [y, gnw])
                ob = orb.next()
                kb.V(lambda e: e.tensor_tensor(out=ob[:], in0=y[:], in1=sg[:], op=ALU.mult), [ob], [y, sg])
                for c in range(8):
                    kb.tr(psb, psb[:, c * 128:(c + 1) * 128], ob, ob[:, c * 128:(c + 1) * 128], identb, identb[:])
                kb.V(lambda e: e.tensor_copy(out=ORT[:, :, o * 128:(o + 1) * 128],
                                             in_=psb[:].rearrange("p (c t) -> p c t", c=8)), [ORT], [psb])
            if blk < NB - 1:
                for n in range(2):
                    pu = PSALL.next()
                    for hh in range(2):
                        h = 2 * n + hh
                        kb.mm(pu, pu[:, hh * 256:(hh + 1) * 256], KD, KD[:, h * 128:(h + 1) * 128],
                              VB, VB[:, h * 256:(h + 1) * 256], True, True)
                    for hh in range(2):
                        h = 2 * n + hh
                        kb.V(lambda e: e.scalar_tensor_tensor(out=ST[:, h, :], in0=ST[:, h, :], scalar=GAMMAS[h] ** 128,
                                                              in1=pu[:, hh * 256:(hh + 1) * 256], op0=ALU.mult,
                                                              op1=ALU.add), [ST], [ST, pu])
                kb.V(lambda e: e.tensor_copy(out=STb[:], in_=ST[:]), [STb], [ST])
        dump("ORT", ORT, ORT[:], [128, 8, NO * 128])
        kb.barrier()
        live.remove(p3)
    ck(3)

    x1_blk = x1d.rearrange("(b p) f -> b p f", p=128)
    with ExitStack() as p4:
        live.append(p4)
        Wgm = kb.sb(p4, [128, 8, 2048], BF, "Wgm")
        Wnsa = kb.sb(p4, [128, 4, 1024], BF, "Wnsa")
        Wret = kb.sb(p4, [128, 8, 1024], BF, "Wret")
        Wmix = kb.sb(p4, [128, 8, 1024], BF, "Wmix")
        with Staging() as stg:
            load_w(stg, Wgm, lambda c0, n: Wgm[:, :, c0:c0 + n], w_in3[:, :, 4376:6424], 8, 2048)
            load_w(stg, Wnsa, lambda c0, n: Wnsa[:, :, c0:c0 + n], D["w_nsa"].rearrange("(c p) n -> p c n", p=128), 4, 1024)
            load_w(stg, Wret, lambda c0, n: Wret[:, :, c0:c0 + n], D["w_ret"].rearrange("(c p) n -> p c n", p=128), 8, 1024)
            load_w(stg, Wmix, lambda c0, n: Wmix[:, :, c0:c0 + n], D["w_mix"].rearrange("(c p) n -> p c n", p=128), 8, 1024)
        GMs = Rot([kb.sb(p4, [128, 16, 128], F32, "GM") for _ in range(2)])
        MTs = Rot([kb.sb(p4, [128, 8, 128], BF, "MT") for _ in range(2)])
        mt1 = Rot([kb.sb(p4, [128, 512], F32, "mt1") for _ in range(2)])
        mt2 = Rot([kb.sb(p4, [128, 512], F32, "mt2") for _ in range(2)])
        x1s = Rot([kb.sb(p4, [128, 1024], F32, "x1") for _ in range(2)])
        for o in range(NO):
            blk = OB0 + o
            hT, xt, _ = norm_T(xl_blk[blk], nmw, want_xt=True)
            gm = GMs.next()
            for q4 in range(4):
                pgm = PSALL.next()
                for j in range(4):
                    mc = q4 * 4 + j
                    for c in range(8):
                        kb.mm(pgm, pgm[:, j * 128:(j + 1) * 128], Wgm, Wgm[:, c, mc * 128:(mc + 1) * 128],
                              hT, hT[:, c, :], c == 0, c == 7)
                kb.A(lambda e: e.activation(out=gm[:, q4 * 4:q4 * 4 + 4, :].rearrange("p a t -> p (a t)"),
                                            in_=pgm[:], func=AF.Sigmoid), [gm], [pgm])
            mt = MTs.next()
            for half in range(2):
                pya, pyb = PSALL.next(), PSALL.next()
                for j in range(4):
                    fc = half * 4 + j
                    for c in range(4):
                        kb.mm(pya, pya[:, j * 128:(j + 1) * 128], Wnsa, Wnsa[:, c, fc * 128:(fc + 1) * 128],
                              OAT, OAT[:, c, o * 128:(o + 1) * 128], c == 0, c == 3)
                    for c in range(8):
                        kb.mm(pyb, pyb[:, j * 128:(j + 1) * 128], Wret, Wret[:, c, fc * 128:(fc + 1) * 128],
                              ORT, ORT[:, c, o * 128:(o + 1) * 128], c == 0, c == 7)
                a1, a2 = mt1.next(), mt2.next()
                kb.V(lambda e: e.tensor_tensor(out=a1[:], in0=pya[:],
                                               in1=gm[:, half * 4:half * 4 + 4, :].rearrange("p a t -> p (a t)"),
                                               op=ALU.mult), [a1], [pya, gm])
                kb.V(lambda e: e.tensor_tensor(out=a2[:], in0=pyb[:],
                                               in1=gm[:, 8 + half * 4:12 + half * 4, :].rearrange("p a t -> p (a t)"),
                                               op=ALU.mult), [a2], [pyb, gm])
                kb.V(lambda e: e.tensor_tensor(out=mt[:, half * 4:half * 4 + 4, :].rearrange("p a t -> p (a t)"),
                                               in0=a1[:], in1=a2[:], op=ALU.add), [mt], [a1, a2])
            x1 = x1s.next()
            for n in range(2):
                pm = PSALL.next()
                for fc in range(8):
                    kb.mm(pm, pm[:, :], mt, mt[:, fc, :], Wmix, Wmix[:, fc, n * 512:(n + 1) * 512], fc == 0, fc == 7)
                kb.V(lambda e: e.tensor_tensor(out=x1[:, n * 512:(n + 1) * 512], in0=pm[:],
                                               in1=xt[:, n * 512:(n + 1) * 512], op=ALU.add), [x1], [pm, xt])
            kb.dma(SP, x1_blk[o], x1[:], ins=[x1])
            if o == 1:
                dump("x1", x1, x1[:], [128, 1024])
        kb.barrier()
        live.remove(p4)
    mixs.close()
    live.remove(mixs)
    ck(4)

    with ExitStack() as p5:
        live.append(p5)
        Wup = kb.sb(p5, [128, 8, 5632], BF, "Wup")
        Wdn = kb.sb(p5, [128, 22, 1024], BF, "Wdn")
        with Staging() as stg:
            load_w(stg, Wup, lambda c0, n: Wup[:, :, c0:c0 + n], D["w_up"].rearrange("(c p) n -> p c n", p=128), 8, 5632)
            load_w(stg, Wdn, lambda c0, n: Wdn[:, :, c0:c0 + n], D["w_dn"].rearrange("(c p) n -> p c n", p=128), 22, 1024)
        convw = cload(p5, "convw")
        convb = cload(p5, "convb")
        nlw = cload(p5, "nlw")
        uh = kb.sb(p5, [128, 44, 2], F32, "uh")
        kb.V(lambda e: e.memset(uh[:], 0.0), [uh], [])
        PW = 256
        hT2s = Rot([kb.sb(p5, [128, 8, PW], BF, "hT2") for _ in range(2)])
        uts = Rot([kb.sb(p5, [128, PW + 2], F32, "ut") for _ in range(4)])
        cvs = Rot([kb.sb(p5, [128, PW], F32, "cv") for _ in range(4)])
        cvg = Rot([kb.sb(p5, [128, PW], F32, "cvg") for _ in range(3)])
        ATs = Rot([kb.sb(p5, [128, 22, PW], BF, "AT") for _ in range(1)])
        x2s = Rot([kb.sb(p5, [128, 1024], F32, "x2") for _ in range(1)])
        ys5 = Rot([kb.sb(p5, [128, 1024], F32, "y5") for _ in range(2)])
        out_blk = out_d.rearrange("(b p) f -> b p f", p=128)
        groups = [[0]] + [[1 + 2 * i, 2 + 2 * i] for i in range(8)]
        for grp in groups:
            N = 128 * len(grp)
            hT2 = hT2s.next()
            for j, o in enumerate(grp):
                norm_T(x1_blk[o], nfw, dst=(hT2, hT2[:, :, j * 128:(j + 1) * 128]))
            at = ATs.next()

            def conv_chunk(ch):
                pu = PSS.next()
                for c in range(8):
                    kb.mm(pu, pu[:, 0:N], Wup, Wup[:, c, ch * 128:(ch + 1) * 128], hT2, hT2[:, c, 0:N], c == 0, c == 7)
                ut = uts.next()
                cv = cvs.next()
                kb.A(lambda e: e.copy(out=ut[:, 2:2 + N], in_=pu[:, 0:N]), [ut], [pu])
                kb.A(lambda e: e.activation(out=cv[:, 0:N], in_=pu[:, 0:N], func=AF.Identity,
                                            scale=convw[:, 2, ch:ch + 1], bias=convb[:, ch:ch + 1]),
                     [cv], [pu, convw, convb])
                kb.V(lambda e: e.tensor_copy(out=ut[:, 0:2], in_=uh[:, ch, :]), [ut], [uh])
                kb.V(lambda e: e.tensor_copy(out=uh[:, ch, :], in_=ut[:, N:N + 2]), [uh], [ut])
                kb.V(lambda e: e.scalar_tensor_tensor(out=cv[:, 0:N], in0=ut[:, 1:1 + N], scalar=convw[:, 1, ch:ch + 1],
                                                      in1=cv[:, 0:N], op0=ALU.mult, op1=ALU.add), [cv], [ut, convw, cv])
                kb.V(lambda e: e.scalar_tensor_tensor(out=cv[:, 0:N], in0=ut[:, 0:N], scalar=convw[:, 0, ch:ch + 1],
                                                      in1=cv[:, 0:N], op0=ALU.mult, op1=ALU.add), [cv], [ut, convw, cv])
                return cv

            for c22 in range(22):
                cg_ = conv_chunk(c22)
                sgt = cvg.next()
                kb.A(lambda e: e.activation(out=sgt[:, 0:N], in_=cg_[:, 0:N], func=AF.Silu), [sgt], [cg_])
                cvv = conv_chunk(22 + c22)
                kb.V(lambda e: e.tensor_tensor(out=at[:, c22, 0:N], in0=sgt[:, 0:N], in1=cvv[:, 0:N], op=ALU.mult),
                     [at], [sgt, cvv])
            if grp == [0]:
                continue
            for j, o in enumerate(grp):
                xt = xts.next()
                kb.dma(SP, xt[:], x1_blk[o], outs=[xt])
                x2 = x2s.next()
                for n in range(2):
                    pd = PSA.next()
                    for c in range(22):
                        kb.mm(pd, pd[:, :], at, at[:, c, j * 128:(j + 1) * 128], Wdn, Wdn[:, c, n * 512:(n + 1) * 512],
                              c == 0, c == 21)
                    kb.V(lambda e: e.tensor_tensor(out=x2[:, n * 512:(n + 1) * 512], in0=pd[:],
                                                   in1=xt[:, n * 512:(n + 1) * 512], op=ALU.add), [x2], [pd, xt])
                st = stat.next()
                kb.V(lambda e: e.scalar_tensor_tensor(out=sq_junk[:], in0=x2[:], scalar=1.0, in1=x2[:], op0=ALU.mult,
                                                      op1=ALU.mult, accum_out=st[:, 0:1]), [sq_junk, st], [x2])
                kb.V(lambda e: e.tensor_scalar(out=st[:, 1:2], in0=st[:, 0:1], scalar1=1.0 / 1024, scalar2=EPS,
                                               op0=ALU.mult, op1=ALU.add), [st], [st])
                kb.A(lambda e: e.activation(out=st[:, 3:4], in_=st[:, 1:2], func=AF.Sqrt), [st], [st])
                kb.V(lambda e: e.reciprocal(out=st[:, 2:3], in_=st[:, 3:4]), [st], [st])
                y5 = ys5.next()
                kb.V(lambda e: e.scalar_tensor_tensor(out=y5[:], in0=x2[:], scalar=st[:, 2:3], in1=nlw[:], op0=ALU.mult,
                                                      op1=ALU.mult), [y5], [x2, st, nlw])
                kb.dma(SP, out_blk[o - 1], y5[:], ins=[y5])
        kb.barrier()
        live.remove(p5)
    return dbg_out


def finish(kb, top, stacks, dbg_out):
    kb.barrier()
    for s in stacks:
        s.close()
    top.close()
    return kb.nc, dbg_out


def _consts(s):
    f = np.float32
    c = {}
    c["ident"] = np.eye(128, dtype=f)
    tloc = np.arange(4096)
    pos = np.where(tloc < 2048, tloc, 2048 * s + tloc - 2048).astype(np.float64)
    if s == 0:
        pos[:2048] = 0.0

    def rope_tab(theta, half):
        inv = np.power(np.float32(theta), -np.arange(half, dtype=np.float32) / half).astype(np.float32)
        ang = pos.astype(np.float32)[:, None] * inv[None, :]
        cs, sn = np.cos(ang).astype(f), np.sin(ang).astype(f)
        to = lambda a: np.ascontiguousarray(a.reshape(32, 128, half).transpose(1, 0, 2))
        return to(cs), to(sn)
    c["ropeN_c"], c["ropeN_s"] = rope_tab(500000.0, 8)
    c["ropeR_c"], c["ropeR_s"] = rope_tab(10000.0, 64)
    c["expand"] = (np.arange(4096)[None, :] // 64 == np.arange(64)[:, None]).astype(f)
    cst = np.arange(256) * 16
    sst = np.arange(64) * 64
    ov = np.clip(np.minimum(cst[:, None] + 32, sst[None, :] + 64) - np.maximum(cst[:, None], sst[None, :]), 0, None) / 32.0
    ov[255] = 0.0
    c["ov"] = np.ascontiguousarray(ov.reshape(2, 128, 64).transpose(1, 0, 2)).astype(f)
    lo_c = 0 if s == 1 else 128
    lo_j = 0 if s == 1 else 32
    tq = (OB0 * 128 + np.arange(NO * 128))
    cm = (16 * np.arange(256)[:, None] + 31 <= tq[None, :]) & (np.arange(256)[:, None] >= lo_c) & (np.arange(256)[:, None] < 255)
    c["cmpmask"] = np.ascontiguousarray(cm.reshape(2, 128, NO, 128).transpose(1, 0, 2, 3)).astype(f)
    cur = tq // 64
    jb = np.arange(64)[None, :]
    valid = (jb <= cur[:, None]) & (jb >= lo_j)
    forced = ((jb == lo_j) | (jb == cur[:, None]) | (jb == cur[:, None] - 1)) & valid
    add = np.where(forced, 1e4 + jb, np.where(valid, 0.0, -1e4 - jb))
    vm = (valid & ~forced)
    c["selvalid"] = np.ascontiguousarray(vm.reshape(NO, 128, 64).transpose(1, 0, 2)).astype(f)
    c["seladd"] = np.ascontiguousarray(add.reshape(NO, 128, 64).transpose(1, 0, 2)).astype(f)
    c["selkeep"] = np.ascontiguousarray(valid.reshape(NO, 128, 64).transpose(1, 0, 2)).astype(f)
    k_, q_ = np.arange(128)[:, None], np.arange(128)[None, :]
    c["lowmask"] = (k_ <= q_).astype(f)
    c["upmask"] = (k_ > q_).astype(f)
    c["pbias"] = np.full((128, 1), 0.0 if s == 1 else -30000.0, dtype=f)
    g = np.array(GAMMAS, dtype=np.float64)
    i_ = np.arange(128)
    dk = 128 ** -0.5
    dec = np.where(i_[None, :] >= i_[:, None], g[:, None, None] ** np.maximum(i_[None, :] - i_[:, None], 0)[None], 0.0)
    c["decT"] = np.ascontiguousarray((dec * dk).transpose(1, 0, 2)).astype(f)
    c["qdecrow"] = np.ascontiguousarray(np.broadcast_to((g[:, None] ** (i_[None, :] + 1.0))[None], (128, 4, 128))).astype(f)
    c["kdec"] = np.ascontiguousarray((g[None, :] ** (127.0 - i_[:, None])) * dk).astype(f)
    return c


def kernel(x, norm_mix_w, w_in, cmp_pe_k, cmp_w1_k, cmp_w2_k, cmp_pe_v, cmp_w1_v, cmp_w2_v,
           w_nsa_branch, ret_gn_w, w_ret_branch, w_mix_out, norm_ffn_w, w_ffn_up, ffn_conv_w,
           ffn_conv_b, w_ffn_down, norm_final_w, _build_kwargs=None, _return_all=False):
    f = np.float32
    A = lambda a: np.ascontiguousarray(np.asarray(a, dtype=f))
    x = A(x)
    shared = {
        "w_in": A(w_in)[0], "cmp_w1_k": A(cmp_w1_k)[0], "cmp_w2_k": A(cmp_w2_k)[0], "cmp_w1_v": A(cmp_w1_v)[0],
        "cmp_w2_v": A(cmp_w2_v)[0], "w_nsa": A(w_nsa_branch)[0], "w_ret": A(w_ret_branch)[0], "w_mix": A(w_mix_out)[0],
        "w_up": A(w_ffn_up)[0], "w_dn": A(w_ffn_down)[0],
        "nmw": np.ascontiguousarray(A(norm_mix_w)[0].reshape(8, 128).T),
        "nfw": np.ascontiguousarray(A(norm_ffn_w)[0].reshape(8, 128).T),
        "nlw": np.ascontiguousarray(np.broadcast_to(A(norm_final_w)[None, :], (128, 1024))),
        "gnw": np.ascontiguousarray(np.broadcast_to(A(ret_gn_w)[0].reshape(1, 1024), (128, 1024))),
        "convw": np.ascontiguousarray(A(ffn_conv_w)[0].reshape(3, 44, 128).transpose(2, 0, 1)),
        "convb": np.ascontiguousarray(A(ffn_conv_b)[0].reshape(44, 128).T),
        "peTk": np.ascontiguousarray(np.concatenate([A(cmp_pe_k)[0].T] * 2, axis=0)),
        "peTv": np.ascontiguousarray(np.concatenate([A(cmp_pe_v)[0].T] * 2, axis=0)),
    }
    consts = [_consts(0), _consts(1)]
    nc, dbg = build(**(_build_kwargs or {}))
    in_maps = []
    for core in range(8):
        b, s = core // 2, core % 2
        xl = np.zeros((4096, 1024), dtype=f)
        if s == 1:
            xl[:] = x[b]
        else:
            xl[2048:] = x[b, :2048]
        m = dict(shared)
        m["xl"] = xl
        cs = consts[s]
        for k_, v in cs.items():
            if not k_.startswith("_"):
                m[k_] = v
        in_maps.append(m)
    res = run_bass_kernel_spmd(nc, in_maps, core_ids=list(range(8)))
    if _return_all:
        return res
    out = np.zeros((4, 4096, 1024), dtype=f)
    for core in range(8):
        b, s = core // 2, core % 2
        out[b, 2048 * s:2048 * (s + 1)] = res.results[core]["out"]
    return out
```

```python
import numpy as np
from contextlib import ExitStack
import concourse.bass as bass
import concourse.mybir as mybir
from concourse.bass_utils import run_bass_kernel_spmd

F32 = mybir.dt.float32
BF = mybir.dt.bfloat16
AF = mybir.ActivationFunctionType
ALU = mybir.AluOpType
AX = mybir.AxisListType

NB = 32
OB0 = 15
NO = 17
EPS = 1e-6
SEM_MAX = 12000
import os
_VAR = os.environ.get('KVAR', '')


class Eng:
    def __init__(self, nc, e, name, is_pe=False):
        self.nc, self.e, self.name, self.is_pe = nc, e, name, is_pe
        self.k = 0
        self.own = set()
        self.waited = {}
        self.new_sem()

    def new_sem(self):
        self.sem = self.nc.alloc_semaphore(f"{self.name}_c{self.k}")
        self.k += 1
        self.count = 0
        self.own.add(self.sem)


class T:
    def __init__(self, t, psum=False):
        self.t = t
        self.w = None
        self.r = {}
        self.psum = psum

    def __getitem__(self, k):
        return self.t[k]


class K:
    def __init__(self):
        nc = bass.Bass("TRN2", target_bir_lowering=False)
        self.nc = nc
        self.PE = Eng(nc, nc.tensor, "pe", True)
        self.ACT = Eng(nc, nc.scalar, "act")
        self.DVE = Eng(nc, nc.vector, "dve")
        self.POOL = Eng(nc, nc.gpsimd, "pool")
        self.SP = Eng(nc, nc.sync, "sp")
        self.dsems = [nc.alloc_semaphore(f"dma{i}") for i in range(20)]
        self.dvals = [0] * 20
        self.di = 0
        self.uid = 0

    def sb(self, stack, shape, dt, name=None):
        self.uid += 1
        t = stack.enter_context(self.nc.sbuf_tensor(f"{name or 't'}_{self.uid}", list(shape), dt))
        return T(t)

    def ps(self, stack, shape, dt, name=None):
        self.uid += 1
        t = stack.enter_context(self.nc.psum_tensor(f"{name or 'p'}_{self.uid}", list(shape), dt))
        return T(t, psum=True)

    def _waits(self, E, outs, ins):
        need = {}

        def add(sem, v):
            if need.get(sem, 0) < v:
                need[sem] = v
        for b in ins:
            if b.w is not None:
                add(*b.w)
            if b.psum:
                for sem, v in b.r.items():
                    if sem not in E.own:
                        add(sem, v)
        for b in outs:
            if b.w is not None:
                add(*b.w)
            for sem, v in b.r.items():
                add(sem, v)
        for sem, v in need.items():
            if E.is_pe and sem in E.own:
                continue
            if E.waited.get(sem, 0) >= v:
                continue
            E.e.wait_ge(sem, v)
            E.waited[sem] = v

    def _mark(self, d, outs, ins):
        sem, v = d
        for b in ins:
            if b.r.get(sem, 0) < v:
                b.r[sem] = v
        for b in outs:
            b.w = d
            b.r = {}

    def op(self, E, fn, outs=(), ins=()):
        self._waits(E, outs, ins)
        if E.count >= SEM_MAX:
            E.new_sem()
        i = fn(E.e)
        E.count += 1
        i.then_inc(E.sem, 1)
        self._mark((E.sem, E.count), outs, ins)

    def dma(self, Q, out_ap, in_ap, outs=(), ins=(), **kw):
        self._waits(Q, outs, ins)
        i = self.di
        self.di = (self.di + 1) % len(self.dsems)
        if self.dvals[i] >= SEM_MAX:
            if Q.waited.get(self.dsems[i], 0) < self.dvals[i]:
                Q.e.wait_ge(self.dsems[i], self.dvals[i])
            self.uid += 1
            self.dsems[i] = self.nc.alloc_semaphore(f"dmax{self.uid}")
            self.dvals[i] = 0
        sem, prev = self.dsems[i], self.dvals[i]
        if prev > 0 and Q.waited.get(sem, 0) < prev:
            Q.e.wait_ge(sem, prev)
            Q.waited[sem] = prev
        Q.e.dma_start(out=out_ap, in_=in_ap, **kw).then_inc(sem, 16)
        self.dvals[i] = prev + 16
        self._mark((sem, prev + 16), outs, ins)
        return (sem, prev + 16)

    def barrier(self):
        engs = [self.PE, self.ACT, self.DVE, self.POOL, self.SP]
        for E in engs:
            for P in engs:
                if P is E or P.count == 0:
                    continue
                if E.waited.get(P.sem, 0) < P.count:
                    E.e.wait_ge(P.sem, P.count)
                    E.waited[P.sem] = P.count
            for sem, v in zip(self.dsems, self.dvals):
                if v > 0 and E.waited.get(sem, 0) < v:
                    E.e.wait_ge(sem, v)
                    E.waited[sem] = v

    def mm(self, out_t, out_ap, lhsT_t, lhsT_ap, rhs_t, rhs_ap, start, stop, skip=False, extra=()):
        self.op(self.PE, lambda e: e.matmul(out_ap, lhsT=lhsT_ap, rhs=rhs_ap, start=start, stop=stop,
                                            skip_group_check=skip),
                outs=[out_t], ins=[lhsT_t, rhs_t] + list(extra))

    def tr(self, out_t, out_ap, in_t, in_ap, ident_t, ident_ap):
        self.op(self.PE, lambda e: e.transpose(out_ap, in_ap, ident_ap), outs=[out_t], ins=[in_t, ident_t])

    def V(self, fn, outs, ins):
        self.op(self.DVE, fn, outs, ins)

    def A(self, fn, outs, ins):
        self.op(self.ACT, fn, outs, ins)

    def G(self, fn, outs, ins):
        self.op(self.POOL, fn, outs, ins)


class Rot:
    def __init__(self, items):
        self.items = items
        self.i = 0

    def next(self):
        t = self.items[self.i]
        self.i = (self.i + 1) % len(self.items)
        return t


CONST_SPECS = {
    "ident": [128, 128], "nmw": [128, 8], "nfw": [128, 8], "nlw": [128, 1024],
    "ropeN_c": [128, 32, 8], "ropeN_s": [128, 32, 8], "ropeR_c": [128, 32, 64], "ropeR_s": [128, 32, 64],
    "expand": [64, 4096], "ov": [128, 2, 64], "cmpmask": [128, 2, 17, 128],
    "selvalid": [128, 17, 64], "seladd": [128, 17, 64], "selkeep": [128, 17, 64], "lowmask": [128, 128], "upmask": [128, 128],
    "pbias": [128, 1], "decT": [128, 4, 128], "qdecrow": [128, 4, 128], "kdec": [128, 4],
    "gnw": [128, 1024], "convw": [128, 3, 44], "convb": [128, 44], "peTk": [128, 32], "peTv": [128, 32],
}
WEIGHT_SPECS = {
    "w_in": [1024, 6424], "cmp_w1_k": [2048, 256], "cmp_w2_k": [256, 64], "cmp_w1_v": [2048, 256],
    "cmp_w2_v": [256, 64], "w_nsa": [512, 1024], "w_ret": [1024, 1024], "w_mix": [1024, 1024],
    "w_up": [1024, 5632], "w_dn": [2816, 1024],
}
GAMMAS = [1.0 - 2.0 ** (-5.0 - h) for h in range(4)]


class _Stop(Exception):
    pass


def build(stop_after=None, debug=()):
    kb = K()
    live = []
    top = ExitStack()
    try:
        dbg_out = _build_body(kb, top, live, stop_after, debug)
    except _Stop as e:
        dbg_out = e.args[0]
    return finish(kb, top, list(reversed(live)), dbg_out)


def _build_body(kb, top, live, stop_after, debug):
    nc = kb.nc
    PE, ACT, DVE, POOL, SP = kb.PE, kb.ACT, kb.DVE, kb.POOL, kb.SP
    D = {}
    D["xl"] = nc.dram_tensor("xl", [4096, 1024], F32, kind="ExternalInput").ap()
    for n, s in list(CONST_SPECS.items()) + list(WEIGHT_SPECS.items()):
        D[n] = nc.dram_tensor(n, s, F32, kind="ExternalInput").ap()
    out_d = nc.dram_tensor("out", [2048, 1024], F32, kind="ExternalOutput").ap()
    x1d = nc.dram_tensor("x1d", [NO * 128, 1024], F32, kind="Internal").ap()
    dbg_out = {}

    def ck(tag):
        if stop_after == tag:
            kb.barrier()
            for st_ in reversed(live):
                st_.close()
            del live[:]
            raise _Stop(dbg_out)

    def cload(stack, name, dt=F32, eng=None):
        shape = CONST_SPECS[name]
        t = kb.sb(stack, shape, F32, name)
        kb.dma(SP, t[:], D[name], outs=[t])
        if dt == F32:
            return t
        tb = kb.sb(stack, shape, dt, name + "b")
        kb.V(lambda e: e.tensor_copy(out=tb[:], in_=t[:]), [tb], [t])
        return tb

    identf = cload(top, "ident")
    identb = kb.sb(top, [128, 128], BF, "identb")
    kb.V(lambda e: e.tensor_copy(out=identb[:], in_=identf[:]), [identb], [identf])
    nmw = cload(top, "nmw")
    nfw = cload(top, "nfw")
    lowmask = cload(top, "lowmask", BF)
    upmask = cload(top, "upmask", BF)
    pbias = cload(top, "pbias")
    zbias = kb.sb(top, [128, 1], F32, "zbias")
    kb.V(lambda e: e.memset(zbias[:], 0.0), [zbias], [])

    psf = [kb.ps(top, [128, 512], F32, f"psf{i}") for i in range(7)]
    psb = kb.ps(top, [128, 1024], BF, "psb")
    PSS = Rot(psf[0:4])
    PSA = Rot(psf[4:7])
    PSALL = Rot(psf)

    xts = Rot([kb.sb(top, [128, 1024], F32, "xt") for _ in range(2)])
    xns = Rot([kb.sb(top, [128, 1024], BF, "xn") for _ in range(2)])
    hTs = Rot([kb.sb(top, [128, 8, 128], BF, "hT") for _ in range(2)])
    sq_junk = kb.sb(top, [128, 1024], BF, "sqj")
    stat = Rot([kb.sb(top, [128, 4], F32, "stat") for _ in range(4)])

    def norm_T(src_ap, wcol, want_xt=False, dst=None):
        xt = xts.next()
        kb.dma(SP, xt[:], src_ap, outs=[xt])
        st = stat.next()
        kb.V(lambda e: e.scalar_tensor_tensor(out=sq_junk[:], in0=xt[:], scalar=1.0, in1=xt[:], op0=ALU.mult,
                                              op1=ALU.mult, accum_out=st[:, 0:1]), [sq_junk, st], [xt])
        kb.V(lambda e: e.tensor_scalar(out=st[:, 1:2], in0=st[:, 0:1], scalar1=1.0 / 1024, scalar2=EPS,
                                       op0=ALU.mult, op1=ALU.add), [st], [st])
        kb.A(lambda e: e.activation(out=st[:, 3:4], in_=st[:, 1:2], func=AF.Sqrt), [st], [st])
        kb.V(lambda e: e.reciprocal(out=st[:, 2:3], in_=st[:, 3:4]), [st], [st])
        xn = xns.next()
        kb.A(lambda e: e.activation(out=xn[:], in_=xt[:], func=AF.Copy, scale=st[:, 2:3]), [xn], [xt, st])
        for c in range(8):
            kb.tr(psb, psb[:, c * 128:(c + 1) * 128], xn, xn[:, c * 128:(c + 1) * 128], identb, identb[:])
        if dst is None:
            hT = hTs.next()
            hT_ap = hT[:]
        else:
            hT, hT_ap = dst
        kb.V(lambda e: e.tensor_tensor(out=hT_ap, in0=psb[:].rearrange("p (c t) -> p c t", c=8),
                                       in1=wcol[:].unsqueeze(2).to_broadcast([128, 8, 128]), op=ALU.mult),
             [hT], [psb, wcol])
        if want_xt:
            return hT, xt, st
        return hT

    class Staging:
        def __enter__(self):
            self.ws = ExitStack()
            live.append(self.ws)
            self.rot = Rot([kb.sb(self.ws, [128, 2048], F32, "stg") for _ in range(4)])
            return self.rot

        def __exit__(self, *a):
            kb.barrier()
            live.remove(self.ws)
            self.ws.close()
            return False

    class Prefetch:
        def __init__(self, srcs, wcol, want_xt=False):
            self.srcs, self.wcol, self.want_xt = srcs, wcol, want_xt
            self.nxt = norm_T(srcs[0], wcol, want_xt) if srcs else None
            self.i = 0

        def cur(self):
            return self.nxt

        def prefetch(self):
            self.i += 1
            if self.i < len(self.srcs):
                self.nxt = norm_T(self.srcs[self.i], self.wcol, self.want_xt)

    cast_rr = [0]

    def load_w(stg, dst, dst_ap_fn, src3d, kc, ncols):
        step = 1 << ((2048 // kc).bit_length() - 1)
        for c0 in range(0, ncols, step):
            n = min(step, ncols - c0)
            s = stg.next()
            sv = s[:, 0:kc * n].rearrange("p (c n) -> p c n", c=kc)
            kb.dma(SP, sv, src3d[:, :, c0:c0 + n], outs=[s])
            cast_rr[0] = (cast_rr[0] + 1) % 3
            if cast_rr[0] == 0:
                kb.G(lambda e: e.tensor_copy(out=dst_ap_fn(c0, n), in_=sv), [dst], [s])
            elif cast_rr[0] == 1:
                kb.V(lambda e: e.tensor_copy(out=dst_ap_fn(c0, n), in_=sv), [dst], [s])
            else:
                kb.A(lambda e: e.copy(out=dst_ap_fn(c0, n), in_=sv), [dst], [s])

    def dump(name, t, ap, shape):
        if name in debug:
            d = nc.dram_tensor("dbg_" + name, list(shape), ap.dtype, kind="ExternalOutput").ap()
            kb.dma(SP, d, ap, ins=[t])
            dbg_out[name] = d

    w_in3 = D["w_in"].rearrange("(c p) n -> p c n", p=128)
    xl_blk = D["xl"].rearrange("(b p) f -> b p f", p=128)

    mixs = ExitStack()
    live.append(mixs)
    OAT = kb.sb(mixs, [128, 4, NO * 128], BF, "OAT")
    nsa = ExitStack()
    live.append(nsa)
    KE = [kb.sb(nsa, [128, 4096], BF, f"KE{g}") for g in range(2)]
    KW = kb.sb(nsa, [128, 4096], BF, "KW")
    VS = kb.sb(nsa, [128, 32, 2, 66], BF, "VS")
    VW = kb.sb(nsa, [128, 32, 2, 66], BF, "VW")
    KCT = kb.sb(nsa, [128, 256], BF, "KCT")
    VCA = kb.sb(nsa, [128, 2, 2, 128], BF, "VCA")

    kb.V(lambda e: e.memset(VS[:], 1.0), [VS], [])
    kb.V(lambda e: e.memset(VW[:], 1.0), [VW], [])
    kb.V(lambda e: e.memset(KCT[:], 0.0), [KCT], [])
    with Staging() as stg:
        for g, rows in ((0, slice(64, 128)), (1, slice(0, 64))):
            for c0 in range(0, 4096, 2048):
                s = stg.next()
                kb.dma(SP, s[rows, 0:2048], D["expand"][:, c0:c0 + 2048], outs=[s])
                kb.G(lambda e: e.tensor_copy(out=KE[g][rows, c0:c0 + 2048], in_=s[rows, 0:2048]), [KE[g]], [s])

    ck(0.3)
    with ExitStack() as p1:
        live.append(p1)
        Wkv = kb.sb(p1, [128, 8, 768], BF, "Wkv")
        W1 = {nm: kb.sb(p1, [128, 32, 256], BF, "W1" + nm) for nm in ("k", "v")}
        W2k = kb.sb(p1, [128, 2, 128], BF, "W2k")
        W2v = kb.sb(p1, [128, 2, 64], BF, "W2v")
        peT = {"k": cload(p1, "peTk", BF), "v": cload(p1, "peTv", BF)}
        ropeN_c = cload(p1, "ropeN_c")
        ropeN_s = cload(p1, "ropeN_s")
        ovb = cload(p1, "ov", BF)
        for g in range(2):
            for ct in range(2):
                kb.V(lambda e: e.tensor_copy(out=VCA[:, g, ct, 64:128], in_=ovb[:, ct, :]), [VCA], [ovb])
        with Staging() as stg:
            load_w(stg, Wkv, lambda c0, n: Wkv[:, :, c0:c0 + n], w_in3[:, :, 512:1280], 8, 768)
            for nm in ("k", "v"):
                src = D["cmp_w1_" + nm].rearrange("(l d) m -> d l m", d=64)
                for half in range(2):
                    rows = slice(64 * half, 64 * half + 64)
                    for l0 in range(0, 32, 8):
                        s = stg.next()
                        sv = s[rows, :].rearrange("p (l m) -> p l m", l=8)
                        kb.dma(SP, sv, src[:, l0:l0 + 8, :], outs=[s])
                        kb.G(lambda e: e.tensor_copy(out=W1[nm][rows, l0:l0 + 8, :], in_=sv), [W1[nm]], [s])
            s = stg.next()
            kb.dma(SP, s[:, 0:128].rearrange("p (c d) -> p c d", c=2),
                   D["cmp_w2_k"].rearrange("(c p) d -> p c d", p=128), outs=[s])
            kb.dma(SP, s[:, 128:256].rearrange("p (c d) -> p c d", c=2),
                   D["cmp_w2_v"].rearrange("(c p) d -> p c d", p=128), outs=[s])
            sk = s[:, 0:128].rearrange("p (c d) -> p c d", c=2)
            kb.G(lambda e: e.tensor_copy(out=W2k[:, :, 0:64], in_=sk), [W2k], [s])
            kb.G(lambda e: e.tensor_copy(out=W2k[:, :, 64:128], in_=sk), [W2k], [s])
            kb.G(lambda e: e.tensor_copy(out=W2v[:], in_=s[:, 128:256].rearrange("p (c d) -> p c d", c=2)), [W2v], [s])
        KC = kb.sb(p1, [128, 4096], BF, "KC")
        VC = kb.sb(p1, [128, 4096], BF, "VC")
        ktin = Rot([kb.sb(p1, [128, 4, 128], BF, "ktin") for _ in range(2)])
        rtmp = Rot([kb.sb(p1, [128, 2, 8], F32, "rtmp") for _ in range(16)])

        ck(0.5)
        if stop_after == 0.55:
            hT = norm_T(xl_blk[17], nmw)
            dump("hT", hT, hT[:], [128, 8, 128])
            ck(0.55)
        for blk in range(NB if not (isinstance(stop_after, float) and 0.6 <= stop_after < 0.7) else 1):
            if blk == 0:
                pf1 = Prefetch([xl_blk[b_] for b_ in range(NB)], nmw)
            hT = pf1.cur()
            pa, pb = PSS.next(), PSS.next()
            for c in range(8):
                kb.mm(pa, pa[:, 0:512], hT, hT[:, c, :], Wkv, Wkv[:, c, 0:512], c == 0, c == 7)
            for c in range(8):
                kb.mm(pb, pb[:, 0:256], hT, hT[:, c, :], Wkv, Wkv[:, c, 512:768], c == 0, c == 7)
            pf1.prefetch()
            ck(0.61)
            kt = ktin.next()
            kb.A(lambda e: e.copy(out=kt[:, 0:2, :], in_=pa[:, 0:256].rearrange("p (a d) -> p a d", a=2)), [kt], [pa])
            kb.A(lambda e: e.copy(out=VS[:, blk, :, 0:64], in_=pa[:, 384:512].rearrange("p (g d) -> p g d", g=2)),
                 [VS], [pa])
            kb.A(lambda e: e.copy(out=VW[:, blk, :, 0:64], in_=pb[:, 128:256].rearrange("p (g d) -> p g d", g=2)),
                 [VW], [pb])
            ck(0.62)
            for slot, (pt, lo) in ((2, (pa, 256)), (3, (pb, 0))):
                src3 = pt[:, lo:lo + 128].rearrange("p (g d) -> p g d", g=2)
                dst3 = kt[:, slot, :].rearrange("p (g d) -> p g d", g=2)
                cb = ropeN_c[:, blk, :].unsqueeze(1).to_broadcast([128, 2, 8])
                sb_ = ropeN_s[:, blk, :].unsqueeze(1).to_broadcast([128, 2, 8])
                x1, x2 = src3[:, :, 0:8], src3[:, :, 8:16]
                t1, t2, t3, t4 = rtmp.next(), rtmp.next(), rtmp.next(), rtmp.next()
                kb.V(lambda e: e.tensor_tensor(out=t1[:], in0=x1, in1=cb, op=ALU.mult), [t1], [pt, ropeN_c])
                kb.V(lambda e: e.tensor_tensor(out=t2[:], in0=x2, in1=sb_, op=ALU.mult), [t2], [pt, ropeN_s])
                kb.V(lambda e: e.tensor_tensor(out=t3[:], in0=x1, in1=sb_, op=ALU.mult), [t3], [pt, ropeN_s])
                kb.V(lambda e: e.tensor_tensor(out=t4[:], in0=x2, in1=cb, op=ALU.mult), [t4], [pt, ropeN_c])
                kb.V(lambda e: e.tensor_tensor(out=dst3[:, :, 0:8], in0=t1[:], in1=t2[:], op=ALU.subtract), [kt], [t1, t2])
                kb.V(lambda e: e.tensor_tensor(out=dst3[:, :, 8:16], in0=t3[:], in1=t4[:], op=ALU.add), [kt], [t3, t4])
                if "noactcopy" in _VAR:
                    kb.V(lambda e: e.tensor_copy(out=dst3[:, :, 16:64], in_=src3[:, :, 16:64]), [kt], [pt])
                else:
                    kb.A(lambda e: e.copy(out=dst3[:, :, 16:64], in_=src3[:, :, 16:64]), [kt], [pt])
            ck(0.63)
            for slot in range(4):
                kb.tr(psb, psb[:, slot * 128:(slot + 1) * 128], kt, kt[:, slot, :], identb, identb[:])
            ck(0.64)
            cs = slice(blk * 128, (blk + 1) * 128)
            kb.V(lambda e: e.tensor_copy(out=KC[:, cs], in_=psb[:, 0:128]), [KC], [psb])
            kb.V(lambda e: e.tensor_copy(out=VC[:, cs], in_=psb[:, 128:256]), [VC], [psb])
            kb.V(lambda e: e.tensor_copy(out=KE[0][0:64, cs], in_=psb[0:64, 256:384]), [KE[0]], [psb])
            kb.V(lambda e: e.tensor_copy(out=KE[1][64:128, cs], in_=psb[64:128, 256:384]), [KE[1]], [psb])
            kb.V(lambda e: e.tensor_copy(out=KW[:, cs], in_=psb[:, 384:512]), [KW], [psb])

        dump("KE0", KE[0], KE[0][:], [128, 4096])
        dump("KE1", KE[1], KE[1][:], [128, 4096])
        dump("KW", KW, KW[:], [128, 4096])
        dump("KC", KC, KC[:], [128, 4096])
        dump("VS", VS, VS[:], [128, 32, 2, 66])

        ck(0.6)
        ck(0.7)
        GT = kb.sb(p1, [128, 2, 256], BF, "GT")
        hb = kb.sb(p1, [128, 1], F32, "hb")
        gx = [kb.sb(p1, [128, 255], F32, f"gx{i}") for i in range(4)]
        for nm, src in (("k", KC), ("v", VC)):
            for g in range(2):
                rows = slice(64 * g, 64 * g + 64)
                kb.V(lambda e: e.memset(GT[:], 0.0), [GT], [])
                for mc in range(2):
                    ph, pbi = PSS.next(), PSS.next()
                    for l in range(32):
                        kb.mm(ph, ph[:, 0:255], W1[nm], W1[nm][rows, l, mc * 128:(mc + 1) * 128],
                              src, src[rows, l:l + 16 * 254 + 1:16], l == 0, l == 31)
                    for l in range(32):
                        kb.mm(pbi, pbi[:, 0:1], W1[nm], W1[nm][rows, l, mc * 128:(mc + 1) * 128],
                              peT[nm], peT[nm][rows, l:l + 1], l == 0, l == 31)
                    kb.V(lambda e: e.tensor_copy(out=hb[:], in_=pbi[:, 0:1]), [hb], [pbi])
                    x0, x2_, x3_, sg = gx
                    kb.V(lambda e: e.tensor_scalar(out=x0[:], in0=ph[:, 0:255], scalar1=hb[:, 0:1], scalar2=None,
                                                   op0=ALU.add), [x0], [ph, hb])
                    kb.V(lambda e: e.tensor_tensor(out=x2_[:], in0=x0[:], in1=x0[:], op=ALU.mult), [x2_], [x0])
                    kb.V(lambda e: e.tensor_scalar(out=x2_[:], in0=x2_[:], scalar1=0.044715, scalar2=1.0,
                                                   op0=ALU.mult, op1=ALU.add), [x2_], [x2_])
                    kb.V(lambda e: e.tensor_tensor(out=x3_[:], in0=x2_[:], in1=x0[:], op=ALU.mult), [x3_], [x2_, x0])
                    kb.A(lambda e: e.activation(out=sg[:], in_=x3_[:], func=AF.Sigmoid, scale=1.5957691216057308),
                         [sg], [x3_])
                    kb.V(lambda e: e.tensor_tensor(out=GT[:, mc, 0:255], in0=x0[:], in1=sg[:], op=ALU.mult),
                         [GT], [x0, sg])
                if nm == "k":
                    pk = PSS.next()
                    for mc in range(2):
                        kb.mm(pk, pk[:, 0:255], W2k, W2k[:, mc, :], GT, GT[:, mc, 0:255], mc == 0, mc == 1)
                    kb.V(lambda e: e.tensor_copy(out=KCT[rows, 0:255], in_=pk[rows, 0:255]), [KCT], [pk])
                else:
                    for ct in range(2):
                        pv = PSS.next()
                        for mc in range(2):
                            kb.mm(pv, pv[:, 0:64], GT, GT[:, mc, ct * 128:(ct + 1) * 128], W2v, W2v[:, mc, :],
                                  mc == 0, mc == 1)
                        kb.V(lambda e: e.tensor_copy(out=VCA[:, g, ct, 0:64], in_=pv[:, 0:64]), [VCA], [pv])
        dump("KCT", KCT, KCT[:], [128, 256])
        dump("VCA", VCA, VCA[:], [128, 2, 2, 128])
        kb.barrier()
        live.remove(p1)
    ck(1)

    with ExitStack() as p2:
        live.append(p2)
        Wq = kb.sb(p2, [128, 8, 536], BF, "Wq")
        with Staging() as stg:
            load_w(stg, Wq, lambda c0, n: Wq[:, :, c0:c0 + n], w_in3[:, :, 0:512], 8, 512)
            load_w(stg, Wq, lambda c0, n: Wq[:, :, 512 + c0:512 + c0 + n], w_in3[:, :, 1280:1304], 8, 24)
        ropeN_c = cload(p2, "ropeN_c")
        ropeN_s = cload(p2, "ropeN_s")
        cmpmask = cload(p2, "cmpmask", BF)
        selvalid = cload(p2, "selvalid")
        seladd = cload(p2, "seladd")
        selkeep = cload(p2, "selkeep")
        def qb_pair(g):
            t = kb.sb(p2, [128, 512], BF, f"QB{g}")
            return (t, T(t.t))
        QB = [Rot([qb_pair(g) for _ in range(2)]) for g in range(2)]
        qr = Rot([kb.sb(p2, [128, 4, 128], BF, "qr") for _ in range(2)])
        gat = Rot([kb.sb(p2, [128, 24], F32, "gat") for _ in range(2)])
        rt = Rot([kb.sb(p2, [128, 8, 8], F32, "rt") for _ in range(8)])
        PTs = Rot([kb.sb(p2, [128, 512], BF, "PT") for _ in range(4)])
        sm = Rot([kb.sb(p2, [128, 16], F32, "sm") for _ in range(12)])
        impt = Rot([kb.sb(p2, [128, 64], F32, "imp") for _ in range(8)])
        m8 = Rot([kb.sb(p2, [128, 8], F32, "m8") for _ in range(4)])
        ZS = [kb.sb(p2, [128, 128], BF, f"ZS{g}") for g in range(2)]
        for g in range(2):
            kb.V(lambda e: e.memset(ZS[g][:], 0.0), [ZS[g]], [])
        oacc = Rot([kb.sb(p2, [128, 8, 64], F32, "oacc") for _ in range(2)])
        otmp = Rot([kb.sb(p2, [128, 4, 64], F32, "otmp") for _ in range(3)])
        oab = Rot([kb.sb(p2, [128, 512], BF, "oab") for _ in range(2)])

        def coef_from(rs_ap, rs_t, gate_ap, gate_t):
            c = sm.next()
            kb.V(lambda e: e.tensor_scalar(out=c[:, 0:4], in0=rs_ap, scalar1=1e-30, scalar2=None, op0=ALU.max),
                 [c], [rs_t])
            kb.V(lambda e: e.reciprocal(out=c[:, 4:8], in_=c[:, 0:4]), [c], [c])
            kb.V(lambda e: e.tensor_tensor(out=c[:, 8:12], in0=c[:, 4:8], in1=gate_ap, op=ALU.mult), [c], [c, gate_t])
            return c

        def accumulate(oa, g, first, ps_t, o_view, coef):
            cb = coef[:, 8:12].unsqueeze(2).to_broadcast([128, 4, 64])
            if first:
                kb.V(lambda e: e.tensor_tensor(out=oa[:, 4 * g:4 * g + 4, :], in0=o_view, in1=cb, op=ALU.mult),
                     [oa], [ps_t, coef])
            else:
                tm = otmp.next()
                kb.V(lambda e: e.tensor_tensor(out=tm[:], in0=o_view, in1=cb, op=ALU.mult), [tm], [ps_t, coef])
                kb.V(lambda e: e.tensor_tensor(out=oa[:, 4 * g:4 * g + 4, :], in0=oa[:, 4 * g:4 * g + 4, :],
                                               in1=tm[:], op=ALU.add), [oa], [oa, tm])

        for o in range(NO):
            blk = OB0 + o
            if o == 0:
                pf2 = Prefetch([xl_blk[OB0 + o_] for o_ in range(NO)], nmw)
            hT = pf2.cur()
            pq, pg = PSS.next(), PSS.next()
            for c in range(8):
                kb.mm(pq, pq[:, 0:512], hT, hT[:, c, :], Wq, Wq[:, c, 0:512], c == 0, c == 7)
            for c in range(8):
                kb.mm(pg, pg[:, 0:24], hT, hT[:, c, :], Wq, Wq[:, c, 512:536], c == 0, c == 7)
            pf2.prefetch()
            ga = gat.next()
            kb.A(lambda e: e.activation(out=ga[:], in_=pg[:, 0:24], func=AF.Sigmoid), [ga], [pg])
            gav = ga[:].rearrange("p (h b) -> p h b", b=3)
            q_ = qr.next()
            src4 = pq[:, 0:512].rearrange("p (j hp d) -> p j hp d", j=2, hp=4)
            dst4 = q_[:].rearrange("p hp (j d) -> p j hp d", j=2)
            cb = ropeN_c[:, blk, :].unsqueeze(1).unsqueeze(1).to_broadcast([128, 2, 4, 8])
            sb_ = ropeN_s[:, blk, :].unsqueeze(1).unsqueeze(1).to_broadcast([128, 2, 4, 8])
            x1, x2 = src4[:, :, :, 0:8], src4[:, :, :, 8:16]
            t1, t2, t3, t4 = rt.next(), rt.next(), rt.next(), rt.next()
            tv = lambda t: t[:].rearrange("p (j hp) d -> p j hp d", j=2)
            kb.V(lambda e: e.tensor_tensor(out=tv(t1), in0=x1, in1=cb, op=ALU.mult), [t1], [pq, ropeN_c])
            kb.V(lambda e: e.tensor_tensor(out=tv(t2), in0=x2, in1=sb_, op=ALU.mult), [t2], [pq, ropeN_s])
            kb.V(lambda e: e.tensor_tensor(out=tv(t3), in0=x1, in1=sb_, op=ALU.mult), [t3], [pq, ropeN_s])
            kb.V(lambda e: e.tensor_tensor(out=tv(t4), in0=x2, in1=cb, op=ALU.mult), [t4], [pq, ropeN_c])
            kb.V(lambda e: e.tensor_tensor(out=dst4[:, :, :, 0:8], in0=tv(t1), in1=tv(t2), op=ALU.subtract), [q_], [t1, t2])
            kb.V(lambda e: e.tensor_tensor(out=dst4[:, :, :, 8:16], in0=tv(t3), in1=tv(t4), op=ALU.add), [q_], [t3, t4])
            kb.A(lambda e: e.copy(out=dst4[:, :, :, 16:64], in_=src4[:, :, :, 16:64]), [q_], [pq])
            for h in range(4):
                kb.tr(psb, psb[:, h * 128:(h + 1) * 128], q_, q_[:, h, :], identb, identb[:])
            qpair = [QB[0].next(), QB[1].next()]
            qq = [qpair[0][0], qpair[1][0]]
            qbias = [qpair[0][1], qpair[1][1]]
            kb.V(lambda e: e.tensor_copy(out=qq[0][0:64, :], in_=psb[0:64, 0:512]), [qq[0]], [psb])
            kb.V(lambda e: e.tensor_copy(out=qq[1][64:128, :], in_=psb[64:128, 0:512]), [qq[1]], [psb])
            oa = oacc.next()
            P4V = lambda t: t[:].rearrange("p (h q) -> p h q", h=4)

            def fin_cmp(g, pso):
                pv4 = pso[:].rearrange("p (h x) -> p h x", h=4)
                rs = sm.next()
                kb.V(lambda e: e.tensor_reduce(out=rs[:, 0:4], in_=pv4[:, :, 64:128], axis=AX.X, op=ALU.add),
                     [rs], [pso])
                cf = coef_from(rs[:, 0:4], rs, gav[:, 4 * g:4 * g + 4, 0], ga)
                accumulate(oa, g, True, pso, pv4[:, :, 0:64], cf)
                im = impt.next()
                kb.V(lambda e: e.tensor_scalar(out=im[:], in0=pv4[:, 0, 64:128], scalar1=cf[:, 4:5], scalar2=None,
                                               op0=ALU.mult), [im], [pso, cf])
                for h in range(1, 4):
                    kb.V(lambda e: e.scalar_tensor_tensor(out=im[:], in0=pv4[:, h, 64:128], scalar=cf[:, 4 + h:5 + h],
                                                          in1=im[:], op0=ALU.mult, op1=ALU.add), [im], [pso, cf, im])
                imf = impt.next()
                kb.V(lambda e: e.tensor_tensor(out=imf[:], in0=im[:], in1=selvalid[:, o, :], op=ALU.mult),
                     [imf], [im, selvalid])
                kb.V(lambda e: e.tensor_tensor(out=imf[:], in0=imf[:], in1=seladd[:, o, :], op=ALU.add),
                     [imf], [imf, seladd])
                ma, mb = m8.next(), m8.next()
                im2 = impt.next()
                kb.V(lambda e: e.max(out=ma[:], in_=imf[:]), [ma], [imf])
                kb.V(lambda e: e.match_replace(out=im2[:], in_to_replace=ma[:], in_values=imf[:], imm_value=-1e9),
                     [im2], [ma, imf])
                kb.V(lambda e: e.max(out=mb[:], in_=im2[:]), [mb], [im2])
                sel = impt.next()
                kb.V(lambda e: e.tensor_scalar(out=sel[:], in0=imf[:], scalar1=mb[:, 7:8], scalar2=None,
                                               op0=ALU.is_ge), [sel], [imf, mb])
                kb.V(lambda e: e.tensor_tensor(out=sel[:], in0=sel[:], in1=selkeep[:, o, :], op=ALU.mult),
                     [sel], [sel, selkeep])
                zc = slice(64, 128) if g == 0 else slice(0, 64)
                kb.V(lambda e: e.tensor_scalar(out=ZS[g][:, zc], in0=sel[:], scalar1=1.0, scalar2=1e5,
                                               op0=ALU.subtract, op1=ALU.mult), [ZS[g]], [sel])

            def put_bias(g):
                zc = slice(64, 128) if g == 0 else slice(0, 64)
                kb.tr(psb, psb[:, 512 + g * 128:512 + (g + 1) * 128], ZS[g], ZS[g][:], identb, identb[:])
                kb.V(lambda e: e.tensor_copy(
                    out=qbias[g][zc, :].rearrange("p (h q) -> p h q", h=4),
                    in_=psb[zc, 512 + g * 128:512 + (g + 1) * 128].unsqueeze(1).to_broadcast([64, 4, 128])),
                    [qbias[g]], [psb])

            def fin_soft(g, ps_t, br):
                pv4 = ps_t[:, 0:264].rearrange("p (h x) -> p h x", h=4)
                cf = coef_from(pv4[:, :, 64], ps_t, gav[:, 4 * g:4 * g + 4, br], ga)
                accumulate(oa, g, False, ps_t, pv4[:, :, 0:64], cf)

            jobs = []
            for g in range(2):
                rows = slice(64 * g, 64 * g + 64)
                for ct in range(2):
                    jobs.append(dict(kind="cmp", g=g, first=ct == 0, last=ct == 1, w=128,
                                     lhsT=(KCT, KCT[rows, ct * 128:(ct + 1) * 128]), rhs=(qq[g], qq[g][rows, :]), extra=(),
                                     bias=None, mask=(cmpmask, cmpmask[:, ct, o, :]), v=(VCA, VCA[:, g, ct, :])))
            for g in range(2):
                rows = slice(64 * g, 64 * g + 64)
                for i, kt in enumerate(range(blk - 4, blk + 1)):
                    mk = (upmask, upmask[:]) if i == 0 else ((lowmask, lowmask[:]) if i == 4 else None)
                    jobs.append(dict(kind="win", g=g, first=i == 0, last=i == 4, w=66,
                                     lhsT=(KW, KW[rows, kt * 128:(kt + 1) * 128]), rhs=(qq[g], qq[g][rows, :]), extra=(),
                                     bias=pbias if kt < 16 else None, mask=mk, v=(VW, VW[:, kt, g, :]),
                                     bias_after=(i == 2)))
            for g in range(2):
                for kt in range(blk + 1):
                    mk = (lowmask, lowmask[:]) if kt == blk else None
                    jobs.append(dict(kind="slc", g=g, first=kt == 0, last=kt == blk, w=66,
                                     lhsT=(KE[g], KE[g][:, kt * 128:(kt + 1) * 128]), rhs=(qq[g], qpair[g][0][:, :]),
                                     extra=(qbias[g],), bias=None, mask=mk, v=(VS, VS[:, kt, g, :])))

            def stageA(j):
                pS = PSS.next()
                kb.mm(pS, pS[:, :], j["lhsT"][0], j["lhsT"][1], j["rhs"][0], j["rhs"][1], True, True, extra=j["extra"])
                pt = PTs.next()
                if j["bias"] is None:
                    kb.A(lambda e: e.activation(out=pt[:], in_=pS[:], func=AF.Exp, scale=0.125), [pt], [pS])
                else:
                    bt = j["bias"]
                    kb.A(lambda e: e.activation(out=pt[:], in_=pS[:], func=AF.Exp, scale=0.125, bias=bt[:, 0:1]),
                         [pt], [pS, bt])
                if j["mask"] is not None:
                    mt_, map_ = j["mask"]
                    kb.V(lambda e: e.tensor_tensor(out=P4V(pt), in0=P4V(pt),
                                                   in1=map_.unsqueeze(1).to_broadcast([128, 4, 128]), op=ALU.mult),
                         [pt], [pt, mt_])
                j["pt"] = pt

            acc = {}

            def stageC(j):
                key = (j["kind"], j["g"])
                if j["first"]:
                    acc[key] = PSA.next()
                pa_ = acc[key]
                w, pt = j["w"], j["pt"]
                for h in range(4):
                    kb.mm(pa_, pa_[:, h * w:(h + 1) * w], pt, pt[:, h * 128:(h + 1) * 128], j["v"][0], j["v"][1],
                          j["first"] and h == 0, j["last"] and h == 3, skip=True)
                if j.get("bias_after"):
                    put_bias(j["g"])
                if j["last"]:
                    if j["kind"] == "cmp":
                        fin_cmp(j["g"], pa_)
                    else:
                        fin_soft(j["g"], pa_, 2 if j["kind"] == "win" else 1)

            LA = 2
            for i in range(len(jobs) + LA):
                if i < len(jobs):
                    stageA(jobs[i])
                if i >= LA:
                    stageC(jobs[i - LA])

            ob = oab.next()
            kb.V(lambda e: e.tensor_copy(out=ob[:], in_=oa[:].rearrange("p h d -> p (h d)")), [ob], [oa])
            if o == 1:
                dump("oa", oa, oa[:], [128, 8, 64])
            for c in range(4):
                kb.tr(psb, psb[:, c * 128:(c + 1) * 128], ob, ob[:, c * 128:(c + 1) * 128], identb, identb[:])
            kb.V(lambda e: e.tensor_copy(out=OAT[:, :, o * 128:(o + 1) * 128],
                                         in_=psb[:, 0:512].rearrange("p (c t) -> p c t", c=4)), [OAT], [psb])
        kb.barrier()
        live.remove(p2)
    nsa.close()
    live.remove(nsa)
    ck(2)

    ORT = kb.sb(mixs, [128, 8, NO * 128], BF, "ORT")
    with ExitStack() as p3:
        live.append(p3)
        Wr = kb.sb(p3, [128, 8, 3072], BF, "Wr")
        with Staging() as stg:
            load_w(stg, Wr, lambda c0, n: Wr[:, :, c0:c0 + n], w_in3[:, :, 1304:4376], 8, 3072)
        rtc = Rot([kb.sb(p3, [128, 64], F32, "rtc") for _ in range(2)])
        rts = Rot([kb.sb(p3, [128, 64], F32, "rts") for _ in range(2)])
        cur_rt = {}
        decT = cload(p3, "decT")
        qdecrow = cload(p3, "qdecrow")
        kdec = cload(p3, "kdec")
        gnw = cload(p3, "gnw")
        ST = kb.sb(p3, [128, 4, 256], F32, "ST")
        STb = kb.sb(p3, [128, 4, 256], BF, "STb")
        kb.V(lambda e: e.memset(ST[:], 0.0), [ST], [])
        kb.V(lambda e: e.memset(STb[:], 0.0), [STb], [])
        rr = Rot([kb.sb(p3, [128, 4, 64], F32, "rr") for _ in range(4)])
        krs = Rot([kb.sb(p3, [128, 4, 128], F32, "kr") for _ in range(3)])
        KDs = Rot([kb.sb(p3, [128, 512], BF, "KD") for _ in range(2)])
        VBs = Rot([kb.sb(p3, [128, 1024], BF, "VB") for _ in range(2)])
        qkb = Rot([kb.sb(p3, [128, 2, 512], BF, "qkb") for _ in range(2)])
        QKT = Rot([kb.sb(p3, [128, 8, 128], BF, "QKT") for _ in range(2)])
        QTd = Rot([kb.sb(p3, [128, 4, 128], BF, "QTd") for _ in range(2)])
        SCs = Rot([kb.sb(p3, [128, 4, 128], BF, "SC") for _ in range(2)])
        sgs = Rot([kb.sb(p3, [128, 1024], F32, "sg") for _ in range(1)])
        ys = Rot([kb.sb(p3, [128, 1024], F32, "y") for _ in range(1)])
        orb = Rot([kb.sb(p3, [128, 1024], BF, "orb") for _ in range(2)])
        bst = Rot([kb.sb(p3, [128, 4, 6], F32, "bst") for _ in range(2)])
        mv = Rot([kb.sb(p3, [128, 4, 4], F32, "mv") for _ in range(2)])

        def rope_full(dst3, dst_t, pt, blk):
            src3 = pt[:, 0:512].rearrange("p (h d) -> p h d", h=4)
            if cur_rt.get("blk") != blk:
                ropeR_c, ropeR_s = rtc.next(), rts.next()
                kb.dma(SP, ropeR_c[:], D["ropeR_c"][:, blk, :], outs=[ropeR_c])
                kb.dma(SP, ropeR_s[:], D["ropeR_s"][:, blk, :], outs=[ropeR_s])
                cur_rt.update(blk=blk, c=ropeR_c, s=ropeR_s)
            ropeR_c, ropeR_s = cur_rt["c"], cur_rt["s"]
            cb = ropeR_c[:].unsqueeze(1).to_broadcast([128, 4, 64])
            sb_ = ropeR_s[:].unsqueeze(1).to_broadcast([128, 4, 64])
            x1, x2 = src3[:, :, 0:64], src3[:, :, 64:128]
            t1, t2 = rr.next(), rr.next()
            kb.V(lambda e: e.tensor_tensor(out=t1[:], in0=x1, in1=cb, op=ALU.mult), [t1], [pt, ropeR_c])
            kb.V(lambda e: e.tensor_tensor(out=t2[:], in0=x2, in1=sb_, op=ALU.mult), [t2], [pt, ropeR_s])
            kb.V(lambda e: e.tensor_tensor(out=dst3[:, :, 0:64], in0=t1[:], in1=t2[:], op=ALU.subtract),
                 [dst_t], [t1, t2])
            t3, t4 = rr.next(), rr.next()
            kb.V(lambda e: e.tensor_tensor(out=t3[:], in0=x1, in1=sb_, op=ALU.mult), [t3], [pt, ropeR_s])
            kb.V(lambda e: e.tensor_tensor(out=t4[:], in0=x2, in1=cb, op=ALU.mult), [t4], [pt, ropeR_c])
            kb.V(lambda e: e.tensor_tensor(out=dst3[:, :, 64:128], in0=t3[:], in1=t4[:], op=ALU.add),
                 [dst_t], [t3, t4])

        pf3 = Prefetch([xl_blk[b_] for b_ in range(NB)], nmw)
        for blk in range(NB):
            o = blk - OB0
            hT = pf3.cur()
            pk = PSALL.next()
            for c in range(8):
                kb.mm(pk, pk[:, :], hT, hT[:, c, :], Wr, Wr[:, c, 512:1024], c == 0, c == 7)
            kr = krs.next()
            rope_full(kr[:], kr, pk, blk)
            KD = KDs.next()
            kb.V(lambda e: e.tensor_tensor(out=KD[:].rearrange("p (h d) -> p h d", h=4), in0=kr[:],
                                           in1=kdec[:].unsqueeze(2).to_broadcast([128, 4, 128]), op=ALU.mult),
                 [KD], [kr, kdec])
            VB = VBs.next()
            for n in range(2):
                pvv = PSALL.next()
                for c in range(8):
                    kb.mm(pvv, pvv[:, :], hT, hT[:, c, :], Wr, Wr[:, c, 1024 + n * 512:1536 + n * 512], c == 0, c == 7)
                kb.A(lambda e: e.copy(out=VB[:, n * 512:(n + 1) * 512], in_=pvv[:]), [VB], [pvv])
            if o < 0:
                pf3.prefetch()
            if o >= 0:
                pq = PSALL.next()
                for c in range(8):
                    kb.mm(pq, pq[:, :], hT, hT[:, c, :], Wr, Wr[:, c, 0:512], c == 0, c == 7)
                qrf = krs.next()
                rope_full(qrf[:], qrf, pq, blk)
                qk = qkb.next()
                kb.V(lambda e: e.tensor_copy(out=qk[:, 0, :].rearrange("p (h d) -> p h d", h=4), in_=qrf[:]), [qk], [qrf])
                kb.V(lambda e: e.tensor_copy(out=qk[:, 1, :].rearrange("p (h d) -> p h d", h=4), in_=kr[:]), [qk], [kr])
                sg = sgs.next()
                for n in range(2):
                    pgg = PSALL.next()
                    for c in range(8):
                        kb.mm(pgg, pgg[:, :], hT, hT[:, c, :], Wr, Wr[:, c, 2048 + n * 512:2560 + n * 512],
                              c == 0, c == 7)
                    kb.A(lambda e: e.activation(out=sg[:, n * 512:(n + 1) * 512], in_=pgg[:], func=AF.Silu), [sg], [pgg])
                pf3.prefetch()
                for j in range(8):
                    kb.tr(psb, psb[:, j * 128:(j + 1) * 128], qk, qk[:, j // 4, (j % 4) * 128:(j % 4 + 1) * 128],
                          identb, identb[:])
                qkt = QKT.next()
                kb.V(lambda e: e.tensor_copy(out=qkt[:], in_=psb[:].rearrange("p (j t) -> p j t", j=8)), [qkt], [psb])
                qtd = QTd.next()
                kb.V(lambda e: e.tensor_tensor(out=qtd[:], in0=qkt[:, 0:4, :], in1=qdecrow[:], op=ALU.mult),
                     [qtd], [qkt, qdecrow])
                psc = PSALL.next()
                for h in range(4):
                    kb.mm(psc, psc[:, h * 128:(h + 1) * 128], qkt, qkt[:, 4 + h, :], qkt, qkt[:, h, :], True, True)
                sc = SCs.next()
                kb.V(lambda e: e.tensor_tensor(out=sc[:], in0=psc[:].rearrange("p (h i) -> p h i", h=4), in1=decT[:],
                                               op=ALU.mult), [sc], [psc, decT])
                y = ys.next()
                bs, mvv = bst.next(), mv.next()
                for n in range(2):
                    po = PSALL.next()
                    for hh in range(2):
                        h = 2 * n + hh
                        kb.mm(po, po[:, hh * 256:(hh + 1) * 256], sc, sc[:, h, :], VB, VB[:, h * 256:(h + 1) * 256],
                              True, False)
                        kb.mm(po, po[:, hh * 256:(hh + 1) * 256], qtd, qtd[:, h, :], STb, STb[:, h, :], False, True)
                    for hh in range(2):
                        h = 2 * n + hh
                        kb.V(lambda e: e.bn_stats(out=bs[:, h, :], in_=po[:, hh * 256:(hh + 1) * 256]), [bs], [po])
                        kb.V(lambda e: e.bn_aggr(out=mvv[:, h, 0:2], in_=bs[:, h, :]), [mvv], [bs])
                        kb.V(lambda e: e.tensor_scalar(out=mvv[:, h, 3:4], in0=mvv[:, h, 1:2], scalar1=EPS, scalar2=None,
                                                       op0=ALU.add), [mvv], [mvv])
                        kb.A(lambda e: e.activation(out=mvv[:, h, 3:4], in_=mvv[:, h, 3:4], func=AF.Sqrt), [mvv], [mvv])
                        kb.V(lambda e: e.reciprocal(out=mvv[:, h, 2:3], in_=mvv[:, h, 3:4]), [mvv], [mvv])
                        kb.V(lambda e: e.tensor_scalar(out=y[:, h * 256:(h + 1) * 256], in0=po[:, hh * 256:(hh + 1) * 256],
                                                       scalar1=mvv[:, h, 0:1], scalar2=mvv[:, h, 2:3],
                                                       op0=ALU.subtract, op1=ALU.mult), [y], [po, mvv])
                if o == 1:
                    dump("ret_y", y, y[:], [128, 1024])
                kb.V(lambda e: e.tensor_tensor(out=y[:], in0=y[:], in1=gnw[:], op=ALU.mult), [y], [y, gnw])
                ob = orb.next()
                kb.V(lambda e: e.tensor_tensor(out=ob[:], in0=y[:], in1=sg[:], op=ALU.mult), [ob], [y, sg])
                for c in range(8):
                    kb.tr(psb, psb[:, c * 128:(c + 1) * 128], ob, ob[:, c * 128:(c + 1) * 128], identb, identb[:])
                kb.V(lambda e: e.tensor_copy(out=ORT[:, :, o * 128:(o + 1) * 128],
                                             in_=psb[:].rearrange("p (c t) -> p c t", c=8)), [ORT], [psb])
            if blk < NB - 1:
                for n in range(2):
                    pu = PSALL.next()
                    for hh in range(2):
                        h = 2 * n + hh
                        kb.mm(pu, pu[:, hh * 256:(hh + 1) * 256], KD, KD[:, h * 128:(h + 1) * 128],
                              VB, VB[:, h * 256:(h + 1) * 256], True, True)
                    for hh in range(2):
                        h = 2 * n + hh
                        kb.V(lambda e: e.scalar_tensor_tensor(out=ST[:, h, :], in0=ST[:, h, :], scalar=GAMMAS[h] ** 128,
                                                              in1=pu[:, hh * 256:(hh + 1) * 256], op0=ALU.mult,
                                                              op1=ALU.add), [ST], [ST, pu])
                kb.V(lambda e: e.tensor_copy(out=STb[:], in_=ST[:]), [STb], [ST])
        dump("ORT", ORT, ORT[:], [128, 8, NO * 128])
        kb.barrier()
        live.remove(p3)
    ck(3)

    x1_blk = x1d.rearrange("(b p) f -> b p f", p=128)
    with ExitStack() as p4:
        live.append(p4)
        Wgm = kb.sb(p4, [128, 8, 2048], BF, "Wgm")
        Wnsa = kb.sb(p4, [128, 4, 1024], BF, "Wnsa")
        Wret = kb.sb(p4, [128, 8, 1024], BF, "Wret")
        Wmix = kb.sb(p4, [128, 8, 1024], BF, "Wmix")
        with Staging() as stg:
            load_w(stg, Wgm, lambda c0, n: Wgm[:, :, c0:c0 + n], w_in3[:, :, 4376:6424], 8, 2048)
            load_w(stg, Wnsa, lambda c0, n: Wnsa[:, :, c0:c0 + n], D["w_nsa"].rearrange("(c p) n -> p c n", p=128), 4, 1024)
            load_w(stg, Wret, lambda c0, n: Wret[:, :, c0:c0 + n], D["w_ret"].rearrange("(c p) n -> p c n", p=128), 8, 1024)
            load_w(stg, Wmix, lambda c0, n: Wmix[:, :, c0:c0 + n], D["w_mix"].rearrange("(c p) n -> p c n", p=128), 8, 1024)
        GMs = Rot([kb.sb(p4, [128, 16, 128], F32, "GM") for _ in range(2)])
        MTs = Rot([kb.sb(p4, [128, 8, 128], BF, "MT") for _ in range(2)])
        mt1 = Rot([kb.sb(p4, [128, 512], F32, "mt1") for _ in range(2)])
        mt2 = Rot([kb.sb(p4, [128, 512], F32, "mt2") for _ in range(2)])
        x1s = Rot([kb.sb(p4, [128, 1024], F32, "x1") for _ in range(2)])
        pf4 = Prefetch([xl_blk[OB0 + o_] for o_ in range(NO)], nmw, want_xt=True)
        for o in range(NO):
            blk = OB0 + o
            hT, xt, _ = pf4.cur()
            gm = GMs.next()
            for q4 in range(4):
                pgm = PSALL.next()
                for j in range(4):
                    mc = q4 * 4 + j
                    for c in range(8):
                        kb.mm(pgm, pgm[:, j * 128:(j + 1) * 128], Wgm, Wgm[:, c, mc * 128:(mc + 1) * 128],
                              hT, hT[:, c, :], c == 0, c == 7)
                kb.A(lambda e: e.activation(out=gm[:, q4 * 4:q4 * 4 + 4, :].rearrange("p a t -> p (a t)"),
                                            in_=pgm[:], func=AF.Sigmoid), [gm], [pgm])
            pf4.prefetch()
            mt = MTs.next()
            for half in range(2):
                pya, pyb = PSALL.next(), PSALL.next()
                for j in range(4):
                    fc = half * 4 + j
                    for c in range(4):
                        kb.mm(pya, pya[:, j * 128:(j + 1) * 128], Wnsa, Wnsa[:, c, fc * 128:(fc + 1) * 128],
                              OAT, OAT[:, c, o * 128:(o + 1) * 128], c == 0, c == 3)
                    for c in range(8):
                        kb.mm(pyb, pyb[:, j * 128:(j + 1) * 128], Wret, Wret[:, c, fc * 128:(fc + 1) * 128],
                              ORT, ORT[:, c, o * 128:(o + 1) * 128], c == 0, c == 7)
                a1, a2 = mt1.next(), mt2.next()
                kb.V(lambda e: e.tensor_tensor(out=a1[:], in0=pya[:],
                                               in1=gm[:, half * 4:half * 4 + 4, :].rearrange("p a t -> p (a t)"),
                                               op=ALU.mult), [a1], [pya, gm])
                kb.V(lambda e: e.tensor_tensor(out=a2[:], in0=pyb[:],
                                               in1=gm[:, 8 + half * 4:12 + half * 4, :].rearrange("p a t -> p (a t)"),
                                               op=ALU.mult), [a2], [pyb, gm])
                kb.V(lambda e: e.tensor_tensor(out=mt[:, half * 4:half * 4 + 4, :].rearrange("p a t -> p (a t)"),
                                               in0=a1[:], in1=a2[:], op=ALU.add), [mt], [a1, a2])
            x1 = x1s.next()
            for n in range(2):
                pm = PSALL.next()
                for fc in range(8):
                    kb.mm(pm, pm[:, :], mt, mt[:, fc, :], Wmix, Wmix[:, fc, n * 512:(n + 1) * 512], fc == 0, fc == 7)
                kb.V(lambda e: e.tensor_tensor(out=x1[:, n * 512:(n + 1) * 512], in0=pm[:],
                                               in1=xt[:, n * 512:(n + 1) * 512], op=ALU.add), [x1], [pm, xt])
            kb.dma(SP, x1_blk[o], x1[:], ins=[x1])
            if o == 1:
                dump("x1", x1, x1[:], [128, 1024])
        kb.barrier()
        live.remove(p4)
    mixs.close()
    live.remove(mixs)
    ck(4)

    with ExitStack() as p5:
        live.append(p5)
        Wup = kb.sb(p5, [128, 8, 5632], BF, "Wup")
        Wdn = kb.sb(p5, [128, 22, 1024], BF, "Wdn")
        with Staging() as stg:
            load_w(stg, Wup, lambda c0, n: Wup[:, :, c0:c0 + n], D["w_up"].rearrange("(c p) n -> p c n", p=128), 8, 5632)
            load_w(stg, Wdn, lambda c0, n: Wdn[:, :, c0:c0 + n], D["w_dn"].rearrange("(c p) n -> p c n", p=128), 22, 1024)
        convw = cload(p5, "convw")
        convb = cload(p5, "convb")
        nlw = cload(p5, "nlw")
        uh = kb.sb(p5, [128, 44, 2], F32, "uh")
        kb.V(lambda e: e.memset(uh[:], 0.0), [uh], [])
        uhc = [T(uh.t) for _ in range(44)]
        for t_ in uhc:
            t_.w = uh.w
        PW = 256
        hT2s = Rot([kb.sb(p5, [128, 8, PW], BF, "hT2") for _ in range(2)])
        uts = Rot([kb.sb(p5, [128, PW + 2], F32, "ut") for _ in range(4)])
        cvs = Rot([kb.sb(p5, [128, PW], F32, "cv") for _ in range(4)])
        cvg = Rot([kb.sb(p5, [128, PW], F32, "cvg") for _ in range(3)])
        ATs = Rot([kb.sb(p5, [128, 22, PW], BF, "AT") for _ in range(1)])
        x2s = Rot([kb.sb(p5, [128, 1024], F32, "x2") for _ in range(1)])
        ys5 = Rot([kb.sb(p5, [128, 1024], F32, "y5") for _ in range(2)])
        out_blk = out_d.rearrange("(b p) f -> b p f", p=128)
        groups = [[0]] + [[1 + 2 * i, 2 + 2 * i] for i in range(8)]
        for grp in groups:
            N = 128 * len(grp)
            hT2 = hT2s.next()
            for j, o in enumerate(grp):
                norm_T(x1_blk[o], nfw, dst=(hT2, hT2[:, :, j * 128:(j + 1) * 128]))
            at = ATs.next()
            atc = [T(at.t) for _ in range(22)]
            for t_ in atc:
                t_.w, t_.r = at.w, dict(at.r)

            def conv_chunk(ch):
                pu = PSS.next()
                for c in range(8):
                    kb.mm(pu, pu[:, 0:N], Wup, Wup[:, c, ch * 128:(ch + 1) * 128], hT2, hT2[:, c, 0:N], c == 0, c == 7)
                ut = uts.next()
                cv = cvs.next()
                kb.A(lambda e: e.copy(out=ut[:, 2:2 + N], in_=pu[:, 0:N]), [ut], [pu])
                kb.A(lambda e: e.activation(out=cv[:, 0:N], in_=pu[:, 0:N], func=AF.Identity,
                                            scale=convw[:, 2, ch:ch + 1], bias=convb[:, ch:ch + 1]),
                     [cv], [pu, convw, convb])
                kb.G(lambda e: e.tensor_copy(out=ut[:, 0:2], in_=uh[:, ch, :]), [ut], [uhc[ch]])
                kb.G(lambda e: e.tensor_copy(out=uh[:, ch, :], in_=ut[:, N:N + 2]), [uhc[ch]], [ut])
                kb.V(lambda e: e.scalar_tensor_tensor(out=cv[:, 0:N], in0=ut[:, 1:1 + N], scalar=convw[:, 1, ch:ch + 1],
                                                      in1=cv[:, 0:N], op0=ALU.mult, op1=ALU.add), [cv], [ut, convw, cv])
                kb.V(lambda e: e.scalar_tensor_tensor(out=cv[:, 0:N], in0=ut[:, 0:N], scalar=convw[:, 0, ch:ch + 1],
                                                      in1=cv[:, 0:N], op0=ALU.mult, op1=ALU.add), [cv], [ut, convw, cv])
                return cv

            for c22 in range(22):
                cg_ = conv_chunk(c22)
                sgt = cvg.next()
                kb.A(lambda e: e.activation(out=sgt[:, 0:N], in_=cg_[:, 0:N], func=AF.Silu), [sgt], [cg_])
                cvv = conv_chunk(22 + c22)
                kb.V(lambda e: e.tensor_tensor(out=at[:, c22, 0:N], in0=sgt[:, 0:N], in1=cvv[:, 0:N], op=ALU.mult),
                     [atc[c22]], [sgt, cvv])
            def fold_at():
                for t_ in atc:
                    deps = list(t_.r.items()) + ([t_.w] if t_.w is not None else [])
                    for sem_, v_ in deps:
                        if at.r.get(sem_, 0) < v_:
                            at.r[sem_] = v_
            if grp == [0]:
                fold_at()
                continue
            for j, o in enumerate(grp):
                xt = xts.next()
                kb.dma(SP, xt[:], x1_blk[o], outs=[xt])
                x2 = x2s.next()
                for n in range(2):
                    pd = PSA.next()
                    for c in range(22):
                        kb.mm(pd, pd[:, :], atc[c], at[:, c, j * 128:(j + 1) * 128], Wdn, Wdn[:, c, n * 512:(n + 1) * 512],
                              c == 0, c == 21)
                    kb.V(lambda e: e.tensor_tensor(out=x2[:, n * 512:(n + 1) * 512], in0=pd[:],
                                                   in1=xt[:, n * 512:(n + 1) * 512], op=ALU.add), [x2], [pd, xt])
                st = stat.next()
                kb.V(lambda e: e.scalar_tensor_tensor(out=sq_junk[:], in0=x2[:], scalar=1.0, in1=x2[:], op0=ALU.mult,
                                                      op1=ALU.mult, accum_out=st[:, 0:1]), [sq_junk, st], [x2])
                kb.V(lambda e: e.tensor_scalar(out=st[:, 1:2], in0=st[:, 0:1], scalar1=1.0 / 1024, scalar2=EPS,
                                               op0=ALU.mult, op1=ALU.add), [st], [st])
                kb.A(lambda e: e.activation(out=st[:, 3:4], in_=st[:, 1:2], func=AF.Sqrt), [st], [st])
                kb.V(lambda e: e.reciprocal(out=st[:, 2:3], in_=st[:, 3:4]), [st], [st])
                y5 = ys5.next()
                kb.V(lambda e: e.scalar_tensor_tensor(out=y5[:], in0=x2[:], scalar=st[:, 2:3], in1=nlw[:], op0=ALU.mult,
                                                      op1=ALU.mult), [y5], [x2, st, nlw])
                kb.dma(SP, out_blk[o - 1], y5[:], ins=[y5])
            fold_at()
        kb.barrier()
        live.remove(p5)
    return dbg_out


def finish(kb, top, stacks, dbg_out):
    kb.barrier()
    for s in stacks:
        s.close()
    top.close()
    return kb.nc, dbg_out


def _consts(s):
    f = np.float32
    c = {}
    c["ident"] = np.eye(128, dtype=f)
    tloc = np.arange(4096)
    pos = np.where(tloc < 2048, tloc, 2048 * s + tloc - 2048).astype(np.float64)
    if s == 0:
        pos[:2048] = 0.0

    def rope_tab(theta, half):
        inv = np.power(np.float32(theta), -np.arange(half, dtype=np.float32) / half).astype(np.float32)
        ang = pos.astype(np.float32)[:, None] * inv[None, :]
        cs, sn = np.cos(ang).astype(f), np.sin(ang).astype(f)
        to = lambda a: np.ascontiguousarray(a.reshape(32, 128, half).transpose(1, 0, 2))
        return to(cs), to(sn)
    c["ropeN_c"], c["ropeN_s"] = rope_tab(500000.0, 8)
    c["ropeR_c"], c["ropeR_s"] = rope_tab(10000.0, 64)
    c["expand"] = (np.arange(4096)[None, :] // 64 == np.arange(64)[:, None]).astype(f)
    cst = np.arange(256) * 16
    sst = np.arange(64) * 64
    ov = np.clip(np.minimum(cst[:, None] + 32, sst[None, :] + 64) - np.maximum(cst[:, None], sst[None, :]), 0, None) / 32.0
    ov[255] = 0.0
    c["ov"] = np.ascontiguousarray(ov.reshape(2, 128, 64).transpose(1, 0, 2)).astype(f)
    lo_c = 0 if s == 1 else 128
    lo_j = 0 if s == 1 else 32
    tq = (OB0 * 128 + np.arange(NO * 128))
    cm = (16 * np.arange(256)[:, None] + 31 <= tq[None, :]) & (np.arange(256)[:, None] >= lo_c) & (np.arange(256)[:, None] < 255)
    c["cmpmask"] = np.ascontiguousarray(cm.reshape(2, 128, NO, 128).transpose(1, 0, 2, 3)).astype(f)
    cur = tq // 64
    jb = np.arange(64)[None, :]
    valid = (jb <= cur[:, None]) & (jb >= lo_j)
    forced = ((jb == lo_j) | (jb == cur[:, None]) | (jb == cur[:, None] - 1)) & valid
    add = np.where(forced, 1e4 + jb, np.where(valid, 0.0, -1e4 - jb))
    vm = (valid & ~forced)
    c["selvalid"] = np.ascontiguousarray(vm.reshape(NO, 128, 64).transpose(1, 0, 2)).astype(f)
    c["seladd"] = np.ascontiguousarray(add.reshape(NO, 128, 64).transpose(1, 0, 2)).astype(f)
    c["selkeep"] = np.ascontiguousarray(valid.reshape(NO, 128, 64).transpose(1, 0, 2)).astype(f)
    k_, q_ = np.arange(128)[:, None], np.arange(128)[None, :]
    c["lowmask"] = (k_ <= q_).astype(f)
    c["upmask"] = (k_ > q_).astype(f)
    c["pbias"] = np.full((128, 1), 0.0 if s == 1 else -30000.0, dtype=f)
    g = np.array(GAMMAS, dtype=np.float64)
    i_ = np.arange(128)
    dk = 128 ** -0.5
    dec = np.where(i_[None, :] >= i_[:, None], g[:, None, None] ** np.maximum(i_[None, :] - i_[:, None], 0)[None], 0.0)
    c["decT"] = np.ascontiguousarray((dec * dk).transpose(1, 0, 2)).astype(f)
    c["qdecrow"] = np.ascontiguousarray(np.broadcast_to((g[:, None] ** (i_[None, :] + 1.0))[None], (128, 4, 128))).astype(f)
    c["kdec"] = np.ascontiguousarray((g[None, :] ** (127.0 - i_[:, None])) * dk).astype(f)
    return c


def kernel(x, norm_mix_w, w_in, cmp_pe_k, cmp_w1_k, cmp_w2_k, cmp_pe_v, cmp_w1_v, cmp_w2_v,
           w_nsa_branch, ret_gn_w, w_ret_branch, w_mix_out, norm_ffn_w, w_ffn_up, ffn_conv_w,
           ffn_conv_b, w_ffn_down, norm_final_w, _build_kwargs=None, _return_all=False):
    f = np.float32
    A = lambda a: np.ascontiguousarray(np.asarray(a, dtype=f))
    x = A(x)
    shared = {
        "w_in": A(w_in)[0], "cmp_w1_k": A(cmp_w1_k)[0], "cmp_w2_k": A(cmp_w2_k)[0], "cmp_w1_v": A(cmp_w1_v)[0],
        "cmp_w2_v": A(cmp_w2_v)[0], "w_nsa": A(w_nsa_branch)[0], "w_ret": A(w_ret_branch)[0], "w_mix": A(w_mix_out)[0],
        "w_up": A(w_ffn_up)[0], "w_dn": A(w_ffn_down)[0],
        "nmw": np.ascontiguousarray(A(norm_mix_w)[0].reshape(8, 128).T),
        "nfw": np.ascontiguousarray(A(norm_ffn_w)[0].reshape(8, 128).T),
        "nlw": np.ascontiguousarray(np.broadcast_to(A(norm_final_w)[None, :], (128, 1024))),
        "gnw": np.ascontiguousarray(np.broadcast_to(A(ret_gn_w)[0].reshape(1, 1024), (128, 1024))),
        "convw": np.ascontiguousarray(A(ffn_conv_w)[0].reshape(3, 44, 128).transpose(2, 0, 1)),
        "convb": np.ascontiguousarray(A(ffn_conv_b)[0].reshape(44, 128).T),
        "peTk": np.ascontiguousarray(np.concatenate([A(cmp_pe_k)[0].T] * 2, axis=0)),
        "peTv": np.ascontiguousarray(np.concatenate([A(cmp_pe_v)[0].T] * 2, axis=0)),
    }
    consts = [_consts(0), _consts(1)]
    nc, dbg = build(**(_build_kwargs or {}))
    in_maps = []
    for core in range(8):
        b, s = core // 2, core % 2
        xl = np.zeros((4096, 1024), dtype=f)
        if s == 1:
            xl[:] = x[b]
        else:
            xl[2048:] = x[b, :2048]
        m = dict(shared)
        m["xl"] = xl
        cs = consts[s]
        for k_, v in cs.items():
            if not k_.startswith("_"):
                m[k_] = v
        in_maps.append(m)
    res = run_bass_kernel_spmd(nc, in_maps, core_ids=list(range(8)))
    if _return_all:
        return res
    out = np.zeros((4, 4096, 1024), dtype=f)
    for core in range(8):
        b, s = core // 2, core % 2
        out[b, 2048 * s:2048 * (s + 1)] = res.results[core]["out"]
    return out
```

```python
import numpy as np
from contextlib import ExitStack
import concourse.bass as bass
import concourse.mybir as mybir
from concourse.bass_utils import run_bass_kernel_spmd

F32 = mybir.dt.float32
BF = mybir.dt.bfloat16
AF = mybir.ActivationFunctionType
ALU = mybir.AluOpType
AX = mybir.AxisListType

NB = 32
OB0 = 15
NO = 17
EPS = 1e-6
SEM_MAX = 12000
import os
_VAR = os.environ.get('KVAR', '')


class Eng:
    def __init__(self, nc, e, name, is_pe=False):
        self.nc, self.e, self.name, self.is_pe = nc, e, name, is_pe
        self.k = 0
        self.own = set()
        self.waited = {}
        self.new_sem()

    def new_sem(self):
        self.sem = self.nc.alloc_semaphore(f"{self.name}_c{self.k}")
        self.k += 1
        self.count = 0
        self.own.add(self.sem)


class T:
    def __init__(self, t, psum=False):
        self.t = t
        self.w = None
        self.r = {}
        self.psum = psum

    def __getitem__(self, k):
        return self.t[k]


class K:
    def __init__(self):
        nc = bass.Bass("TRN2", target_bir_lowering=False)
        self.nc = nc
        self.PE = Eng(nc, nc.tensor, "pe", True)
        self.ACT = Eng(nc, nc.scalar, "act")
        self.DVE = Eng(nc, nc.vector, "dve")
        self.POOL = Eng(nc, nc.gpsimd, "pool")
        self.SP = Eng(nc, nc.sync, "sp")
        self.dsems = [nc.alloc_semaphore(f"dma{i}") for i in range(20)]
        self.dvals = [0] * 20
        self.di = 0
        self.uid = 0

    def sb(self, stack, shape, dt, name=None):
        self.uid += 1
        t = stack.enter_context(self.nc.sbuf_tensor(f"{name or 't'}_{self.uid}", list(shape), dt))
        return T(t)

    def ps(self, stack, shape, dt, name=None):
        self.uid += 1
        t = stack.enter_context(self.nc.psum_tensor(f"{name or 'p'}_{self.uid}", list(shape), dt))
        return T(t, psum=True)

    def _waits(self, E, outs, ins):
        need = {}

        def add(sem, v):
            if need.get(sem, 0) < v:
                need[sem] = v
        for b in ins:
            if b.w is not None:
                add(*b.w)
            if b.psum:
                for sem, v in b.r.items():
                    if sem not in E.own:
                        add(sem, v)
        for b in outs:
            if b.w is not None:
                add(*b.w)
            for sem, v in b.r.items():
                add(sem, v)
        for sem, v in need.items():
            if E.is_pe and sem in E.own:
                continue
            if E.waited.get(sem, 0) >= v:
                continue
            E.e.wait_ge(sem, v)
            E.waited[sem] = v

    def _mark(self, d, outs, ins):
        sem, v = d
        for b in ins:
            if b.r.get(sem, 0) < v:
                b.r[sem] = v
        for b in outs:
            b.w = d
            b.r = {}

    def op(self, E, fn, outs=(), ins=()):
        self._waits(E, outs, ins)
        if E.count >= SEM_MAX:
            E.new_sem()
        i = fn(E.e)
        E.count += 1
        i.then_inc(E.sem, 1)
        self._mark((E.sem, E.count), outs, ins)

    def dma(self, Q, out_ap, in_ap, outs=(), ins=(), **kw):
        self._waits(Q, outs, ins)
        i = self.di
        self.di = (self.di + 1) % len(self.dsems)
        if self.dvals[i] >= SEM_MAX:
            if Q.waited.get(self.dsems[i], 0) < self.dvals[i]:
                Q.e.wait_ge(self.dsems[i], self.dvals[i])
            self.uid += 1
            self.dsems[i] = self.nc.alloc_semaphore(f"dmax{self.uid}")
            self.dvals[i] = 0
        sem, prev = self.dsems[i], self.dvals[i]
        if prev > 0 and Q.waited.get(sem, 0) < prev:
            Q.e.wait_ge(sem, prev)
            Q.waited[sem] = prev
        Q.e.dma_start(out=out_ap, in_=in_ap, **kw).then_inc(sem, 16)
        self.dvals[i] = prev + 16
        self._mark((sem, prev + 16), outs, ins)
        return (sem, prev + 16)

    def barrier(self):
        engs = [self.PE, self.ACT, self.DVE, self.POOL, self.SP]
        for E in engs:
            for P in engs:
                if P is E or P.count == 0:
                    continue
                if E.waited.get(P.sem, 0) < P.count:
                    E.e.wait_ge(P.sem, P.count)
                    E.waited[P.sem] = P.count
            for sem, v in zip(self.dsems, self.dvals):
                if v > 0 and E.waited.get(sem, 0) < v:
                    E.e.wait_ge(sem, v)
                    E.waited[sem] = v

    def mm(self, out_t, out_ap, lhsT_t, lhsT_ap, rhs_t, rhs_ap, start, stop, skip=False, extra=()):
        self.op(self.PE, lambda e: e.matmul(out_ap, lhsT=lhsT_ap, rhs=rhs_ap, start=start, stop=stop,
                                            skip_group_check=skip),
                outs=[out_t], ins=[lhsT_t, rhs_t] + list(extra))

    def tr(self, out_t, out_ap, in_t, in_ap, ident_t, ident_ap):
        self.op(self.PE, lambda e: e.transpose(out_ap, in_ap, ident_ap), outs=[out_t], ins=[in_t, ident_t])

    def V(self, fn, outs, ins):
        self.op(self.DVE, fn, outs, ins)

    def A(self, fn, outs, ins):
        self.op(self.ACT, fn, outs, ins)

    def G(self, fn, outs, ins):
        self.op(self.POOL, fn, outs, ins)


class Rot:
    def __init__(self, items):
        self.items = items
        self.i = 0

    def next(self):
        t = self.items[self.i]
        self.i = (self.i + 1) % len(self.items)
        return t


CONST_SPECS = {
    "ident": [128, 128], "nmw": [128, 8], "nfw": [128, 8], "nlw": [128, 1024],
    "ropeN_c": [128, 32, 8], "ropeN_s": [128, 32, 8], "ropeR_c": [128, 32, 64], "ropeR_s": [128, 32, 64],
    "expand": [64, 4096], "ov": [128, 2, 64], "cmpmask": [128, 2, 17, 128],
    "selvalid": [128, 17, 64], "seladd": [128, 17, 64], "selkeep": [128, 17, 64], "lowmask": [128, 128], "upmask": [128, 128],
    "pbias": [128, 1], "decT": [128, 4, 128], "qdecrow": [128, 4, 128], "kdec": [128, 4],
    "gnw": [128, 1024], "convw": [128, 3, 44], "convb": [128, 44], "peTk": [128, 32], "peTv": [128, 32],
}
WEIGHT_SPECS = {
    "w_in": [1024, 6424], "cmp_w1_k": [2048, 256], "cmp_w2_k": [256, 64], "cmp_w1_v": [2048, 256],
    "cmp_w2_v": [256, 64], "w_nsa": [512, 1024], "w_ret": [1024, 1024], "w_mix": [1024, 1024],
    "w_up": [1024, 5632], "w_dn": [2816, 1024],
}
GAMMAS = [1.0 - 2.0 ** (-5.0 - h) for h in range(4)]


class _Stop(Exception):
    pass


def build(stop_after=None, debug=()):
    kb = K()
    live = []
    top = ExitStack()
    try:
        dbg_out = _build_body(kb, top, live, stop_after, debug)
    except _Stop as e:
        dbg_out = e.args[0]
    return finish(kb, top, list(reversed(live)), dbg_out)


def _build_body(kb, top, live, stop_after, debug):
    nc = kb.nc
    PE, ACT, DVE, POOL, SP = kb.PE, kb.ACT, kb.DVE, kb.POOL, kb.SP
    D = {}
    D["xl"] = nc.dram_tensor("xl", [4096, 1024], F32, kind="ExternalInput").ap()
    for n, s in list(CONST_SPECS.items()) + list(WEIGHT_SPECS.items()):
        D[n] = nc.dram_tensor(n, s, F32, kind="ExternalInput").ap()
    out_d = nc.dram_tensor("out", [2048, 1024], F32, kind="ExternalOutput").ap()
    x1d = nc.dram_tensor("x1d", [NO * 128, 1024], F32, kind="Internal").ap()
    dbg_out = {}

    def ck(tag):
        if stop_after == tag:
            kb.barrier()
            for st_ in reversed(live):
                st_.close()
            del live[:]
            raise _Stop(dbg_out)

    def cload(stack, name, dt=F32, eng=None):
        shape = CONST_SPECS[name]
        t = kb.sb(stack, shape, F32, name)
        kb.dma(SP, t[:], D[name], outs=[t])
        if dt == F32:
            return t
        tb = kb.sb(stack, shape, dt, name + "b")
        kb.V(lambda e: e.tensor_copy(out=tb[:], in_=t[:]), [tb], [t])
        return tb

    identf = cload(top, "ident")
    identb = kb.sb(top, [128, 128], BF, "identb")
    kb.V(lambda e: e.tensor_copy(out=identb[:], in_=identf[:]), [identb], [identf])
    nmw = cload(top, "nmw")
    nfw = cload(top, "nfw")
    lowmask = cload(top, "lowmask", BF)
    upmask = cload(top, "upmask", BF)
    pbias = cload(top, "pbias")
    zbias = kb.sb(top, [128, 1], F32, "zbias")
    kb.V(lambda e: e.memset(zbias[:], 0.0), [zbias], [])

    psf = [kb.ps(top, [128, 512], F32, f"psf{i}") for i in range(6)]
    psbs = Rot([kb.ps(top, [128, 1024], BF, f"psb{i}") for i in range(2)])
    psb = psbs.next()
    PSS = Rot(psf[0:4])
    PSA = Rot(psf[4:6])
    PSALL = Rot(psf)

    xts = Rot([kb.sb(top, [128, 1024], F32, "xt") for _ in range(2)])
    xns = Rot([kb.sb(top, [128, 1024], BF, "xn") for _ in range(2)])
    hTs = Rot([kb.sb(top, [128, 8, 128], BF, "hT") for _ in range(2)])
    sq_junk = kb.sb(top, [128, 1024], BF, "sqj")
    stat = Rot([kb.sb(top, [128, 4], F32, "stat") for _ in range(4)])

    def norm_T(src_ap, wcol, want_xt=False, dst=None):
        xt = xts.next()
        kb.dma(SP, xt[:], src_ap, outs=[xt])
        st = stat.next()
        kb.V(lambda e: e.scalar_tensor_tensor(out=sq_junk[:], in0=xt[:], scalar=1.0, in1=xt[:], op0=ALU.mult,
                                              op1=ALU.mult, accum_out=st[:, 0:1]), [sq_junk, st], [xt])
        kb.V(lambda e: e.tensor_scalar(out=st[:, 1:2], in0=st[:, 0:1], scalar1=1.0 / 1024, scalar2=EPS,
                                       op0=ALU.mult, op1=ALU.add), [st], [st])
        kb.A(lambda e: e.activation(out=st[:, 3:4], in_=st[:, 1:2], func=AF.Sqrt), [st], [st])
        kb.V(lambda e: e.reciprocal(out=st[:, 2:3], in_=st[:, 3:4]), [st], [st])
        xn = xns.next()
        pbn = psbs.next()
        kb.A(lambda e: e.activation(out=xn[:], in_=xt[:], func=AF.Copy, scale=st[:, 2:3]), [xn], [xt, st])
        for c in range(8):
            kb.tr(pbn, pbn[:, c * 128:(c + 1) * 128], xn, xn[:, c * 128:(c + 1) * 128], identb, identb[:])
        if dst is None:
            hT = hTs.next()
            hT_ap = hT[:]
        else:
            hT, hT_ap = dst
        kb.V(lambda e: e.tensor_tensor(out=hT_ap, in0=pbn[:].rearrange("p (c t) -> p c t", c=8),
                                       in1=wcol[:].unsqueeze(2).to_broadcast([128, 8, 128]), op=ALU.mult),
             [hT], [pbn, wcol])
        if want_xt:
            return hT, xt, st
        return hT

    class Staging:
        def __enter__(self):
            self.ws = ExitStack()
            live.append(self.ws)
            self.rot = Rot([kb.sb(self.ws, [128, 2048], F32, "stg") for _ in range(4)])
            return self.rot

        def __exit__(self, *a):
            kb.barrier()
            live.remove(self.ws)
            self.ws.close()
            return False

    class Prefetch:
        def __init__(self, srcs, wcol, want_xt=False):
            self.srcs, self.wcol, self.want_xt = srcs, wcol, want_xt
            self.nxt = norm_T(srcs[0], wcol, want_xt) if srcs else None
            self.i = 0

        def cur(self):
            return self.nxt

        def prefetch(self):
            self.i += 1
            if self.i < len(self.srcs):
                self.nxt = norm_T(self.srcs[self.i], self.wcol, self.want_xt)

    cast_rr = [0]

    def load_w(stg, dst, dst_ap_fn, src3d, kc, ncols):
        step = 1 << ((2048 // kc).bit_length() - 1)
        for c0 in range(0, ncols, step):
            n = min(step, ncols - c0)
            s = stg.next()
            sv = s[:, 0:kc * n].rearrange("p (c n) -> p c n", c=kc)
            kb.dma(SP, sv, src3d[:, :, c0:c0 + n], outs=[s])
            cast_rr[0] = (cast_rr[0] + 1) % 3
            if cast_rr[0] == 0:
                kb.G(lambda e: e.tensor_copy(out=dst_ap_fn(c0, n), in_=sv), [dst], [s])
            elif cast_rr[0] == 1:
                kb.V(lambda e: e.tensor_copy(out=dst_ap_fn(c0, n), in_=sv), [dst], [s])
            else:
                kb.A(lambda e: e.copy(out=dst_ap_fn(c0, n), in_=sv), [dst], [s])

    def dump(name, t, ap, shape):
        if name in debug:
            d = nc.dram_tensor("dbg_" + name, list(shape), ap.dtype, kind="ExternalOutput").ap()
            kb.dma(SP, d, ap, ins=[t])
            dbg_out[name] = d

    w_in3 = D["w_in"].rearrange("(c p) n -> p c n", p=128)
    xl_blk = D["xl"].rearrange("(b p) f -> b p f", p=128)

    mixs = ExitStack()
    live.append(mixs)
    OAT = kb.sb(mixs, [128, 4, NO * 128], BF, "OAT")
    nsa = ExitStack()
    live.append(nsa)
    KE = [kb.sb(nsa, [128, 4096], BF, f"KE{g}") for g in range(2)]
    KW = kb.sb(nsa, [128, 4096], BF, "KW")
    VS = kb.sb(nsa, [128, 32, 2, 66], BF, "VS")
    VW = kb.sb(nsa, [128, 32, 2, 66], BF, "VW")
    KCT = kb.sb(nsa, [128, 256], BF, "KCT")
    VCA = kb.sb(nsa, [128, 2, 2, 128], BF, "VCA")

    kb.V(lambda e: e.memset(VS[:], 1.0), [VS], [])
    kb.V(lambda e: e.memset(VW[:], 1.0), [VW], [])
    kb.V(lambda e: e.memset(KCT[:], 0.0), [KCT], [])
    with Staging() as stg:
        for g, rows in ((0, slice(64, 128)), (1, slice(0, 64))):
            for c0 in range(0, 4096, 2048):
                s = stg.next()
                kb.dma(SP, s[rows, 0:2048], D["expand"][:, c0:c0 + 2048], outs=[s])
                kb.G(lambda e: e.tensor_copy(out=KE[g][rows, c0:c0 + 2048], in_=s[rows, 0:2048]), [KE[g]], [s])

    ck(0.3)
    with ExitStack() as p1:
        live.append(p1)
        Wkv = kb.sb(p1, [128, 8, 768], BF, "Wkv")
        W1 = {nm: kb.sb(p1, [128, 32, 256], BF, "W1" + nm) for nm in ("k", "v")}
        W2k = kb.sb(p1, [128, 2, 128], BF, "W2k")
        W2v = kb.sb(p1, [128, 2, 64], BF, "W2v")
        peT = {"k": cload(p1, "peTk", BF), "v": cload(p1, "peTv", BF)}
        ropeN_c = cload(p1, "ropeN_c")
        ropeN_s = cload(p1, "ropeN_s")
        ovb = cload(p1, "ov", BF)
        for g in range(2):
            for ct in range(2):
                kb.V(lambda e: e.tensor_copy(out=VCA[:, g, ct, 64:128], in_=ovb[:, ct, :]), [VCA], [ovb])
        with Staging() as stg:
            load_w(stg, Wkv, lambda c0, n: Wkv[:, :, c0:c0 + n], w_in3[:, :, 512:1280], 8, 768)
            for nm in ("k", "v"):
                src = D["cmp_w1_" + nm].rearrange("(l d) m -> d l m", d=64)
                for half in range(2):
                    rows = slice(64 * half, 64 * half + 64)
                    for l0 in range(0, 32, 8):
                        s = stg.next()
                        sv = s[rows, :].rearrange("p (l m) -> p l m", l=8)
                        kb.dma(SP, sv, src[:, l0:l0 + 8, :], outs=[s])
                        kb.G(lambda e: e.tensor_copy(out=W1[nm][rows, l0:l0 + 8, :], in_=sv), [W1[nm]], [s])
            s = stg.next()
            kb.dma(SP, s[:, 0:128].rearrange("p (c d) -> p c d", c=2),
                   D["cmp_w2_k"].rearrange("(c p) d -> p c d", p=128), outs=[s])
            kb.dma(SP, s[:, 128:256].rearrange("p (c d) -> p c d", c=2),
                   D["cmp_w2_v"].rearrange("(c p) d -> p c d", p=128), outs=[s])
            sk = s[:, 0:128].rearrange("p (c d) -> p c d", c=2)
            kb.G(lambda e: e.tensor_copy(out=W2k[:, :, 0:64], in_=sk), [W2k], [s])
            kb.G(lambda e: e.tensor_copy(out=W2k[:, :, 64:128], in_=sk), [W2k], [s])
            kb.G(lambda e: e.tensor_copy(out=W2v[:], in_=s[:, 128:256].rearrange("p (c d) -> p c d", c=2)), [W2v], [s])
        KC = kb.sb(p1, [128, 4096], BF, "KC")
        VC = kb.sb(p1, [128, 4096], BF, "VC")
        ktin = Rot([kb.sb(p1, [128, 4, 128], BF, "ktin") for _ in range(2)])
        rtmp = Rot([kb.sb(p1, [128, 2, 8], F32, "rtmp") for _ in range(16)])

        ck(0.5)
        if stop_after == 0.55:
            hT = norm_T(xl_blk[17], nmw)
            dump("hT", hT, hT[:], [128, 8, 128])
            ck(0.55)
        for blk in range(NB if not (isinstance(stop_after, float) and 0.6 <= stop_after < 0.7) else 1):
            if blk == 0:
                pf1 = Prefetch([xl_blk[b_] for b_ in range(NB)], nmw)
            hT = pf1.cur()
            pa, pb = PSS.next(), PSS.next()
            for c in range(8):
                kb.mm(pa, pa[:, 0:512], hT, hT[:, c, :], Wkv, Wkv[:, c, 0:512], c == 0, c == 7)
            for c in range(8):
                kb.mm(pb, pb[:, 0:256], hT, hT[:, c, :], Wkv, Wkv[:, c, 512:768], c == 0, c == 7)
            pf1.prefetch()
            ck(0.61)
            kt = ktin.next()
            kb.A(lambda e: e.copy(out=kt[:, 0:2, :], in_=pa[:, 0:256].rearrange("p (a d) -> p a d", a=2)), [kt], [pa])
            kb.A(lambda e: e.copy(out=VS[:, blk, :, 0:64], in_=pa[:, 384:512].rearrange("p (g d) -> p g d", g=2)),
                 [VS], [pa])
            kb.A(lambda e: e.copy(out=VW[:, blk, :, 0:64], in_=pb[:, 128:256].rearrange("p (g d) -> p g d", g=2)),
                 [VW], [pb])
            ck(0.62)
            for slot, (pt, lo) in ((2, (pa, 256)), (3, (pb, 0))):
                src3 = pt[:, lo:lo + 128].rearrange("p (g d) -> p g d", g=2)
                dst3 = kt[:, slot, :].rearrange("p (g d) -> p g d", g=2)
                cb = ropeN_c[:, blk, :].unsqueeze(1).to_broadcast([128, 2, 8])
                sb_ = ropeN_s[:, blk, :].unsqueeze(1).to_broadcast([128, 2, 8])
                x1, x2 = src3[:, :, 0:8], src3[:, :, 8:16]
                t1, t2, t3, t4 = rtmp.next(), rtmp.next(), rtmp.next(), rtmp.next()
                kb.V(lambda e: e.tensor_tensor(out=t1[:], in0=x1, in1=cb, op=ALU.mult), [t1], [pt, ropeN_c])
                kb.V(lambda e: e.tensor_tensor(out=t2[:], in0=x2, in1=sb_, op=ALU.mult), [t2], [pt, ropeN_s])
                kb.V(lambda e: e.tensor_tensor(out=t3[:], in0=x1, in1=sb_, op=ALU.mult), [t3], [pt, ropeN_s])
                kb.V(lambda e: e.tensor_tensor(out=t4[:], in0=x2, in1=cb, op=ALU.mult), [t4], [pt, ropeN_c])
                kb.V(lambda e: e.tensor_tensor(out=dst3[:, :, 0:8], in0=t1[:], in1=t2[:], op=ALU.subtract), [kt], [t1, t2])
                kb.V(lambda e: e.tensor_tensor(out=dst3[:, :, 8:16], in0=t3[:], in1=t4[:], op=ALU.add), [kt], [t3, t4])
                if "noactcopy" in _VAR:
                    kb.V(lambda e: e.tensor_copy(out=dst3[:, :, 16:64], in_=src3[:, :, 16:64]), [kt], [pt])
                else:
                    kb.A(lambda e: e.copy(out=dst3[:, :, 16:64], in_=src3[:, :, 16:64]), [kt], [pt])
            ck(0.63)
            psb = psbs.next()
            for slot in range(4):
                kb.tr(psb, psb[:, slot * 128:(slot + 1) * 128], kt, kt[:, slot, :], identb, identb[:])
            ck(0.64)
            cs = slice(blk * 128, (blk + 1) * 128)
            kb.V(lambda e: e.tensor_copy(out=KC[:, cs], in_=psb[:, 0:128]), [KC], [psb])
            kb.V(lambda e: e.tensor_copy(out=VC[:, cs], in_=psb[:, 128:256]), [VC], [psb])
            kb.V(lambda e: e.tensor_copy(out=KE[0][0:64, cs], in_=psb[0:64, 256:384]), [KE[0]], [psb])
            kb.V(lambda e: e.tensor_copy(out=KE[1][64:128, cs], in_=psb[64:128, 256:384]), [KE[1]], [psb])
            kb.V(lambda e: e.tensor_copy(out=KW[:, cs], in_=psb[:, 384:512]), [KW], [psb])

        dump("KE0", KE[0], KE[0][:], [128, 4096])
        dump("KE1", KE[1], KE[1][:], [128, 4096])
        dump("KW", KW, KW[:], [128, 4096])
        dump("KC", KC, KC[:], [128, 4096])
        dump("VS", VS, VS[:], [128, 32, 2, 66])

        ck(0.6)
        ck(0.7)
        GT = kb.sb(p1, [128, 2, 256], BF, "GT")
        hb = kb.sb(p1, [128, 1], F32, "hb")
        gx = [kb.sb(p1, [128, 255], F32, f"gx{i}") for i in range(4)]
        for nm, src in (("k", KC), ("v", VC)):
            for g in range(2):
                rows = slice(64 * g, 64 * g + 64)
                kb.V(lambda e: e.memset(GT[:], 0.0), [GT], [])
                for mc in range(2):
                    ph, pbi = PSS.next(), PSS.next()
                    for l in range(32):
                        kb.mm(ph, ph[:, 0:255], W1[nm], W1[nm][rows, l, mc * 128:(mc + 1) * 128],
                              src, src[rows, l:l + 16 * 254 + 1:16], l == 0, l == 31)
                    for l in range(32):
                        kb.mm(pbi, pbi[:, 0:1], W1[nm], W1[nm][rows, l, mc * 128:(mc + 1) * 128],
                              peT[nm], peT[nm][rows, l:l + 1], l == 0, l == 31)
                    kb.V(lambda e: e.tensor_copy(out=hb[:], in_=pbi[:, 0:1]), [hb], [pbi])
                    x0, x2_, x3_, sg = gx
                    kb.V(lambda e: e.tensor_scalar(out=x0[:], in0=ph[:, 0:255], scalar1=hb[:, 0:1], scalar2=None,
                                                   op0=ALU.add), [x0], [ph, hb])
                    kb.V(lambda e: e.tensor_tensor(out=x2_[:], in0=x0[:], in1=x0[:], op=ALU.mult), [x2_], [x0])
                    kb.V(lambda e: e.tensor_scalar(out=x2_[:], in0=x2_[:], scalar1=0.044715, scalar2=1.0,
                                                   op0=ALU.mult, op1=ALU.add), [x2_], [x2_])
                    kb.V(lambda e: e.tensor_tensor(out=x3_[:], in0=x2_[:], in1=x0[:], op=ALU.mult), [x3_], [x2_, x0])
                    kb.A(lambda e: e.activation(out=sg[:], in_=x3_[:], func=AF.Sigmoid, scale=1.5957691216057308),
                         [sg], [x3_])
                    kb.V(lambda e: e.tensor_tensor(out=GT[:, mc, 0:255], in0=x0[:], in1=sg[:], op=ALU.mult),
                         [GT], [x0, sg])
                if nm == "k":
                    pk = PSS.next()
                    for mc in range(2):
                        kb.mm(pk, pk[:, 0:255], W2k, W2k[:, mc, :], GT, GT[:, mc, 0:255], mc == 0, mc == 1)
                    kb.V(lambda e: e.tensor_copy(out=KCT[rows, 0:255], in_=pk[rows, 0:255]), [KCT], [pk])
                else:
                    for ct in range(2):
                        pv = PSS.next()
                        for mc in range(2):
                            kb.mm(pv, pv[:, 0:64], GT, GT[:, mc, ct * 128:(ct + 1) * 128], W2v, W2v[:, mc, :],
                                  mc == 0, mc == 1)
                        kb.V(lambda e: e.tensor_copy(out=VCA[:, g, ct, 0:64], in_=pv[:, 0:64]), [VCA], [pv])
        dump("KCT", KCT, KCT[:], [128, 256])
        dump("VCA", VCA, VCA[:], [128, 2, 2, 128])
        kb.barrier()
        live.remove(p1)
    ck(1)

    with ExitStack() as p2:
        live.append(p2)
        Wq = kb.sb(p2, [128, 8, 536], BF, "Wq")
        with Staging() as stg:
            load_w(stg, Wq, lambda c0, n: Wq[:, :, c0:c0 + n], w_in3[:, :, 0:512], 8, 512)
            load_w(stg, Wq, lambda c0, n: Wq[:, :, 512 + c0:512 + c0 + n], w_in3[:, :, 1280:1304], 8, 24)
        ropeN_c = cload(p2, "ropeN_c")
        ropeN_s = cload(p2, "ropeN_s")
        cmpmask = cload(p2, "cmpmask", BF)
        selvalid = cload(p2, "selvalid")
        seladd = cload(p2, "seladd")
        selkeep = cload(p2, "selkeep")
        def qb_pair(g):
            t = kb.sb(p2, [128, 512], BF, f"QB{g}")
            return (t, T(t.t))
        QB = [Rot([qb_pair(g) for _ in range(2)]) for g in range(2)]
        qr = Rot([kb.sb(p2, [128, 4, 128], BF, "qr") for _ in range(2)])
        gat = Rot([kb.sb(p2, [128, 24], F32, "gat") for _ in range(2)])
        rt = Rot([kb.sb(p2, [128, 8, 8], F32, "rt") for _ in range(8)])
        PTs = Rot([kb.sb(p2, [128, 512], BF, "PT") for _ in range(4)])
        sm = Rot([kb.sb(p2, [128, 16], F32, "sm") for _ in range(12)])
        impt = Rot([kb.sb(p2, [128, 64], F32, "imp") for _ in range(8)])
        m8 = Rot([kb.sb(p2, [128, 8], F32, "m8") for _ in range(4)])
        ZS = [kb.sb(p2, [128, 128], BF, f"ZS{g}") for g in range(2)]
        for g in range(2):
            kb.V(lambda e: e.memset(ZS[g][:], 0.0), [ZS[g]], [])
        oacc = Rot([kb.sb(p2, [128, 8, 64], F32, "oacc") for _ in range(2)])
        otmp = Rot([kb.sb(p2, [128, 4, 64], F32, "otmp") for _ in range(3)])
        oab = Rot([kb.sb(p2, [128, 512], BF, "oab") for _ in range(2)])

        def coef_from(rs_ap, rs_t, gate_ap, gate_t):
            c = sm.next()
            kb.V(lambda e: e.tensor_scalar(out=c[:, 0:4], in0=rs_ap, scalar1=1e-30, scalar2=None, op0=ALU.max),
                 [c], [rs_t])
            kb.V(lambda e: e.reciprocal(out=c[:, 4:8], in_=c[:, 0:4]), [c], [c])
            kb.V(lambda e: e.tensor_tensor(out=c[:, 8:12], in0=c[:, 4:8], in1=gate_ap, op=ALU.mult), [c], [c, gate_t])
            return c

        def accumulate(oa, g, first, ps_t, o_view, coef):
            cb = coef[:, 8:12].unsqueeze(2).to_broadcast([128, 4, 64])
            if first:
                kb.V(lambda e: e.tensor_tensor(out=oa[:, 4 * g:4 * g + 4, :], in0=o_view, in1=cb, op=ALU.mult),
                     [oa], [ps_t, coef])
            else:
                tm = otmp.next()
                kb.V(lambda e: e.tensor_tensor(out=tm[:], in0=o_view, in1=cb, op=ALU.mult), [tm], [ps_t, coef])
                kb.V(lambda e: e.tensor_tensor(out=oa[:, 4 * g:4 * g + 4, :], in0=oa[:, 4 * g:4 * g + 4, :],
                                               in1=tm[:], op=ALU.add), [oa], [oa, tm])

        for o in range(NO):
            blk = OB0 + o
            if o == 0:
                pf2 = Prefetch([xl_blk[OB0 + o_] for o_ in range(NO)], nmw)
            hT = pf2.cur()
            pq, pg = PSS.next(), PSS.next()
            for c in range(8):
                kb.mm(pq, pq[:, 0:512], hT, hT[:, c, :], Wq, Wq[:, c, 0:512], c == 0, c == 7)
            for c in range(8):
                kb.mm(pg, pg[:, 0:24], hT, hT[:, c, :], Wq, Wq[:, c, 512:536], c == 0, c == 7)
            pf2.prefetch()
            ga = gat.next()
            kb.A(lambda e: e.activation(out=ga[:], in_=pg[:, 0:24], func=AF.Sigmoid), [ga], [pg])
            gav = ga[:].rearrange("p (h b) -> p h b", b=3)
            q_ = qr.next()
            src4 = pq[:, 0:512].rearrange("p (j hp d) -> p j hp d", j=2, hp=4)
            dst4 = q_[:].rearrange("p hp (j d) -> p j hp d", j=2)
            cb = ropeN_c[:, blk, :].unsqueeze(1).unsqueeze(1).to_broadcast([128, 2, 4, 8])
            sb_ = ropeN_s[:, blk, :].unsqueeze(1).unsqueeze(1).to_broadcast([128, 2, 4, 8])
            x1, x2 = src4[:, :, :, 0:8], src4[:, :, :, 8:16]
            t1, t2, t3, t4 = rt.next(), rt.next(), rt.next(), rt.next()
            tv = lambda t: t[:].rearrange("p (j hp) d -> p j hp d", j=2)
            kb.V(lambda e: e.tensor_tensor(out=tv(t1), in0=x1, in1=cb, op=ALU.mult), [t1], [pq, ropeN_c])
            kb.V(lambda e: e.tensor_tensor(out=tv(t2), in0=x2, in1=sb_, op=ALU.mult), [t2], [pq, ropeN_s])
            kb.V(lambda e: e.tensor_tensor(out=tv(t3), in0=x1, in1=sb_, op=ALU.mult), [t3], [pq, ropeN_s])
            kb.V(lambda e: e.tensor_tensor(out=tv(t4), in0=x2, in1=cb, op=ALU.mult), [t4], [pq, ropeN_c])
            kb.V(lambda e: e.tensor_tensor(out=dst4[:, :, :, 0:8], in0=tv(t1), in1=tv(t2), op=ALU.subtract), [q_], [t1, t2])
            kb.V(lambda e: e.tensor_tensor(out=dst4[:, :, :, 8:16], in0=tv(t3), in1=tv(t4), op=ALU.add), [q_], [t3, t4])
            kb.A(lambda e: e.copy(out=dst4[:, :, :, 16:64], in_=src4[:, :, :, 16:64]), [q_], [pq])
            psb = psbs.next()
            for h in range(4):
                kb.tr(psb, psb[:, h * 128:(h + 1) * 128], q_, q_[:, h, :], identb, identb[:])
            qpair = [QB[0].next(), QB[1].next()]
            qq = [qpair[0][0], qpair[1][0]]
            qbias = [qpair[0][1], qpair[1][1]]
            kb.V(lambda e: e.tensor_copy(out=qq[0][0:64, :], in_=psb[0:64, 0:512]), [qq[0]], [psb])
            kb.V(lambda e: e.tensor_copy(out=qq[1][64:128, :], in_=psb[64:128, 0:512]), [qq[1]], [psb])
            oa = oacc.next()
            P4V = lambda t: t[:].rearrange("p (h q) -> p h q", h=4)

            def fin_cmp(g, pso):
                pv4 = pso[:].rearrange("p (h x) -> p h x", h=4)
                rs = sm.next()
                kb.V(lambda e: e.tensor_reduce(out=rs[:, 0:4], in_=pv4[:, :, 64:128], axis=AX.X, op=ALU.add),
                     [rs], [pso])
                cf = coef_from(rs[:, 0:4], rs, gav[:, 4 * g:4 * g + 4, 0], ga)
                accumulate(oa, g, True, pso, pv4[:, :, 0:64], cf)
                im = impt.next()
                kb.V(lambda e: e.tensor_scalar(out=im[:], in0=pv4[:, 0, 64:128], scalar1=cf[:, 4:5], scalar2=None,
                                               op0=ALU.mult), [im], [pso, cf])
                for h in range(1, 4):
                    kb.V(lambda e: e.scalar_tensor_tensor(out=im[:], in0=pv4[:, h, 64:128], scalar=cf[:, 4 + h:5 + h],
                                                          in1=im[:], op0=ALU.mult, op1=ALU.add), [im], [pso, cf, im])
                imf = impt.next()
                kb.V(lambda e: e.tensor_tensor(out=imf[:], in0=im[:], in1=selvalid[:, o, :], op=ALU.mult),
                     [imf], [im, selvalid])
                kb.V(lambda e: e.tensor_tensor(out=imf[:], in0=imf[:], in1=seladd[:, o, :], op=ALU.add),
                     [imf], [imf, seladd])
                ma, mb = m8.next(), m8.next()
                im2 = impt.next()
                kb.V(lambda e: e.max(out=ma[:], in_=imf[:]), [ma], [imf])
                kb.V(lambda e: e.match_replace(out=im2[:], in_to_replace=ma[:], in_values=imf[:], imm_value=-1e9),
                     [im2], [ma, imf])
                kb.V(lambda e: e.max(out=mb[:], in_=im2[:]), [mb], [im2])
                sel = impt.next()
                kb.V(lambda e: e.tensor_scalar(out=sel[:], in0=imf[:], scalar1=mb[:, 7:8], scalar2=None,
                                               op0=ALU.is_ge), [sel], [imf, mb])
                kb.V(lambda e: e.tensor_tensor(out=sel[:], in0=sel[:], in1=selkeep[:, o, :], op=ALU.mult),
                     [sel], [sel, selkeep])
                zc = slice(64, 128) if g == 0 else slice(0, 64)
                kb.V(lambda e: e.tensor_scalar(out=ZS[g][:, zc], in0=sel[:], scalar1=1.0, scalar2=1e5,
                                               op0=ALU.subtract, op1=ALU.mult), [ZS[g]], [sel])

            def put_bias(g):
                zc = slice(64, 128) if g == 0 else slice(0, 64)
                pz = psbs.next()
                kb.tr(pz, pz[:, 0:128], ZS[g], ZS[g][:], identb, identb[:])
                kb.V(lambda e: e.tensor_copy(
                    out=qbias[g][zc, :].rearrange("p (h q) -> p h q", h=4),
                    in_=pz[zc, 0:128].unsqueeze(1).to_broadcast([64, 4, 128])),
                    [qbias[g]], [pz])

            def fin_soft(g, ps_t, br):
                pv4 = ps_t[:, 0:264].rearrange("p (h x) -> p h x", h=4)
                cf = coef_from(pv4[:, :, 64], ps_t, gav[:, 4 * g:4 * g + 4, br], ga)
                accumulate(oa, g, False, ps_t, pv4[:, :, 0:64], cf)

            jobs = []
            for g in range(2):
                rows = slice(64 * g, 64 * g + 64)
                for ct in range(2):
                    jobs.append(dict(kind="cmp", g=g, first=ct == 0, last=ct == 1, w=128,
                                     lhsT=(KCT, KCT[rows, ct * 128:(ct + 1) * 128]), rhs=(qq[g], qq[g][rows, :]), extra=(),
                                     bias=None, mask=(cmpmask, cmpmask[:, ct, o, :]), v=(VCA, VCA[:, g, ct, :])))
            for g in range(2):
                rows = slice(64 * g, 64 * g + 64)
                for i, kt in enumerate(range(blk - 4, blk + 1)):
                    mk = (upmask, upmask[:]) if i == 0 else ((lowmask, lowmask[:]) if i == 4 else None)
                    jobs.append(dict(kind="win", g=g, first=i == 0, last=i == 4, w=66,
                                     lhsT=(KW, KW[rows, kt * 128:(kt + 1) * 128]), rhs=(qq[g], qq[g][rows, :]), extra=(),
                                     bias=pbias if kt < 16 else None, mask=mk, v=(VW, VW[:, kt, g, :]),
                                     bias_after=(i == 2)))
            for g in range(2):
                for kt in range(blk + 1):
                    mk = (lowmask, lowmask[:]) if kt == blk else None
                    jobs.append(dict(kind="slc", g=g, first=kt == 0, last=kt == blk, w=66,
                                     lhsT=(KE[g], KE[g][:, kt * 128:(kt + 1) * 128]), rhs=(qq[g], qpair[g][0][:, :]),
                                     extra=(qbias[g],), bias=None, mask=mk, v=(VS, VS[:, kt, g, :])))

            def stageA(j):
                pS = PSS.next()
                kb.mm(pS, pS[:, :], j["lhsT"][0], j["lhsT"][1], j["rhs"][0], j["rhs"][1], True, True, extra=j["extra"])
                pt = PTs.next()
                if j["bias"] is None:
                    kb.A(lambda e: e.activation(out=pt[:], in_=pS[:], func=AF.Exp, scale=0.125), [pt], [pS])
                else:
                    bt = j["bias"]
                    kb.A(lambda e: e.activation(out=pt[:], in_=pS[:], func=AF.Exp, scale=0.125, bias=bt[:, 0:1]),
                         [pt], [pS, bt])
                if j["mask"] is not None:
                    mt_, map_ = j["mask"]
                    kb.V(lambda e: e.tensor_tensor(out=P4V(pt), in0=P4V(pt),
                                                   in1=map_.unsqueeze(1).to_broadcast([128, 4, 128]), op=ALU.mult),
                         [pt], [pt, mt_])
                j["pt"] = pt

            acc = {}

            def stageC(j):
                key = (j["kind"], j["g"])
                if j["first"]:
                    acc[key] = PSA.next()
                pa_ = acc[key]
                w, pt = j["w"], j["pt"]
                for h in range(4):
                    kb.mm(pa_, pa_[:, h * w:(h + 1) * w], pt, pt[:, h * 128:(h + 1) * 128], j["v"][0], j["v"][1],
                          j["first"] and h == 0, j["last"] and h == 3, skip=True)
                if j.get("bias_after"):
                    put_bias(j["g"])
                if j["last"]:
                    if j["kind"] == "cmp":
                        fin_cmp(j["g"], pa_)
                    else:
                        fin_soft(j["g"], pa_, 2 if j["kind"] == "win" else 1)

            LA = 2
            for i in range(len(jobs) + LA):
                if i < len(jobs):
                    stageA(jobs[i])
                if i >= LA:
                    stageC(jobs[i - LA])

            ob = oab.next()
            kb.V(lambda e: e.tensor_copy(out=ob[:], in_=oa[:].rearrange("p h d -> p (h d)")), [ob], [oa])
            if o == 1:
                dump("oa", oa, oa[:], [128, 8, 64])
            psb = psbs.next()
            for c in range(4):
                kb.tr(psb, psb[:, c * 128:(c + 1) * 128], ob, ob[:, c * 128:(c + 1) * 128], identb, identb[:])
            kb.V(lambda e: e.tensor_copy(out=OAT[:, :, o * 128:(o + 1) * 128],
                                         in_=psb[:, 0:512].rearrange("p (c t) -> p c t", c=4)), [OAT], [psb])
        kb.barrier()
        live.remove(p2)
    nsa.close()
    live.remove(nsa)
    ck(2)

    ORT = kb.sb(mixs, [128, 8, NO * 128], BF, "ORT")
    with ExitStack() as p3:
        live.append(p3)
        Wr = kb.sb(p3, [128, 8, 3072], BF, "Wr")
        with Staging() as stg:
            load_w(stg, Wr, lambda c0, n: Wr[:, :, c0:c0 + n], w_in3[:, :, 1304:4376], 8, 3072)
        rtc = Rot([kb.sb(p3, [128, 64], F32, "rtc") for _ in range(2)])
        rts = Rot([kb.sb(p3, [128, 64], F32, "rts") for _ in range(2)])
        cur_rt = {}
        decT = cload(p3, "decT")
        qdecrow = cload(p3, "qdecrow")
        kdec = cload(p3, "kdec")
        gnw = cload(p3, "gnw")
        ST = kb.sb(p3, [128, 4, 256], F32, "ST")
        STb = kb.sb(p3, [128, 4, 256], BF, "STb")
        kb.V(lambda e: e.memset(ST[:], 0.0), [ST], [])
        kb.V(lambda e: e.memset(STb[:], 0.0), [STb], [])
        rr = Rot([kb.sb(p3, [128, 4, 64], F32, "rr") for _ in range(4)])
        krs = Rot([kb.sb(p3, [128, 4, 128], F32, "kr") for _ in range(3)])
        KDs = Rot([kb.sb(p3, [128, 512], BF, "KD") for _ in range(2)])
        VBs = Rot([kb.sb(p3, [128, 1024], BF, "VB") for _ in range(2)])
        qkb = Rot([kb.sb(p3, [128, 2, 512], BF, "qkb") for _ in range(2)])
        QKT = Rot([kb.sb(p3, [128, 8, 128], BF, "QKT") for _ in range(2)])
        QTd = Rot([kb.sb(p3, [128, 4, 128], BF, "QTd") for _ in range(2)])
        SCs = Rot([kb.sb(p3, [128, 4, 128], BF, "SC") for _ in range(2)])
        sgs = Rot([kb.sb(p3, [128, 1024], F32, "sg") for _ in range(1)])
        ys = Rot([kb.sb(p3, [128, 1024], F32, "y") for _ in range(1)])
        orb = Rot([kb.sb(p3, [128, 1024], BF, "orb") for _ in range(2)])
        bst = Rot([kb.sb(p3, [128, 4, 6], F32, "bst") for _ in range(2)])
        mv = Rot([kb.sb(p3, [128, 4, 4], F32, "mv") for _ in range(2)])

        def rope_full(dst3, dst_t, pt, blk):
            src3 = pt[:, 0:512].rearrange("p (h d) -> p h d", h=4)
            if cur_rt.get("blk") != blk:
                ropeR_c, ropeR_s = rtc.next(), rts.next()
                kb.dma(SP, ropeR_c[:], D["ropeR_c"][:, blk, :], outs=[ropeR_c])
                kb.dma(SP, ropeR_s[:], D["ropeR_s"][:, blk, :], outs=[ropeR_s])
                cur_rt.update(blk=blk, c=ropeR_c, s=ropeR_s)
            ropeR_c, ropeR_s = cur_rt["c"], cur_rt["s"]
            cb = ropeR_c[:].unsqueeze(1).to_broadcast([128, 4, 64])
            sb_ = ropeR_s[:].unsqueeze(1).to_broadcast([128, 4, 64])
            x1, x2 = src3[:, :, 0:64], src3[:, :, 64:128]
            t1, t2 = rr.next(), rr.next()
            kb.V(lambda e: e.tensor_tensor(out=t1[:], in0=x1, in1=cb, op=ALU.mult), [t1], [pt, ropeR_c])
            kb.V(lambda e: e.tensor_tensor(out=t2[:], in0=x2, in1=sb_, op=ALU.mult), [t2], [pt, ropeR_s])
            kb.V(lambda e: e.tensor_tensor(out=dst3[:, :, 0:64], in0=t1[:], in1=t2[:], op=ALU.subtract),
                 [dst_t], [t1, t2])
            t3, t4 = rr.next(), rr.next()
            kb.V(lambda e: e.tensor_tensor(out=t3[:], in0=x1, in1=sb_, op=ALU.mult), [t3], [pt, ropeR_s])
            kb.V(lambda e: e.tensor_tensor(out=t4[:], in0=x2, in1=cb, op=ALU.mult), [t4], [pt, ropeR_c])
            kb.V(lambda e: e.tensor_tensor(out=dst3[:, :, 64:128], in0=t3[:], in1=t4[:], op=ALU.add),
                 [dst_t], [t3, t4])

        pf3 = Prefetch([xl_blk[b_] for b_ in range(NB)], nmw)
        for blk in range(NB):
            o = blk - OB0
            hT = pf3.cur()
            pk = PSALL.next()
            for c in range(8):
                kb.mm(pk, pk[:, :], hT, hT[:, c, :], Wr, Wr[:, c, 512:1024], c == 0, c == 7)
            kr = krs.next()
            rope_full(kr[:], kr, pk, blk)
            KD = KDs.next()
            kb.V(lambda e: e.tensor_tensor(out=KD[:].rearrange("p (h d) -> p h d", h=4), in0=kr[:],
                                           in1=kdec[:].unsqueeze(2).to_broadcast([128, 4, 128]), op=ALU.mult),
                 [KD], [kr, kdec])
            VB = VBs.next()
            for n in range(2):
                pvv = PSALL.next()
                for c in range(8):
                    kb.mm(pvv, pvv[:, :], hT, hT[:, c, :], Wr, Wr[:, c, 1024 + n * 512:1536 + n * 512], c == 0, c == 7)
                kb.A(lambda e: e.copy(out=VB[:, n * 512:(n + 1) * 512], in_=pvv[:]), [VB], [pvv])
            if o < 0:
                pf3.prefetch()
            if o >= 0:
                pq = PSALL.next()
                for c in range(8):
                    kb.mm(pq, pq[:, :], hT, hT[:, c, :], Wr, Wr[:, c, 0:512], c == 0, c == 7)
                qrf = krs.next()
                rope_full(qrf[:], qrf, pq, blk)
                qk = qkb.next()
                kb.A(lambda e: e.copy(out=qk[:, 0, :].rearrange("p (h d) -> p h d", h=4), in_=qrf[:]), [qk], [qrf])
                kb.A(lambda e: e.copy(out=qk[:, 1, :].rearrange("p (h d) -> p h d", h=4), in_=kr[:]), [qk], [kr])
                sg = sgs.next()
                for n in range(2):
                    pgg = PSALL.next()
                    for c in range(8):
                        kb.mm(pgg, pgg[:, :], hT, hT[:, c, :], Wr, Wr[:, c, 2048 + n * 512:2560 + n * 512],
                              c == 0, c == 7)
                    kb.A(lambda e: e.activation(out=sg[:, n * 512:(n + 1) * 512], in_=pgg[:], func=AF.Silu), [sg], [pgg])
                pf3.prefetch()
                psb = psbs.next()
                for j in range(8):
                    kb.tr(psb, psb[:, j * 128:(j + 1) * 128], qk, qk[:, j // 4, (j % 4) * 128:(j % 4 + 1) * 128],
                          identb, identb[:])
                qkt = QKT.next()
                kb.A(lambda e: e.copy(out=qkt[:], in_=psb[:].rearrange("p (j t) -> p j t", j=8)), [qkt], [psb])
                qtd = QTd.next()
                kb.V(lambda e: e.tensor_tensor(out=qtd[:], in0=qkt[:, 0:4, :], in1=qdecrow[:], op=ALU.mult),
                     [qtd], [qkt, qdecrow])
                psc = PSALL.next()
                for h in range(4):
                    kb.mm(psc, psc[:, h * 128:(h + 1) * 128], qkt, qkt[:, 4 + h, :], qkt, qkt[:, h, :], True, True)
                sc = SCs.next()
                kb.V(lambda e: e.tensor_tensor(out=sc[:], in0=psc[:].rearrange("p (h i) -> p h i", h=4), in1=decT[:],
                                               op=ALU.mult), [sc], [psc, decT])
                y = ys.next()
                bs, mvv = bst.next(), mv.next()
                for n in range(2):
                    po = PSALL.next()
                    for hh in range(2):
                        h = 2 * n + hh
                        kb.mm(po, po[:, hh * 256:(hh + 1) * 256], sc, sc[:, h, :], VB, VB[:, h * 256:(h + 1) * 256],
                              True, False)
                        kb.mm(po, po[:, hh * 256:(hh + 1) * 256], qtd, qtd[:, h, :], STb, STb[:, h, :], False, True)
                    for hh in range(2):
                        h = 2 * n + hh
                        kb.V(lambda e: e.bn_stats(out=bs[:, h, :], in_=po[:, hh * 256:(hh + 1) * 256]), [bs], [po])
                        kb.V(lambda e: e.bn_aggr(out=mvv[:, h, 0:2], in_=bs[:, h, :]), [mvv], [bs])
                        kb.V(lambda e: e.tensor_scalar(out=mvv[:, h, 3:4], in0=mvv[:, h, 1:2], scalar1=EPS, scalar2=None,
                                                       op0=ALU.add), [mvv], [mvv])
                        kb.A(lambda e: e.activation(out=mvv[:, h, 3:4], in_=mvv[:, h, 3:4], func=AF.Sqrt), [mvv], [mvv])
                        kb.V(lambda e: e.reciprocal(out=mvv[:, h, 2:3], in_=mvv[:, h, 3:4]), [mvv], [mvv])
                        kb.V(lambda e: e.tensor_scalar(out=y[:, h * 256:(h + 1) * 256], in0=po[:, hh * 256:(hh + 1) * 256],
                                                       scalar1=mvv[:, h, 0:1], scalar2=mvv[:, h, 2:3],
                                                       op0=ALU.subtract, op1=ALU.mult), [y], [po, mvv])
                if o == 1:
                    dump("ret_y", y, y[:], [128, 1024])
                kb.V(lambda e: e.tensor_tensor(out=y[:], in0=y[:], in1=gnw[:], op=ALU.mult), [y], [y, gnw])
                ob = orb.next()
                kb.V(lambda e: e.tensor_tensor(out=ob[:], in0=y[:], in1=sg[:], op=ALU.mult), [ob], [y, sg])
                psb = psbs.next()
                for c in range(8):
                    kb.tr(psb, psb[:, c * 128:(c + 1) * 128], ob, ob[:, c * 128:(c + 1) * 128], identb, identb[:])
                kb.A(lambda e: e.copy(out=ORT[:, :, o * 128:(o + 1) * 128],
                                      in_=psb[:].rearrange("p (c t) -> p c t", c=8)), [ORT], [psb])
            if blk < NB - 1:
                for n in range(2):
                    pu = PSALL.next()
                    for hh in range(2):
                        h = 2 * n + hh
                        kb.mm(pu, pu[:, hh * 256:(hh + 1) * 256], KD, KD[:, h * 128:(h + 1) * 128],
                              VB, VB[:, h * 256:(h + 1) * 256], True, True)
                    for hh in range(2):
                        h = 2 * n + hh
                        kb.V(lambda e: e.scalar_tensor_tensor(out=ST[:, h, :], in0=ST[:, h, :], scalar=GAMMAS[h] ** 128,
                                                              in1=pu[:, hh * 256:(hh + 1) * 256], op0=ALU.mult,
                                                              op1=ALU.add), [ST], [ST, pu])
                kb.A(lambda e: e.copy(out=STb[:], in_=ST[:]), [STb], [ST])
        dump("ORT", ORT, ORT[:], [128, 8, NO * 128])
        kb.barrier()
        live.remove(p3)
    ck(3)

    x1_blk = x1d.rearrange("(b p) f -> b p f", p=128)
    with ExitStack() as p4:
        live.append(p4)
        Wgm = kb.sb(p4, [128, 8, 2048], BF, "Wgm")
        Wnsa = kb.sb(p4, [128, 4, 1024], BF, "Wnsa")
        Wret = kb.sb(p4, [128, 8, 1024], BF, "Wret")
        Wmix = kb.sb(p4, [128, 8, 1024], BF, "Wmix")
        with Staging() as stg:
            load_w(stg, Wgm, lambda c0, n: Wgm[:, :, c0:c0 + n], w_in3[:, :, 4376:6424], 8, 2048)
            load_w(stg, Wnsa, lambda c0, n: Wnsa[:, :, c0:c0 + n], D["w_nsa"].rearrange("(c p) n -> p c n", p=128), 4, 1024)
            load_w(stg, Wret, lambda c0, n: Wret[:, :, c0:c0 + n], D["w_ret"].rearrange("(c p) n -> p c n", p=128), 8, 1024)
            load_w(stg, Wmix, lambda c0, n: Wmix[:, :, c0:c0 + n], D["w_mix"].rearrange("(c p) n -> p c n", p=128), 8, 1024)
        GMs = Rot([kb.sb(p4, [128, 16, 128], F32, "GM") for _ in range(2)])
        MTs = Rot([kb.sb(p4, [128, 8, 128], BF, "MT") for _ in range(2)])
        mt1 = Rot([kb.sb(p4, [128, 512], F32, "mt1") for _ in range(2)])
        mt2 = Rot([kb.sb(p4, [128, 512], F32, "mt2") for _ in range(2)])
        x1s = Rot([kb.sb(p4, [128, 1024], F32, "x1") for _ in range(2)])
        pf4 = Prefetch([xl_blk[OB0 + o_] for o_ in range(NO)], nmw, want_xt=True)
        for o in range(NO):
            blk = OB0 + o
            hT, xt, _ = pf4.cur()
            gm = GMs.next()
            for q4 in range(4):
                pgm = PSALL.next()
                for j in range(4):
                    mc = q4 * 4 + j
                    for c in range(8):
                        kb.mm(pgm, pgm[:, j * 128:(j + 1) * 128], Wgm, Wgm[:, c, mc * 128:(mc + 1) * 128],
                              hT, hT[:, c, :], c == 0, c == 7)
                kb.A(lambda e: e.activation(out=gm[:, q4 * 4:q4 * 4 + 4, :].rearrange("p a t -> p (a t)"),
                                            in_=pgm[:], func=AF.Sigmoid), [gm], [pgm])
            pf4.prefetch()
            mt = MTs.next()
            for half in range(2):
                pya, pyb = PSALL.next(), PSALL.next()
                for j in range(4):
                    fc = half * 4 + j
                    for c in range(4):
                        kb.mm(pya, pya[:, j * 128:(j + 1) * 128], Wnsa, Wnsa[:, c, fc * 128:(fc + 1) * 128],
                              OAT, OAT[:, c, o * 128:(o + 1) * 128], c == 0, c == 3)
                    for c in range(8):
                        kb.mm(pyb, pyb[:, j * 128:(j + 1) * 128], Wret, Wret[:, c, fc * 128:(fc + 1) * 128],
                              ORT, ORT[:, c, o * 128:(o + 1) * 128], c == 0, c == 7)
                a1, a2 = mt1.next(), mt2.next()
                kb.V(lambda e: e.tensor_tensor(out=a1[:], in0=pya[:],
                                               in1=gm[:, half * 4:half * 4 + 4, :].rearrange("p a t -> p (a t)"),
                                               op=ALU.mult), [a1], [pya, gm])
                kb.V(lambda e: e.tensor_tensor(out=a2[:], in0=pyb[:],
                                               in1=gm[:, 8 + half * 4:12 + half * 4, :].rearrange("p a t -> p (a t)"),
                                               op=ALU.mult), [a2], [pyb, gm])
                kb.V(lambda e: e.tensor_tensor(out=mt[:, half * 4:half * 4 + 4, :].rearrange("p a t -> p (a t)"),
                                               in0=a1[:], in1=a2[:], op=ALU.add), [mt], [a1, a2])
            x1 = x1s.next()
            for n in range(2):
                pm = PSALL.next()
                for fc in range(8):
                    kb.mm(pm, pm[:, :], mt, mt[:, fc, :], Wmix, Wmix[:, fc, n * 512:(n + 1) * 512], fc == 0, fc == 7)
                kb.V(lambda e: e.tensor_tensor(out=x1[:, n * 512:(n + 1) * 512], in0=pm[:],
                                               in1=xt[:, n * 512:(n + 1) * 512], op=ALU.add), [x1], [pm, xt])
            kb.dma(SP, x1_blk[o], x1[:], ins=[x1])
            if o == 1:
                dump("x1", x1, x1[:], [128, 1024])
        kb.barrier()
        live.remove(p4)
    mixs.close()
    live.remove(mixs)
    ck(4)

    with ExitStack() as p5:
        live.append(p5)
        Wup = kb.sb(p5, [128, 8, 5632], BF, "Wup")
        Wdn = kb.sb(p5, [128, 22, 1024], BF, "Wdn")
        with Staging() as stg:
            load_w(stg, Wup, lambda c0, n: Wup[:, :, c0:c0 + n], D["w_up"].rearrange("(c p) n -> p c n", p=128), 8, 5632)
            load_w(stg, Wdn, lambda c0, n: Wdn[:, :, c0:c0 + n], D["w_dn"].rearrange("(c p) n -> p c n", p=128), 22, 1024)
        convw = cload(p5, "convw")
        convb = cload(p5, "convb")
        nlw = cload(p5, "nlw")
        uh = kb.sb(p5, [128, 44, 2], F32, "uh")
        kb.V(lambda e: e.memset(uh[:], 0.0), [uh], [])
        uhc = [T(uh.t) for _ in range(44)]
        for t_ in uhc:
            t_.w = uh.w
        PW = 256
        hT2s = Rot([kb.sb(p5, [128, 8, PW], BF, "hT2") for _ in range(2)])
        uts = Rot([kb.sb(p5, [128, PW + 2], F32, "ut") for _ in range(4)])
        cvs = Rot([kb.sb(p5, [128, PW], F32, "cv") for _ in range(4)])
        cvg = Rot([kb.sb(p5, [128, PW], F32, "cvg") for _ in range(3)])
        ATs = Rot([kb.sb(p5, [128, 22, PW], BF, "AT") for _ in range(1)])
        x2s = Rot([kb.sb(p5, [128, 1024], F32, "x2") for _ in range(1)])
        ys5 = Rot([kb.sb(p5, [128, 1024], F32, "y5") for _ in range(2)])
        out_blk = out_d.rearrange("(b p) f -> b p f", p=128)
        groups = [[0]] + [[1 + 2 * i, 2 + 2 * i] for i in range(8)]
        for grp in groups:
            N = 128 * len(grp)
            hT2 = hT2s.next()
            for j, o in enumerate(grp):
                norm_T(x1_blk[o], nfw, dst=(hT2, hT2[:, :, j * 128:(j + 1) * 128]))
            at = ATs.next()
            atc = [T(at.t) for _ in range(22)]
            for t_ in atc:
                t_.w, t_.r = at.w, dict(at.r)

            def conv_chunk(ch):
                pu = PSS.next()
                for c in range(8):
                    kb.mm(pu, pu[:, 0:N], Wup, Wup[:, c, ch * 128:(ch + 1) * 128], hT2, hT2[:, c, 0:N], c == 0, c == 7)
                ut = uts.next()
                cv = cvs.next()
                kb.A(lambda e: e.copy(out=ut[:, 2:2 + N], in_=pu[:, 0:N]), [ut], [pu])
                kb.A(lambda e: e.activation(out=cv[:, 0:N], in_=pu[:, 0:N], func=AF.Identity,
                                            scale=convw[:, 2, ch:ch + 1], bias=convb[:, ch:ch + 1]),
                     [cv], [pu, convw, convb])
                kb.G(lambda e: e.tensor_copy(out=ut[:, 0:2], in_=uh[:, ch, :]), [ut], [uhc[ch]])
                kb.G(lambda e: e.tensor_copy(out=uh[:, ch, :], in_=ut[:, N:N + 2]), [uhc[ch]], [ut])
                kb.V(lambda e: e.scalar_tensor_tensor(out=cv[:, 0:N], in0=ut[:, 1:1 + N], scalar=convw[:, 1, ch:ch + 1],
                                                      in1=cv[:, 0:N], op0=ALU.mult, op1=ALU.add), [cv], [ut, convw, cv])
                kb.V(lambda e: e.scalar_tensor_tensor(out=cv[:, 0:N], in0=ut[:, 0:N], scalar=convw[:, 0, ch:ch + 1],
                                                      in1=cv[:, 0:N], op0=ALU.mult, op1=ALU.add), [cv], [ut, convw, cv])
                return cv

            for c22 in range(22):
                cg_ = conv_chunk(c22)
                sgt = cvg.next()
                kb.A(lambda e: e.activation(out=sgt[:, 0:N], in_=cg_[:, 0:N], func=AF.Silu), [sgt], [cg_])
                cvv = conv_chunk(22 + c22)
                kb.V(lambda e: e.tensor_tensor(out=at[:, c22, 0:N], in0=sgt[:, 0:N], in1=cvv[:, 0:N], op=ALU.mult),
                     [atc[c22]], [sgt, cvv])
            def fold_at():
                for t_ in atc:
                    deps = list(t_.r.items()) + ([t_.w] if t_.w is not None else [])
                    for sem_, v_ in deps:
                        if at.r.get(sem_, 0) < v_:
                            at.r[sem_] = v_
            if grp == [0]:
                fold_at()
                continue
            for j, o in enumerate(grp):
                xt = xts.next()
                kb.dma(SP, xt[:], x1_blk[o], outs=[xt])
                x2 = x2s.next()
                for n in range(2):
                    pd = PSA.next()
                    for c in range(22):
                        kb.mm(pd, pd[:, :], atc[c], at[:, c, j * 128:(j + 1) * 128], Wdn, Wdn[:, c, n * 512:(n + 1) * 512],
                              c == 0, c == 21)
                    kb.V(lambda e: e.tensor_tensor(out=x2[:, n * 512:(n + 1) * 512], in0=pd[:],
                                                   in1=xt[:, n * 512:(n + 1) * 512], op=ALU.add), [x2], [pd, xt])
                st = stat.next()
                kb.V(lambda e: e.scalar_tensor_tensor(out=sq_junk[:], in0=x2[:], scalar=1.0, in1=x2[:], op0=ALU.mult,
                                                      op1=ALU.mult, accum_out=st[:, 0:1]), [sq_junk, st], [x2])
                kb.V(lambda e: e.tensor_scalar(out=st[:, 1:2], in0=st[:, 0:1], scalar1=1.0 / 1024, scalar2=EPS,
                                               op0=ALU.mult, op1=ALU.add), [st], [st])
                kb.A(lambda e: e.activation(out=st[:, 3:4], in_=st[:, 1:2], func=AF.Sqrt), [st], [st])
                kb.V(lambda e: e.reciprocal(out=st[:, 2:3], in_=st[:, 3:4]), [st], [st])
                y5 = ys5.next()
                kb.V(lambda e: e.scalar_tensor_tensor(out=y5[:], in0=x2[:], scalar=st[:, 2:3], in1=nlw[:], op0=ALU.mult,
                                                      op1=ALU.mult), [y5], [x2, st, nlw])
                kb.dma(SP, out_blk[o - 1], y5[:], ins=[y5])
            fold_at()
        kb.barrier()
        live.remove(p5)
    return dbg_out


def finish(kb, top, stacks, dbg_out):
    kb.barrier()
    for s in stacks:
        s.close()
    top.close()
    return kb.nc, dbg_out


def _consts(s):
    f = np.float32
    c = {}
    c["ident"] = np.eye(128, dtype=f)
    tloc = np.arange(4096)
    pos = np.where(tloc < 2048, tloc, 2048 * s + tloc - 2048).astype(np.float64)
    if s == 0:
        pos[:2048] = 0.0

    def rope_tab(theta, half):
        inv = np.power(np.float32(theta), -np.arange(half, dtype=np.float32) / half).astype(np.float32)
        ang = pos.astype(np.float32)[:, None] * inv[None, :]
        cs, sn = np.cos(ang).astype(f), np.sin(ang).astype(f)
        to = lambda a: np.ascontiguousarray(a.reshape(32, 128, half).transpose(1, 0, 2))
        return to(cs), to(sn)
    c["ropeN_c"], c["ropeN_s"] = rope_tab(500000.0, 8)
    c["ropeR_c"], c["ropeR_s"] = rope_tab(10000.0, 64)
    c["expand"] = (np.arange(4096)[None, :] // 64 == np.arange(64)[:, None]).astype(f)
    cst = np.arange(256) * 16
    sst = np.arange(64) * 64
    ov = np.clip(np.minimum(cst[:, None] + 32, sst[None, :] + 64) - np.maximum(cst[:, None], sst[None, :]), 0, None) / 32.0
    ov[255] = 0.0
    c["ov"] = np.ascontiguousarray(ov.reshape(2, 128, 64).transpose(1, 0, 2)).astype(f)
    lo_c = 0 if s == 1 else 128
    lo_j = 0 if s == 1 else 32
    tq = (OB0 * 128 + np.arange(NO * 128))
    cm = (16 * np.arange(256)[:, None] + 31 <= tq[None, :]) & (np.arange(256)[:, None] >= lo_c) & (np.arange(256)[:, None] < 255)
    c["cmpmask"] = np.ascontiguousarray(cm.reshape(2, 128, NO, 128).transpose(1, 0, 2, 3)).astype(f)
    cur = tq // 64
    jb = np.arange(64)[None, :]
    valid = (jb <= cur[:, None]) & (jb >= lo_j)
    forced = ((jb == lo_j) | (jb == cur[:, None]) | (jb == cur[:, None] - 1)) & valid
    add = np.where(forced, 1e4 + jb, np.where(valid, 0.0, -1e4 - jb))
    vm = (valid & ~forced)
    c["selvalid"] = np.ascontiguousarray(vm.reshape(NO, 128, 64).transpose(1, 0, 2)).astype(f)
    c["seladd"] = np.ascontiguousarray(add.reshape(NO, 128, 64).transpose(1, 0, 2)).astype(f)
    c["selkeep"] = np.ascontiguousarray(valid.reshape(NO, 128, 64).transpose(1, 0, 2)).astype(f)
    k_, q_ = np.arange(128)[:, None], np.arange(128)[None, :]
    c["lowmask"] = (k_ <= q_).astype(f)
    c["upmask"] = (k_ > q_).astype(f)
    c["pbias"] = np.full((128, 1), 0.0 if s == 1 else -30000.0, dtype=f)
    g = np.array(GAMMAS, dtype=np.float64)
    i_ = np.arange(128)
    dk = 128 ** -0.5
    dec = np.where(i_[None, :] >= i_[:, None], g[:, None, None] ** np.maximum(i_[None, :] - i_[:, None], 0)[None], 0.0)
    c["decT"] = np.ascontiguousarray((dec * dk).transpose(1, 0, 2)).astype(f)
    c["qdecrow"] = np.ascontiguousarray(np.broadcast_to((g[:, None] ** (i_[None, :] + 1.0))[None], (128, 4, 128))).astype(f)
    c["kdec"] = np.ascontiguousarray((g[None, :] ** (127.0 - i_[:, None])) * dk).astype(f)
    return c


def kernel(x, norm_mix_w, w_in, cmp_pe_k, cmp_w1_k, cmp_w2_k, cmp_pe_v, cmp_w1_v, cmp_w2_v,
           w_nsa_branch, ret_gn_w, w_ret_branch, w_mix_out, norm_ffn_w, w_ffn_up, ffn_conv_w,
           ffn_conv_b, w_ffn_down, norm_final_w, _build_kwargs=None, _return_all=False):
    f = np.float32
    A = lambda a: np.ascontiguousarray(np.asarray(a, dtype=f))
    x = A(x)
    shared = {
        "w_in": A(w_in)[0], "cmp_w1_k": A(cmp_w1_k)[0], "cmp_w2_k": A(cmp_w2_k)[0], "cmp_w1_v": A(cmp_w1_v)[0],
        "cmp_w2_v": A(cmp_w2_v)[0], "w_nsa": A(w_nsa_branch)[0], "w_ret": A(w_ret_branch)[0], "w_mix": A(w_mix_out)[0],
        "w_up": A(w_ffn_up)[0], "w_dn": A(w_ffn_down)[0],
        "nmw": np.ascontiguousarray(A(norm_mix_w)[0].reshape(8, 128).T),
        "nfw": np.ascontiguousarray(A(norm_ffn_w)[0].reshape(8, 128).T),
        "nlw": np.ascontiguousarray(np.broadcast_to(A(norm_final_w)[None, :], (128, 1024))),
        "gnw": np.ascontiguousarray(np.broadcast_to(A(ret_gn_w)[0].reshape(1, 1024), (128, 1024))),
        "convw": np.ascontiguousarray(A(ffn_conv_w)[0].reshape(3, 44, 128).transpose(2, 0, 1)),
        "convb": np.ascontiguousarray(A(ffn_conv_b)[0].reshape(44, 128).T),
        "peTk": np.ascontiguousarray(np.concatenate([A(cmp_pe_k)[0].T] * 2, axis=0)),
        "peTv": np.ascontiguousarray(np.concatenate([A(cmp_pe_v)[0].T] * 2, axis=0)),
    }
    consts = [_consts(0), _consts(1)]
    nc, dbg = build(**(_build_kwargs or {}))
    in_maps = []
    for core in range(8):
        b, s = core // 2, core % 2
        xl = np.zeros((4096, 1024), dtype=f)
        if s == 1:
            xl[:] = x[b]
        else:
            xl[2048:] = x[b, :2048]
        m = dict(shared)
        m["xl"] = xl
        cs = consts[s]
        for k_, v in cs.items():
            if not k_.startswith("_"):
                m[k_] = v
        in_maps.append(m)
    res = run_bass_kernel_spmd(nc, in_maps, core_ids=list(range(8)))
    if _return_all:
        return res
    out = np.zeros((4, 4096, 1024), dtype=f)
    for core in range(8):
        b, s = core // 2, core % 2
        out[b, 2048 * s:2048 * (s + 1)] = res.results[core]["out"]
    return out
```

```python
import numpy as np
from contextlib import ExitStack
import concourse.bass as bass
import concourse.mybir as mybir
from concourse.bass_utils import run_bass_kernel_spmd

F32 = mybir.dt.float32
BF = mybir.dt.bfloat16
AF = mybir.ActivationFunctionType
ALU = mybir.AluOpType
AX = mybir.AxisListType

NB = 32
OB0 = 15
NO = 17
EPS = 1e-6
SEM_MAX = 12000
import os
_VAR = os.environ.get('KVAR', '')


class Eng:
    def __init__(self, nc, e, name, is_pe=False):
        self.nc, self.e, self.name, self.is_pe = nc, e, name, is_pe
        self.k = 0
        self.own = set()
        self.waited = {}
        self.new_sem()

    def new_sem(self):
        self.sem = self.nc.alloc_semaphore(f"{self.name}_c{self.k}")
        self.k += 1
        self.count = 0
        self.own.add(self.sem)


class T:
    def __init__(self, t, psum=False):
        self.t = t
        self.w = None
        self.r = {}
        self.psum = psum

    def __getitem__(self, k):
        return self.t[k]


class K:
    def __init__(self):
        nc = bass.Bass("TRN2", target_bir_lowering=False)
        self.nc = nc
        self.PE = Eng(nc, nc.tensor, "pe", True)
        self.ACT = Eng(nc, nc.scalar, "act")
        self.DVE = Eng(nc, nc.vector, "dve")
        self.POOL = Eng(nc, nc.gpsimd, "pool")
        self.SP = Eng(nc, nc.sync, "sp")
        self.dsems = [nc.alloc_semaphore(f"dma{i}") for i in range(20)]
        self.dvals = [0] * 20
        self.di = 0
        self.uid = 0

    def sb(self, stack, shape, dt, name=None):
        self.uid += 1
        t = stack.enter_context(self.nc.sbuf_tensor(f"{name or 't'}_{self.uid}", list(shape), dt))
        return T(t)

    def ps(self, stack, shape, dt, name=None):
        self.uid += 1
        t = stack.enter_context(self.nc.psum_tensor(f"{name or 'p'}_{self.uid}", list(shape), dt))
        return T(t, psum=True)

    def _waits(self, E, outs, ins):
        need = {}

        def add(sem, v):
            if need.get(sem, 0) < v:
                need[sem] = v
        for b in ins:
            if b.w is not None:
                add(*b.w)
            if b.psum:
                for sem, v in b.r.items():
                    if sem not in E.own:
                        add(sem, v)
        for b in outs:
            if b.w is not None:
                add(*b.w)
            for sem, v in b.r.items():
                add(sem, v)
        for sem, v in need.items():
            if E.is_pe and sem in E.own:
                continue
            if E.waited.get(sem, 0) >= v:
                continue
            E.e.wait_ge(sem, v)
            E.waited[sem] = v

    def _mark(self, d, outs, ins):
        sem, v = d
        for b in ins:
            if b.r.get(sem, 0) < v:
                b.r[sem] = v
        for b in outs:
            b.w = d
            b.r = {}

    def op(self, E, fn, outs=(), ins=()):
        self._waits(E, outs, ins)
        if E.count >= SEM_MAX:
            E.new_sem()
        i = fn(E.e)
        E.count += 1
        i.then_inc(E.sem, 1)
        self._mark((E.sem, E.count), outs, ins)

    def dma(self, Q, out_ap, in_ap, outs=(), ins=(), **kw):
        self._waits(Q, outs, ins)
        i = self.di
        self.di = (self.di + 1) % len(self.dsems)
        if self.dvals[i] >= SEM_MAX:
            if Q.waited.get(self.dsems[i], 0) < self.dvals[i]:
                Q.e.wait_ge(self.dsems[i], self.dvals[i])
            self.uid += 1
            self.dsems[i] = self.nc.alloc_semaphore(f"dmax{self.uid}")
            self.dvals[i] = 0
        sem, prev = self.dsems[i], self.dvals[i]
        if prev > 0 and Q.waited.get(sem, 0) < prev:
            Q.e.wait_ge(sem, prev)
            Q.waited[sem] = prev
        Q.e.dma_start(out=out_ap, in_=in_ap, **kw).then_inc(sem, 16)
        self.dvals[i] = prev + 16
        self._mark((sem, prev + 16), outs, ins)
        return (sem, prev + 16)

    def barrier(self):
        engs = [self.PE, self.ACT, self.DVE, self.POOL, self.SP]
        for E in engs:
            for P in engs:
                if P is E or P.count == 0:
                    continue
                if E.waited.get(P.sem, 0) < P.count:
                    E.e.wait_ge(P.sem, P.count)
                    E.waited[P.sem] = P.count
            for sem, v in zip(self.dsems, self.dvals):
                if v > 0 and E.waited.get(sem, 0) < v:
                    E.e.wait_ge(sem, v)
                    E.waited[sem] = v

    def mm(self, out_t, out_ap, lhsT_t, lhsT_ap, rhs_t, rhs_ap, start, stop, skip=False, extra=()):
        self.op(self.PE, lambda e: e.matmul(out_ap, lhsT=lhsT_ap, rhs=rhs_ap, start=start, stop=stop,
                                            skip_group_check=skip),
                outs=[out_t], ins=[lhsT_t, rhs_t] + list(extra))

    def tr(self, out_t, out_ap, in_t, in_ap, ident_t, ident_ap):
        self.op(self.PE, lambda e: e.transpose(out_ap, in_ap, ident_ap), outs=[out_t], ins=[in_t, ident_t])

    def V(self, fn, outs, ins):
        self.op(self.DVE, fn, outs, ins)

    def A(self, fn, outs, ins):
        self.op(self.ACT, fn, outs, ins)

    def G(self, fn, outs, ins):
        self.op(self.POOL, fn, outs, ins)


class Rot:
    def __init__(self, items):
        self.items = items
        self.i = 0

    def next(self):
        t = self.items[self.i]
        self.i = (self.i + 1) % len(self.items)
        return t


CONST_SPECS = {
    "ident": [128, 128], "nmw": [128, 8], "nfw": [128, 8], "nlw": [128, 1024],
    "ropeN_c": [128, 32, 8], "ropeN_s": [128, 32, 8], "ropeR_c": [128, 32, 64], "ropeR_s": [128, 32, 64],
    "expand": [64, 4096], "ov": [128, 2, 64], "cmpmask": [128, 2, 17, 128],
    "selvalid": [128, 17, 64], "seladd": [128, 17, 64], "selkeep": [128, 17, 64], "lowmask": [128, 128], "upmask": [128, 128],
    "pbias": [128, 1], "decT": [128, 4, 128], "qdecrow": [128, 4, 128], "kdec": [128, 4],
    "gnw": [128, 1024], "convw": [128, 3, 44], "convb": [128, 44], "peTk": [128, 32], "peTv": [128, 32],
}
WEIGHT_SPECS = {
    "w_in": [1024, 6424], "cmp_w1_k": [2048, 256], "cmp_w2_k": [256, 64], "cmp_w1_v": [2048, 256],
    "cmp_w2_v": [256, 64], "w_nsa": [512, 1024], "w_ret": [1024, 1024], "w_mix": [1024, 1024],
    "w_up": [1024, 5632], "w_dn": [2816, 1024],
}
GAMMAS = [1.0 - 2.0 ** (-5.0 - h) for h in range(4)]


class _Stop(Exception):
    pass


def build(stop_after=None, debug=()):
    kb = K()
    live = []
    top = ExitStack()
    try:
        dbg_out = _build_body(kb, top, live, stop_after, debug)
    except _Stop as e:
        dbg_out = e.args[0]
    return finish(kb, top, list(reversed(live)), dbg_out)


def _build_body(kb, top, live, stop_after, debug):
    nc = kb.nc
    PE, ACT, DVE, POOL, SP = kb.PE, kb.ACT, kb.DVE, kb.POOL, kb.SP
    D = {}
    D["xl"] = nc.dram_tensor("xl", [4096, 1024], F32, kind="ExternalInput").ap()
    for n, s in list(CONST_SPECS.items()) + list(WEIGHT_SPECS.items()):
        D[n] = nc.dram_tensor(n, s, F32, kind="ExternalInput").ap()
    out_d = nc.dram_tensor("out", [2048, 1024], F32, kind="ExternalOutput").ap()
    x1d = nc.dram_tensor("x1d", [NO * 128, 1024], F32, kind="Internal").ap()
    dbg_out = {}

    def ck(tag):
        if stop_after == tag:
            kb.barrier()
            for st_ in reversed(live):
                st_.close()
            del live[:]
            raise _Stop(dbg_out)

    def cload(stack, name, dt=F32, eng=None):
        shape = CONST_SPECS[name]
        t = kb.sb(stack, shape, F32, name)
        kb.dma(SP, t[:], D[name], outs=[t])
        if dt == F32:
            return t
        tb = kb.sb(stack, shape, dt, name + "b")
        kb.V(lambda e: e.tensor_copy(out=tb[:], in_=t[:]), [tb], [t])
        return tb

    identf = cload(top, "ident")
    identb = kb.sb(top, [128, 128], BF, "identb")
    kb.V(lambda e: e.tensor_copy(out=identb[:], in_=identf[:]), [identb], [identf])
    nmw = cload(top, "nmw")
    nfw = cload(top, "nfw")
    lowmask = cload(top, "lowmask", BF)
    upmask = cload(top, "upmask", BF)
    pbias = cload(top, "pbias")
    zbias = kb.sb(top, [128, 1], F32, "zbias")
    kb.V(lambda e: e.memset(zbias[:], 0.0), [zbias], [])

    psf = [kb.ps(top, [128, 512], F32, f"psf{i}") for i in range(6)]
    psbs = Rot([kb.ps(top, [128, 1024], BF, f"psb{i}") for i in range(2)])
    psb = psbs.next()
    PSS = Rot(psf[0:4])
    PSA = Rot(psf[4:6])
    PSALL = Rot(psf)

    xts = Rot([kb.sb(top, [128, 1024], F32, "xt") for _ in range(2)])
    xns = Rot([kb.sb(top, [128, 1024], BF, "xn") for _ in range(2)])
    hTs = Rot([kb.sb(top, [128, 8, 128], BF, "hT") for _ in range(2)])
    sq_junk = kb.sb(top, [128, 1024], BF, "sqj")
    stat = Rot([kb.sb(top, [128, 4], F32, "stat") for _ in range(4)])

    def norm_T(src_ap, wcol, want_xt=False, dst=None):
        xt = xts.next()
        kb.dma(SP, xt[:], src_ap, outs=[xt])
        st = stat.next()
        kb.V(lambda e: e.scalar_tensor_tensor(out=sq_junk[:], in0=xt[:], scalar=1.0, in1=xt[:], op0=ALU.mult,
                                              op1=ALU.mult, accum_out=st[:, 0:1]), [sq_junk, st], [xt])
        kb.V(lambda e: e.tensor_scalar(out=st[:, 1:2], in0=st[:, 0:1], scalar1=1.0 / 1024, scalar2=EPS,
                                       op0=ALU.mult, op1=ALU.add), [st], [st])
        kb.A(lambda e: e.activation(out=st[:, 3:4], in_=st[:, 1:2], func=AF.Sqrt), [st], [st])
        kb.V(lambda e: e.reciprocal(out=st[:, 2:3], in_=st[:, 3:4]), [st], [st])
        xn = xns.next()
        pbn = psbs.next()
        kb.A(lambda e: e.activation(out=xn[:], in_=xt[:], func=AF.Copy, scale=st[:, 2:3]), [xn], [xt, st])
        for c in range(8):
            kb.tr(pbn, pbn[:, c * 128:(c + 1) * 128], xn, xn[:, c * 128:(c + 1) * 128], identb, identb[:])
        if dst is None:
            hT = hTs.next()
            hT_ap = hT[:]
        else:
            hT, hT_ap = dst
        kb.V(lambda e: e.tensor_tensor(out=hT_ap, in0=pbn[:].rearrange("p (c t) -> p c t", c=8),
                                       in1=wcol[:].unsqueeze(2).to_broadcast([128, 8, 128]), op=ALU.mult),
             [hT], [pbn, wcol])
        if want_xt:
            return hT, xt, st
        return hT

    class Staging:
        def __enter__(self):
            self.ws = ExitStack()
            live.append(self.ws)
            self.rot = Rot([kb.sb(self.ws, [128, 2048], F32, "stg") for _ in range(4)])
            return self.rot

        def __exit__(self, *a):
            kb.barrier()
            live.remove(self.ws)
            self.ws.close()
            return False

    class Prefetch:
        def __init__(self, srcs, wcol, want_xt=False):
            self.srcs, self.wcol, self.want_xt = srcs, wcol, want_xt
            self.nxt = norm_T(srcs[0], wcol, want_xt) if srcs else None
            self.i = 0

        def cur(self):
            return self.nxt

        def prefetch(self):
            self.i += 1
            if self.i < len(self.srcs):
                self.nxt = norm_T(self.srcs[self.i], self.wcol, self.want_xt)

    cast_rr = [0]

    def load_w(stg, dst, dst_ap_fn, src3d, kc, ncols):
        step = 1 << ((2048 // kc).bit_length() - 1)
        for c0 in range(0, ncols, step):
            n = min(step, ncols - c0)
            s = stg.next()
            sv = s[:, 0:kc * n].rearrange("p (c n) -> p c n", c=kc)
            kb.dma(SP, sv, src3d[:, :, c0:c0 + n], outs=[s])
            cast_rr[0] = (cast_rr[0] + 1) % 3
            if cast_rr[0] == 0:
                kb.G(lambda e: e.tensor_copy(out=dst_ap_fn(c0, n), in_=sv), [dst], [s])
            elif cast_rr[0] == 1:
                kb.V(lambda e: e.tensor_copy(out=dst_ap_fn(c0, n), in_=sv), [dst], [s])
            else:
                kb.A(lambda e: e.copy(out=dst_ap_fn(c0, n), in_=sv), [dst], [s])

    def dump(name, t, ap, shape):
        if name in debug:
            d = nc.dram_tensor("dbg_" + name, list(shape), ap.dtype, kind="ExternalOutput").ap()
            kb.dma(SP, d, ap, ins=[t])
            dbg_out[name] = d

    w_in3 = D["w_in"].rearrange("(c p) n -> p c n", p=128)
    xl_blk = D["xl"].rearrange("(b p) f -> b p f", p=128)

    mixs = ExitStack()
    live.append(mixs)
    OAT = kb.sb(mixs, [128, 4, NO * 128], BF, "OAT")
    nsa = ExitStack()
    live.append(nsa)
    KE = [kb.sb(nsa, [128, 4096], BF, f"KE{g}") for g in range(2)]
    KW = kb.sb(nsa, [128, 4096], BF, "KW")
    VS = kb.sb(nsa, [128, 32, 2, 66], BF, "VS")
    VW = kb.sb(nsa, [128, 32, 2, 66], BF, "VW")
    KCT = kb.sb(nsa, [128, 256], BF, "KCT")
    VCA = kb.sb(nsa, [128, 2, 2, 128], BF, "VCA")

    kb.V(lambda e: e.memset(VS[:], 1.0), [VS], [])
    kb.V(lambda e: e.memset(VW[:], 1.0), [VW], [])
    kb.V(lambda e: e.memset(KCT[:], 0.0), [KCT], [])
    with Staging() as stg:
        for g, rows in ((0, slice(64, 128)), (1, slice(0, 64))):
            for c0 in range(0, 4096, 2048):
                s = stg.next()
                kb.dma(SP, s[rows, 0:2048], D["expand"][:, c0:c0 + 2048], outs=[s])
                kb.G(lambda e: e.tensor_copy(out=KE[g][rows, c0:c0 + 2048], in_=s[rows, 0:2048]), [KE[g]], [s])

    ck(0.3)
    with ExitStack() as p1:
        live.append(p1)
        Wkv = kb.sb(p1, [128, 8, 768], BF, "Wkv")
        W1 = {nm: kb.sb(p1, [128, 32, 256], BF, "W1" + nm) for nm in ("k", "v")}
        W2k = kb.sb(p1, [128, 2, 128], BF, "W2k")
        W2v = kb.sb(p1, [128, 2, 64], BF, "W2v")
        peT = {"k": cload(p1, "peTk", BF), "v": cload(p1, "peTv", BF)}
        ropeN_c = cload(p1, "ropeN_c")
        ropeN_s = cload(p1, "ropeN_s")
        ovb = cload(p1, "ov", BF)
        for g in range(2):
            for ct in range(2):
                kb.V(lambda e: e.tensor_copy(out=VCA[:, g, ct, 64:128], in_=ovb[:, ct, :]), [VCA], [ovb])
        with Staging() as stg:
            load_w(stg, Wkv, lambda c0, n: Wkv[:, :, c0:c0 + n], w_in3[:, :, 512:1280], 8, 768)
            for nm in ("k", "v"):
                src = D["cmp_w1_" + nm].rearrange("(l d) m -> d l m", d=64)
                for half in range(2):
                    rows = slice(64 * half, 64 * half + 64)
                    for l0 in range(0, 32, 8):
                        s = stg.next()
                        sv = s[rows, :].rearrange("p (l m) -> p l m", l=8)
                        kb.dma(SP, sv, src[:, l0:l0 + 8, :], outs=[s])
                        kb.G(lambda e: e.tensor_copy(out=W1[nm][rows, l0:l0 + 8, :], in_=sv), [W1[nm]], [s])
            s = stg.next()
            kb.dma(SP, s[:, 0:128].rearrange("p (c d) -> p c d", c=2),
                   D["cmp_w2_k"].rearrange("(c p) d -> p c d", p=128), outs=[s])
            kb.dma(SP, s[:, 128:256].rearrange("p (c d) -> p c d", c=2),
                   D["cmp_w2_v"].rearrange("(c p) d -> p c d", p=128), outs=[s])
            sk = s[:, 0:128].rearrange("p (c d) -> p c d", c=2)
            kb.G(lambda e: e.tensor_copy(out=W2k[:, :, 0:64], in_=sk), [W2k], [s])
            kb.G(lambda e: e.tensor_copy(out=W2k[:, :, 64:128], in_=sk), [W2k], [s])
            kb.G(lambda e: e.tensor_copy(out=W2v[:], in_=s[:, 128:256].rearrange("p (c d) -> p c d", c=2)), [W2v], [s])
        KC = kb.sb(p1, [128, 4096], BF, "KC")
        VC = kb.sb(p1, [128, 4096], BF, "VC")
        ktin = Rot([kb.sb(p1, [128, 4, 128], BF, "ktin") for _ in range(2)])
        rtmp = Rot([kb.sb(p1, [128, 2, 8], F32, "rtmp") for _ in range(16)])

        ck(0.5)
        if stop_after == 0.55:
            hT = norm_T(xl_blk[17], nmw)
            dump("hT", hT, hT[:], [128, 8, 128])
            ck(0.55)
        for blk in range(NB if not (isinstance(stop_after, float) and 0.6 <= stop_after < 0.7) else 1):
            if blk == 0:
                pf1 = Prefetch([xl_blk[b_] for b_ in range(NB)], nmw)
            hT = pf1.cur()
            pa, pb = PSS.next(), PSS.next()
            for c in range(8):
                kb.mm(pa, pa[:, 0:512], hT, hT[:, c, :], Wkv, Wkv[:, c, 0:512], c == 0, c == 7)
            for c in range(8):
                kb.mm(pb, pb[:, 0:256], hT, hT[:, c, :], Wkv, Wkv[:, c, 512:768], c == 0, c == 7)
            pf1.prefetch()
            ck(0.61)
            kt = ktin.next()
            kb.A(lambda e: e.copy(out=kt[:, 0:2, :], in_=pa[:, 0:256].rearrange("p (a d) -> p a d", a=2)), [kt], [pa])
            kb.A(lambda e: e.copy(out=VS[:, blk, :, 0:64], in_=pa[:, 384:512].rearrange("p (g d) -> p g d", g=2)),
                 [VS], [pa])
            kb.A(lambda e: e.copy(out=VW[:, blk, :, 0:64], in_=pb[:, 128:256].rearrange("p (g d) -> p g d", g=2)),
                 [VW], [pb])
            ck(0.62)
            for slot, (pt, lo) in ((2, (pa, 256)), (3, (pb, 0))):
                src3 = pt[:, lo:lo + 128].rearrange("p (g d) -> p g d", g=2)
                dst3 = kt[:, slot, :].rearrange("p (g d) -> p g d", g=2)
                cb = ropeN_c[:, blk, :].unsqueeze(1).to_broadcast([128, 2, 8])
                sb_ = ropeN_s[:, blk, :].unsqueeze(1).to_broadcast([128, 2, 8])
                x1, x2 = src3[:, :, 0:8], src3[:, :, 8:16]
                t1, t2, t3, t4 = rtmp.next(), rtmp.next(), rtmp.next(), rtmp.next()
                kb.V(lambda e: e.tensor_tensor(out=t1[:], in0=x1, in1=cb, op=ALU.mult), [t1], [pt, ropeN_c])
                kb.V(lambda e: e.tensor_tensor(out=t2[:], in0=x2, in1=sb_, op=ALU.mult), [t2], [pt, ropeN_s])
                kb.V(lambda e: e.tensor_tensor(out=t3[:], in0=x1, in1=sb_, op=ALU.mult), [t3], [pt, ropeN_s])
                kb.V(lambda e: e.tensor_tensor(out=t4[:], in0=x2, in1=cb, op=ALU.mult), [t4], [pt, ropeN_c])
                kb.V(lambda e: e.tensor_tensor(out=dst3[:, :, 0:8], in0=t1[:], in1=t2[:], op=ALU.subtract), [kt], [t1, t2])
                kb.V(lambda e: e.tensor_tensor(out=dst3[:, :, 8:16], in0=t3[:], in1=t4[:], op=ALU.add), [kt], [t3, t4])
                if "noactcopy" in _VAR:
                    kb.V(lambda e: e.tensor_copy(out=dst3[:, :, 16:64], in_=src3[:, :, 16:64]), [kt], [pt])
                else:
                    kb.A(lambda e: e.copy(out=dst3[:, :, 16:64], in_=src3[:, :, 16:64]), [kt], [pt])
            ck(0.63)
            psb = psbs.next()
            for slot in range(4):
                kb.tr(psb, psb[:, slot * 128:(slot + 1) * 128], kt, kt[:, slot, :], identb, identb[:])
            ck(0.64)
            cs = slice(blk * 128, (blk + 1) * 128)
            kb.V(lambda e: e.tensor_copy(out=KC[:, cs], in_=psb[:, 0:128]), [KC], [psb])
            kb.V(lambda e: e.tensor_copy(out=VC[:, cs], in_=psb[:, 128:256]), [VC], [psb])
            kb.V(lambda e: e.tensor_copy(out=KE[0][0:64, cs], in_=psb[0:64, 256:384]), [KE[0]], [psb])
            kb.V(lambda e: e.tensor_copy(out=KE[1][64:128, cs], in_=psb[64:128, 256:384]), [KE[1]], [psb])
            kb.V(lambda e: e.tensor_copy(out=KW[:, cs], in_=psb[:, 384:512]), [KW], [psb])

        dump("KE0", KE[0], KE[0][:], [128, 4096])
        dump("KE1", KE[1], KE[1][:], [128, 4096])
        dump("KW", KW, KW[:], [128, 4096])
        dump("KC", KC, KC[:], [128, 4096])
        dump("VS", VS, VS[:], [128, 32, 2, 66])

        ck(0.6)
        ck(0.7)
        GT = kb.sb(p1, [128, 2, 256], BF, "GT")
        hb = kb.sb(p1, [128, 1], F32, "hb")
        gx = [kb.sb(p1, [128, 255], F32, f"gx{i}") for i in range(4)]
        for nm, src in (("k", KC), ("v", VC)):
            for g in range(2):
                rows = slice(64 * g, 64 * g + 64)
                kb.V(lambda e: e.memset(GT[:], 0.0), [GT], [])
                for mc in range(2):
                    ph, pbi = PSS.next(), PSS.next()
                    for l in range(32):
                        kb.mm(ph, ph[:, 0:255], W1[nm], W1[nm][rows, l, mc * 128:(mc + 1) * 128],
                              src, src[rows, l:l + 16 * 254 + 1:16], l == 0, l == 31)
                    for l in range(32):
                        kb.mm(pbi, pbi[:, 0:1], W1[nm], W1[nm][rows, l, mc * 128:(mc + 1) * 128],
                              peT[nm], peT[nm][rows, l:l + 1], l == 0, l == 31)
                    kb.V(lambda e: e.tensor_copy(out=hb[:], in_=pbi[:, 0:1]), [hb], [pbi])
                    x0, x2_, x3_, sg = gx
                    kb.V(lambda e: e.tensor_scalar(out=x0[:], in0=ph[:, 0:255], scalar1=hb[:, 0:1], scalar2=None,
                                                   op0=ALU.add), [x0], [ph, hb])
                    kb.V(lambda e: e.tensor_tensor(out=x2_[:], in0=x0[:], in1=x0[:], op=ALU.mult), [x2_], [x0])
                    kb.V(lambda e: e.tensor_scalar(out=x2_[:], in0=x2_[:], scalar1=0.044715, scalar2=1.0,
                                                   op0=ALU.mult, op1=ALU.add), [x2_], [x2_])
                    kb.V(lambda e: e.tensor_tensor(out=x3_[:], in0=x2_[:], in1=x0[:], op=ALU.mult), [x3_], [x2_, x0])
                    kb.A(lambda e: e.activation(out=sg[:], in_=x3_[:], func=AF.Sigmoid, scale=1.5957691216057308),
                         [sg], [x3_])
                    kb.V(lambda e: e.tensor_tensor(out=GT[:, mc, 0:255], in0=x0[:], in1=sg[:], op=ALU.mult),
                         [GT], [x0, sg])
                if nm == "k":
                    pk = PSS.next()
                    for mc in range(2):
                        kb.mm(pk, pk[:, 0:255], W2k, W2k[:, mc, :], GT, GT[:, mc, 0:255], mc == 0, mc == 1)
                    kb.V(lambda e: e.tensor_copy(out=KCT[rows, 0:255], in_=pk[rows, 0:255]), [KCT], [pk])
                else:
                    for ct in range(2):
                        pv = PSS.next()
                        for mc in range(2):
                            kb.mm(pv, pv[:, 0:64], GT, GT[:, mc, ct * 128:(ct + 1) * 128], W2v, W2v[:, mc, :],
                                  mc == 0, mc == 1)
                        kb.V(lambda e: e.tensor_copy(out=VCA[:, g, ct, 0:64], in_=pv[:, 0:64]), [VCA], [pv])
        dump("KCT", KCT, KCT[:], [128, 256])
        dump("VCA", VCA, VCA[:], [128, 2, 2, 128])
        kb.barrier()
        live.remove(p1)
    ck(1)

    with ExitStack() as p2:
        live.append(p2)
        Wq = kb.sb(p2, [128, 8, 536], BF, "Wq")
        with Staging() as stg:
            load_w(stg, Wq, lambda c0, n: Wq[:, :, c0:c0 + n], w_in3[:, :, 0:512], 8, 512)
            load_w(stg, Wq, lambda c0, n: Wq[:, :, 512 + c0:512 + c0 + n], w_in3[:, :, 1280:1304], 8, 24)
        ropeN_c = cload(p2, "ropeN_c")
        ropeN_s = cload(p2, "ropeN_s")
        cmpmask = cload(p2, "cmpmask", BF)
        selvalid = cload(p2, "selvalid")
        seladd = cload(p2, "seladd")
        selkeep = cload(p2, "selkeep")
        def qb_pair(g):
            t = kb.sb(p2, [128, 512], BF, f"QB{g}")
            return (t, T(t.t))
        QB = [Rot([qb_pair(g) for _ in range(2)]) for g in range(2)]
        qr = Rot([kb.sb(p2, [128, 4, 128], BF, "qr") for _ in range(2)])
        gat = Rot([kb.sb(p2, [128, 24], F32, "gat") for _ in range(2)])
        rt = Rot([kb.sb(p2, [128, 8, 8], F32, "rt") for _ in range(8)])
        PTs = Rot([kb.sb(p2, [128, 512], BF, "PT") for _ in range(4)])
        sm = Rot([kb.sb(p2, [128, 16], F32, "sm") for _ in range(12)])
        impt = Rot([kb.sb(p2, [128, 64], F32, "imp") for _ in range(8)])
        m8 = Rot([kb.sb(p2, [128, 8], F32, "m8") for _ in range(4)])
        ZS = [kb.sb(p2, [128, 128], BF, f"ZS{g}") for g in range(2)]
        for g in range(2):
            kb.V(lambda e: e.memset(ZS[g][:], 0.0), [ZS[g]], [])
        oacc = Rot([kb.sb(p2, [128, 8, 64], F32, "oacc") for _ in range(2)])
        otmp = Rot([kb.sb(p2, [128, 4, 64], F32, "otmp") for _ in range(3)])
        oab = Rot([kb.sb(p2, [128, 512], BF, "oab") for _ in range(2)])

        def coef_from(rs_ap, rs_t, gate_ap, gate_t):
            c = sm.next()
            kb.V(lambda e: e.tensor_scalar(out=c[:, 0:4], in0=rs_ap, scalar1=1e-30, scalar2=None, op0=ALU.max),
                 [c], [rs_t])
            kb.V(lambda e: e.reciprocal(out=c[:, 4:8], in_=c[:, 0:4]), [c], [c])
            kb.V(lambda e: e.tensor_tensor(out=c[:, 8:12], in0=c[:, 4:8], in1=gate_ap, op=ALU.mult), [c], [c, gate_t])
            return c

        def accumulate(oa, g, first, ps_t, o_view, coef):
            cb = coef[:, 8:12].unsqueeze(2).to_broadcast([128, 4, 64])
            if first:
                kb.V(lambda e: e.tensor_tensor(out=oa[:, 4 * g:4 * g + 4, :], in0=o_view, in1=cb, op=ALU.mult),
                     [oa], [ps_t, coef])
            else:
                tm = otmp.next()
                kb.V(lambda e: e.tensor_tensor(out=tm[:], in0=o_view, in1=cb, op=ALU.mult), [tm], [ps_t, coef])
                kb.V(lambda e: e.tensor_tensor(out=oa[:, 4 * g:4 * g + 4, :], in0=oa[:, 4 * g:4 * g + 4, :],
                                               in1=tm[:], op=ALU.add), [oa], [oa, tm])

        for o in range(NO):
            blk = OB0 + o
            if o == 0:
                pf2 = Prefetch([xl_blk[OB0 + o_] for o_ in range(NO)], nmw)
            hT = pf2.cur()
            pq, pg = PSS.next(), PSS.next()
            for c in range(8):
                kb.mm(pq, pq[:, 0:512], hT, hT[:, c, :], Wq, Wq[:, c, 0:512], c == 0, c == 7)
            for c in range(8):
                kb.mm(pg, pg[:, 0:24], hT, hT[:, c, :], Wq, Wq[:, c, 512:536], c == 0, c == 7)
            pf2.prefetch()
            ga = gat.next()
            kb.A(lambda e: e.activation(out=ga[:], in_=pg[:, 0:24], func=AF.Sigmoid), [ga], [pg])
            gav = ga[:].rearrange("p (h b) -> p h b", b=3)
            q_ = qr.next()
            src4 = pq[:, 0:512].rearrange("p (j hp d) -> p j hp d", j=2, hp=4)
            dst4 = q_[:].rearrange("p hp (j d) -> p j hp d", j=2)
            cb = ropeN_c[:, blk, :].unsqueeze(1).unsqueeze(1).to_broadcast([128, 2, 4, 8])
            sb_ = ropeN_s[:, blk, :].unsqueeze(1).unsqueeze(1).to_broadcast([128, 2, 4, 8])
            x1, x2 = src4[:, :, :, 0:8], src4[:, :, :, 8:16]
            t1, t2, t3, t4 = rt.next(), rt.next(), rt.next(), rt.next()
            tv = lambda t: t[:].rearrange("p (j hp) d -> p j hp d", j=2)
            kb.V(lambda e: e.tensor_tensor(out=tv(t1), in0=x1, in1=cb, op=ALU.mult), [t1], [pq, ropeN_c])
            kb.V(lambda e: e.tensor_tensor(out=tv(t2), in0=x2, in1=sb_, op=ALU.mult), [t2], [pq, ropeN_s])
            kb.V(lambda e: e.tensor_tensor(out=tv(t3), in0=x1, in1=sb_, op=ALU.mult), [t3], [pq, ropeN_s])
            kb.V(lambda e: e.tensor_tensor(out=tv(t4), in0=x2, in1=cb, op=ALU.mult), [t4], [pq, ropeN_c])
            kb.V(lambda e: e.tensor_tensor(out=dst4[:, :, :, 0:8], in0=tv(t1), in1=tv(t2), op=ALU.subtract), [q_], [t1, t2])
            kb.V(lambda e: e.tensor_tensor(out=dst4[:, :, :, 8:16], in0=tv(t3), in1=tv(t4), op=ALU.add), [q_], [t3, t4])
            kb.A(lambda e: e.copy(out=dst4[:, :, :, 16:64], in_=src4[:, :, :, 16:64]), [q_], [pq])
            psb = psbs.next()
            for h in range(4):
                kb.tr(psb, psb[:, h * 128:(h + 1) * 128], q_, q_[:, h, :], identb, identb[:])
            qpair = [QB[0].next(), QB[1].next()]
            qq = [qpair[0][0], qpair[1][0]]
            qbias = [qpair[0][1], qpair[1][1]]
            kb.V(lambda e: e.tensor_copy(out=qq[0][0:64, :], in_=psb[0:64, 0:512]), [qq[0]], [psb])
            kb.V(lambda e: e.tensor_copy(out=qq[1][64:128, :], in_=psb[64:128, 0:512]), [qq[1]], [psb])
            oa = oacc.next()
            P4V = lambda t: t[:].rearrange("p (h q) -> p h q", h=4)

            def fin_cmp(g, pso):
                pv4 = pso[:].rearrange("p (h x) -> p h x", h=4)
                rs = sm.next()
                kb.V(lambda e: e.tensor_reduce(out=rs[:, 0:4], in_=pv4[:, :, 64:128], axis=AX.X, op=ALU.add),
                     [rs], [pso])
                cf = coef_from(rs[:, 0:4], rs, gav[:, 4 * g:4 * g + 4, 0], ga)
                accumulate(oa, g, True, pso, pv4[:, :, 0:64], cf)
                im = impt.next()
                kb.V(lambda e: e.tensor_scalar(out=im[:], in0=pv4[:, 0, 64:128], scalar1=cf[:, 4:5], scalar2=None,
                                               op0=ALU.mult), [im], [pso, cf])
                for h in range(1, 4):
                    kb.V(lambda e: e.scalar_tensor_tensor(out=im[:], in0=pv4[:, h, 64:128], scalar=cf[:, 4 + h:5 + h],
                                                          in1=im[:], op0=ALU.mult, op1=ALU.add), [im], [pso, cf, im])
                imf = impt.next()
                kb.V(lambda e: e.tensor_tensor(out=imf[:], in0=im[:], in1=selvalid[:, o, :], op=ALU.mult),
                     [imf], [im, selvalid])
                kb.V(lambda e: e.tensor_tensor(out=imf[:], in0=imf[:], in1=seladd[:, o, :], op=ALU.add),
                     [imf], [imf, seladd])
                ma, mb = m8.next(), m8.next()
                im2 = impt.next()
                kb.V(lambda e: e.max(out=ma[:], in_=imf[:]), [ma], [imf])
                kb.V(lambda e: e.match_replace(out=im2[:], in_to_replace=ma[:], in_values=imf[:], imm_value=-1e9),
                     [im2], [ma, imf])
                kb.V(lambda e: e.max(out=mb[:], in_=im2[:]), [mb], [im2])
                sel = impt.next()
                kb.V(lambda e: e.tensor_scalar(out=sel[:], in0=imf[:], scalar1=mb[:, 7:8], scalar2=None,
                                               op0=ALU.is_ge), [sel], [imf, mb])
                kb.V(lambda e: e.tensor_tensor(out=sel[:], in0=sel[:], in1=selkeep[:, o, :], op=ALU.mult),
                     [sel], [sel, selkeep])
                zc = slice(64, 128) if g == 0 else slice(0, 64)
                kb.V(lambda e: e.tensor_scalar(out=ZS[g][:, zc], in0=sel[:], scalar1=1.0, scalar2=1e5,
                                               op0=ALU.subtract, op1=ALU.mult), [ZS[g]], [sel])

            def put_bias(g):
                zc = slice(64, 128) if g == 0 else slice(0, 64)
                pz = psbs.next()
                kb.tr(pz, pz[:, 0:128], ZS[g], ZS[g][:], identb, identb[:])
                kb.V(lambda e: e.tensor_copy(
                    out=qbias[g][zc, :].rearrange("p (h q) -> p h q", h=4),
                    in_=pz[zc, 0:128].unsqueeze(1).to_broadcast([64, 4, 128])),
                    [qbias[g]], [pz])

            def fin_soft(g, ps_t, br):
                pv4 = ps_t[:, 0:264].rearrange("p (h x) -> p h x", h=4)
                cf = coef_from(pv4[:, :, 64], ps_t, gav[:, 4 * g:4 * g + 4, br], ga)
                accumulate(oa, g, False, ps_t, pv4[:, :, 0:64], cf)

            jobs = []
            for g in range(2):
                rows = slice(64 * g, 64 * g + 64)
                for ct in range(2):
                    jobs.append(dict(kind="cmp", g=g, first=ct == 0, last=ct == 1, w=128,
                                     lhsT=(KCT, KCT[rows, ct * 128:(ct + 1) * 128]), rhs=(qq[g], qq[g][rows, :]), extra=(),
                                     bias=None, mask=(cmpmask, cmpmask[:, ct, o, :]), v=(VCA, VCA[:, g, ct, :])))
            for g in range(2):
                rows = slice(64 * g, 64 * g + 64)
                for i, kt in enumerate(range(blk - 4, blk + 1)):
                    mk = (upmask, upmask[:]) if i == 0 else ((lowmask, lowmask[:]) if i == 4 else None)
                    jobs.append(dict(kind="win", g=g, first=i == 0, last=i == 4, w=66,
                                     lhsT=(KW, KW[rows, kt * 128:(kt + 1) * 128]), rhs=(qq[g], qq[g][rows, :]), extra=(),
                                     bias=pbias if kt < 16 else None, mask=mk, v=(VW, VW[:, kt, g, :]),
                                     bias_after=(i == 2)))
            for g in range(2):
                for kt in range(blk + 1):
                    mk = (lowmask, lowmask[:]) if kt == blk else None
                    jobs.append(dict(kind="slc", g=g, first=kt == 0, last=kt == blk, w=66,
                                     lhsT=(KE[g], KE[g][:, kt * 128:(kt + 1) * 128]), rhs=(qq[g], qpair[g][0][:, :]),
                                     extra=(qbias[g],), bias=None, mask=mk, v=(VS, VS[:, kt, g, :])))

            def stageA(j):
                pS = PSS.next()
                kb.mm(pS, pS[:, :], j["lhsT"][0], j["lhsT"][1], j["rhs"][0], j["rhs"][1], True, True, extra=j["extra"])
                pt = PTs.next()
                if j["bias"] is None:
                    kb.A(lambda e: e.activation(out=pt[:], in_=pS[:], func=AF.Exp, scale=0.125), [pt], [pS])
                else:
                    bt = j["bias"]
                    kb.A(lambda e: e.activation(out=pt[:], in_=pS[:], func=AF.Exp, scale=0.125, bias=bt[:, 0:1]),
                         [pt], [pS, bt])
                if j["mask"] is not None:
                    mt_, map_ = j["mask"]
                    kb.V(lambda e: e.tensor_tensor(out=P4V(pt), in0=P4V(pt),
                                                   in1=map_.unsqueeze(1).to_broadcast([128, 4, 128]), op=ALU.mult),
                         [pt], [pt, mt_])
                j["pt"] = pt

            acc = {}

            def stageC(j):
                key = (j["kind"], j["g"])
                if j["first"]:
                    acc[key] = PSA.next()
                pa_ = acc[key]
                w, pt = j["w"], j["pt"]
                for h in range(4):
                    kb.mm(pa_, pa_[:, h * w:(h + 1) * w], pt, pt[:, h * 128:(h + 1) * 128], j["v"][0], j["v"][1],
                          j["first"] and h == 0, j["last"] and h == 3, skip=True)
                if j.get("bias_after"):
                    put_bias(j["g"])
                if j["last"]:
                    if j["kind"] == "cmp":
                        fin_cmp(j["g"], pa_)
                    else:
                        fin_soft(j["g"], pa_, 2 if j["kind"] == "win" else 1)

            LA = 3
            for i in range(len(jobs) + LA):
                if i < len(jobs):
                    stageA(jobs[i])
                if i >= LA:
                    stageC(jobs[i - LA])

            ob = oab.next()
            kb.V(lambda e: e.tensor_copy(out=ob[:], in_=oa[:].rearrange("p h d -> p (h d)")), [ob], [oa])
            if o == 1:
                dump("oa", oa, oa[:], [128, 8, 64])
            psb = psbs.next()
            for c in range(4):
                kb.tr(psb, psb[:, c * 128:(c + 1) * 128], ob, ob[:, c * 128:(c + 1) * 128], identb, identb[:])
            kb.V(lambda e: e.tensor_copy(out=OAT[:, :, o * 128:(o + 1) * 128],
                                         in_=psb[:, 0:512].rearrange("p (c t) -> p c t", c=4)), [OAT], [psb])
        kb.barrier()
        live.remove(p2)
    nsa.close()
    live.remove(nsa)
    ck(2)

    ORT = kb.sb(mixs, [128, 8, NO * 128], BF, "ORT")
    with ExitStack() as p3:
        live.append(p3)
        Wr = kb.sb(p3, [128, 8, 3072], BF, "Wr")
        with Staging() as stg:
            load_w(stg, Wr, lambda c0, n: Wr[:, :, c0:c0 + n], w_in3[:, :, 1304:4376], 8, 3072)
        rtc = Rot([kb.sb(p3, [128, 64], F32, "rtc") for _ in range(2)])
        rts = Rot([kb.sb(p3, [128, 64], F32, "rts") for _ in range(2)])
        cur_rt = {}
        decT = cload(p3, "decT")
        qdecrow = cload(p3, "qdecrow")
        kdec = cload(p3, "kdec")
        gnw = cload(p3, "gnw")
        ST = kb.sb(p3, [128, 4, 256], F32, "ST")
        STb = kb.sb(p3, [128, 4, 256], BF, "STb")
        kb.V(lambda e: e.memset(ST[:], 0.0), [ST], [])
        kb.V(lambda e: e.memset(STb[:], 0.0), [STb], [])
        rr = Rot([kb.sb(p3, [128, 4, 64], F32, "rr") for _ in range(4)])
        krs = Rot([kb.sb(p3, [128, 4, 128], F32, "kr") for _ in range(3)])
        KDs = Rot([kb.sb(p3, [128, 512], BF, "KD") for _ in range(2)])
        VBs = Rot([kb.sb(p3, [128, 1024], BF, "VB") for _ in range(2)])
        qkb = Rot([kb.sb(p3, [128, 2, 512], BF, "qkb") for _ in range(2)])
        QKT = Rot([kb.sb(p3, [128, 8, 128], BF, "QKT") for _ in range(2)])
        QTd = Rot([kb.sb(p3, [128, 4, 128], BF, "QTd") for _ in range(2)])
        SCs = Rot([kb.sb(p3, [128, 4, 128], BF, "SC") for _ in range(2)])
        sgs = Rot([kb.sb(p3, [128, 1024], F32, "sg") for _ in range(1)])
        ys = Rot([kb.sb(p3, [128, 1024], F32, "y") for _ in range(1)])
        orb = Rot([kb.sb(p3, [128, 1024], BF, "orb") for _ in range(2)])
        bst = Rot([kb.sb(p3, [128, 4, 6], F32, "bst") for _ in range(2)])
        mv = Rot([kb.sb(p3, [128, 4, 4], F32, "mv") for _ in range(2)])

        def rope_full(dst3, dst_t, pt, blk):
            src3 = pt[:, 0:512].rearrange("p (h d) -> p h d", h=4)
            if cur_rt.get("blk") != blk:
                ropeR_c, ropeR_s = rtc.next(), rts.next()
                kb.dma(SP, ropeR_c[:], D["ropeR_c"][:, blk, :], outs=[ropeR_c])
                kb.dma(SP, ropeR_s[:], D["ropeR_s"][:, blk, :], outs=[ropeR_s])
                cur_rt.update(blk=blk, c=ropeR_c, s=ropeR_s)
            ropeR_c, ropeR_s = cur_rt["c"], cur_rt["s"]
            cb = ropeR_c[:].unsqueeze(1).to_broadcast([128, 4, 64])
            sb_ = ropeR_s[:].unsqueeze(1).to_broadcast([128, 4, 64])
            x1, x2 = src3[:, :, 0:64], src3[:, :, 64:128]
            t1, t2 = rr.next(), rr.next()
            kb.V(lambda e: e.tensor_tensor(out=t1[:], in0=x1, in1=cb, op=ALU.mult), [t1], [pt, ropeR_c])
            kb.V(lambda e: e.tensor_tensor(out=t2[:], in0=x2, in1=sb_, op=ALU.mult), [t2], [pt, ropeR_s])
            kb.V(lambda e: e.tensor_tensor(out=dst3[:, :, 0:64], in0=t1[:], in1=t2[:], op=ALU.subtract),
                 [dst_t], [t1, t2])
            t3, t4 = rr.next(), rr.next()
            kb.V(lambda e: e.tensor_tensor(out=t3[:], in0=x1, in1=sb_, op=ALU.mult), [t3], [pt, ropeR_s])
            kb.V(lambda e: e.tensor_tensor(out=t4[:], in0=x2, in1=cb, op=ALU.mult), [t4], [pt, ropeR_c])
            kb.V(lambda e: e.tensor_tensor(out=dst3[:, :, 64:128], in0=t3[:], in1=t4[:], op=ALU.add),
                 [dst_t], [t3, t4])

        pf3 = Prefetch([xl_blk[b_] for b_ in range(NB)], nmw)
        for blk in range(NB):
            o = blk - OB0
            hT = pf3.cur()
            pk = PSALL.next()
            for c in range(8):
                kb.mm(pk, pk[:, :], hT, hT[:, c, :], Wr, Wr[:, c, 512:1024], c == 0, c == 7)
            kr = krs.next()
            rope_full(kr[:], kr, pk, blk)
            KD = KDs.next()
            kb.V(lambda e: e.tensor_tensor(out=KD[:].rearrange("p (h d) -> p h d", h=4), in0=kr[:],
                                           in1=kdec[:].unsqueeze(2).to_broadcast([128, 4, 128]), op=ALU.mult),
                 [KD], [kr, kdec])
            VB = VBs.next()
            for n in range(2):
                pvv = PSALL.next()
                for c in range(8):
                    kb.mm(pvv, pvv[:, :], hT, hT[:, c, :], Wr, Wr[:, c, 1024 + n * 512:1536 + n * 512], c == 0, c == 7)
                kb.A(lambda e: e.copy(out=VB[:, n * 512:(n + 1) * 512], in_=pvv[:]), [VB], [pvv])
            if o < 0:
                pf3.prefetch()
            if o >= 0:
                pq = PSALL.next()
                for c in range(8):
                    kb.mm(pq, pq[:, :], hT, hT[:, c, :], Wr, Wr[:, c, 0:512], c == 0, c == 7)
                qrf = krs.next()
                rope_full(qrf[:], qrf, pq, blk)
                qk = qkb.next()
                kb.A(lambda e: e.copy(out=qk[:, 0, :].rearrange("p (h d) -> p h d", h=4), in_=qrf[:]), [qk], [qrf])
                kb.A(lambda e: e.copy(out=qk[:, 1, :].rearrange("p (h d) -> p h d", h=4), in_=kr[:]), [qk], [kr])
                sg = sgs.next()
                for n in range(2):
                    pgg = PSALL.next()
                    for c in range(8):
                        kb.mm(pgg, pgg[:, :], hT, hT[:, c, :], Wr, Wr[:, c, 2048 + n * 512:2560 + n * 512],
                              c == 0, c == 7)
                    kb.A(lambda e: e.activation(out=sg[:, n * 512:(n + 1) * 512], in_=pgg[:], func=AF.Silu), [sg], [pgg])
                pf3.prefetch()
                psb = psbs.next()
                for j in range(8):
                    kb.tr(psb, psb[:, j * 128:(j + 1) * 128], qk, qk[:, j // 4, (j % 4) * 128:(j % 4 + 1) * 128],
                          identb, identb[:])
                qkt = QKT.next()
                kb.A(lambda e: e.copy(out=qkt[:], in_=psb[:].rearrange("p (j t) -> p j t", j=8)), [qkt], [psb])
                qtd = QTd.next()
                kb.V(lambda e: e.tensor_tensor(out=qtd[:], in0=qkt[:, 0:4, :], in1=qdecrow[:], op=ALU.mult),
                     [qtd], [qkt, qdecrow])
                psc = PSALL.next()
                for h in range(4):
                    kb.mm(psc, psc[:, h * 128:(h + 1) * 128], qkt, qkt[:, 4 + h, :], qkt, qkt[:, h, :], True, True)
                sc = SCs.next()
                kb.V(lambda e: e.tensor_tensor(out=sc[:], in0=psc[:].rearrange("p (h i) -> p h i", h=4), in1=decT[:],
                                               op=ALU.mult), [sc], [psc, decT])
                y = ys.next()
                bs, mvv = bst.next(), mv.next()
                for n in range(2):
                    po = PSALL.next()
                    for hh in range(2):
                        h = 2 * n + hh
                        kb.mm(po, po[:, hh * 256:(hh + 1) * 256], sc, sc[:, h, :], VB, VB[:, h * 256:(h + 1) * 256],
                              True, False)
                        kb.mm(po, po[:, hh * 256:(hh + 1) * 256], qtd, qtd[:, h, :], STb, STb[:, h, :], False, True)
                    for hh in range(2):
                        h = 2 * n + hh
                        kb.V(lambda e: e.bn_stats(out=bs[:, h, :], in_=po[:, hh * 256:(hh + 1) * 256]), [bs], [po])
                        kb.V(lambda e: e.bn_aggr(out=mvv[:, h, 0:2], in_=bs[:, h, :]), [mvv], [bs])
                        kb.V(lambda e: e.tensor_scalar(out=mvv[:, h, 3:4], in0=mvv[:, h, 1:2], scalar1=EPS, scalar2=None,
                                                       op0=ALU.add), [mvv], [mvv])
                        kb.A(lambda e: e.activation(out=mvv[:, h, 3:4], in_=mvv[:, h, 3:4], func=AF.Sqrt), [mvv], [mvv])
                        kb.V(lambda e: e.reciprocal(out=mvv[:, h, 2:3], in_=mvv[:, h, 3:4]), [mvv], [mvv])
                        kb.V(lambda e: e.tensor_scalar(out=y[:, h * 256:(h + 1) * 256], in0=po[:, hh * 256:(hh + 1) * 256],
                                                       scalar1=mvv[:, h, 0:1], scalar2=mvv[:, h, 2:3],
                                                       op0=ALU.subtract, op1=ALU.mult), [y], [po, mvv])
                if o == 1:
                    dump("ret_y", y, y[:], [128, 1024])
                kb.V(lambda e: e.tensor_tensor(out=y[:], in0=y[:], in1=gnw[:], op=ALU.mult), [y], [y, gnw])
                ob = orb.next()
                kb.V(lambda e: e.tensor_tensor(out=ob[:], in0=y[:], in1=sg[:], op=ALU.mult), [ob], [y, sg])
                psb = psbs.next()
                for c in range(8):
                    kb.tr(psb, psb[:, c * 128:(c + 1) * 128], ob, ob[:, c * 128:(c + 1) * 128], identb, identb[:])
                kb.A(lambda e: e.copy(out=ORT[:, :, o * 128:(o + 1) * 128],
                                      in_=psb[:].rearrange("p (c t) -> p c t", c=8)), [ORT], [psb])
            if blk < NB - 1:
                for n in range(2):
                    pu = PSALL.next()
                    for hh in range(2):
                        h = 2 * n + hh
                        kb.mm(pu, pu[:, hh * 256:(hh + 1) * 256], KD, KD[:, h * 128:(h + 1) * 128],
                              VB, VB[:, h * 256:(h + 1) * 256], True, True)
                    for hh in range(2):
                        h = 2 * n + hh
                        kb.V(lambda e: e.scalar_tensor_tensor(out=ST[:, h, :], in0=ST[:, h, :], scalar=GAMMAS[h] ** 128,
                                                              in1=pu[:, hh * 256:(hh + 1) * 256], op0=ALU.mult,
                                                              op1=ALU.add), [ST], [ST, pu])
                kb.A(lambda e: e.copy(out=STb[:], in_=ST[:]), [STb], [ST])
        dump("ORT", ORT, ORT[:], [128, 8, NO * 128])
        kb.barrier()
        live.remove(p3)
    ck(3)

    x1_blk = x1d.rearrange("(b p) f -> b p f", p=128)
    with ExitStack() as p4:
        live.append(p4)
        Wgm = kb.sb(p4, [128, 8, 2048], BF, "Wgm")
        Wnsa = kb.sb(p4, [128, 4, 1024], BF, "Wnsa")
        Wret = kb.sb(p4, [128, 8, 1024], BF, "Wret")
        Wmix = kb.sb(p4, [128, 8, 1024], BF, "Wmix")
        with Staging() as stg:
            load_w(stg, Wgm, lambda c0, n: Wgm[:, :, c0:c0 + n], w_in3[:, :, 4376:6424], 8, 2048)
            load_w(stg, Wnsa, lambda c0, n: Wnsa[:, :, c0:c0 + n], D["w_nsa"].rearrange("(c p) n -> p c n", p=128), 4, 1024)
            load_w(stg, Wret, lambda c0, n: Wret[:, :, c0:c0 + n], D["w_ret"].rearrange("(c p) n -> p c n", p=128), 8, 1024)
            load_w(stg, Wmix, lambda c0, n: Wmix[:, :, c0:c0 + n], D["w_mix"].rearrange("(c p) n -> p c n", p=128), 8, 1024)
        GMs = Rot([kb.sb(p4, [128, 16, 128], F32, "GM") for _ in range(2)])
        MTs = Rot([kb.sb(p4, [128, 8, 128], BF, "MT") for _ in range(2)])
        mt1 = Rot([kb.sb(p4, [128, 512], F32, "mt1") for _ in range(2)])
        mt2 = Rot([kb.sb(p4, [128, 512], F32, "mt2") for _ in range(2)])
        x1s = Rot([kb.sb(p4, [128, 1024], F32, "x1") for _ in range(2)])
        pf4 = Prefetch([xl_blk[OB0 + o_] for o_ in range(NO)], nmw, want_xt=True)
        for o in range(NO):
            blk = OB0 + o
            hT, xt, _ = pf4.cur()
            gm = GMs.next()
            for q4 in range(4):
                pgm = PSALL.next()
                for j in range(4):
                    mc = q4 * 4 + j
                    for c in range(8):
                        kb.mm(pgm, pgm[:, j * 128:(j + 1) * 128], Wgm, Wgm[:, c, mc * 128:(mc + 1) * 128],
                              hT, hT[:, c, :], c == 0, c == 7)
                kb.A(lambda e: e.activation(out=gm[:, q4 * 4:q4 * 4 + 4, :].rearrange("p a t -> p (a t)"),
                                            in_=pgm[:], func=AF.Sigmoid), [gm], [pgm])
            pf4.prefetch()
            mt = MTs.next()
            for half in range(2):
                pya, pyb = PSALL.next(), PSALL.next()
                for j in range(4):
                    fc = half * 4 + j
                    for c in range(4):
                        kb.mm(pya, pya[:, j * 128:(j + 1) * 128], Wnsa, Wnsa[:, c, fc * 128:(fc + 1) * 128],
                              OAT, OAT[:, c, o * 128:(o + 1) * 128], c == 0, c == 3)
                    for c in range(8):
                        kb.mm(pyb, pyb[:, j * 128:(j + 1) * 128], Wret, Wret[:, c, fc * 128:(fc + 1) * 128],
                              ORT, ORT[:, c, o * 128:(o + 1) * 128], c == 0, c == 7)
                a1, a2 = mt1.next(), mt2.next()
                kb.V(lambda e: e.tensor_tensor(out=a1[:], in0=pya[:],
                                               in1=gm[:, half * 4:half * 4 + 4, :].rearrange("p a t -> p (a t)"),
                                               op=ALU.mult), [a1], [pya, gm])
                kb.V(lambda e: e.tensor_tensor(out=a2[:], in0=pyb[:],
                                               in1=gm[:, 8 + half * 4:12 + half * 4, :].rearrange("p a t -> p (a t)"),
                                               op=ALU.mult), [a2], [pyb, gm])
                kb.V(lambda e: e.tensor_tensor(out=mt[:, half * 4:half * 4 + 4, :].rearrange("p a t -> p (a t)"),
                                               in0=a1[:], in1=a2[:], op=ALU.add), [mt], [a1, a2])
            x1 = x1s.next()
            for n in range(2):
                pm = PSALL.next()
                for fc in range(8):
                    kb.mm(pm, pm[:, :], mt, mt[:, fc, :], Wmix, Wmix[:, fc, n * 512:(n + 1) * 512], fc == 0, fc == 7)
                kb.V(lambda e: e.tensor_tensor(out=x1[:, n * 512:(n + 1) * 512], in0=pm[:],
                                               in1=xt[:, n * 512:(n + 1) * 512], op=ALU.add), [x1], [pm, xt])
            kb.dma(SP, x1_blk[o], x1[:], ins=[x1])
            if o == 1:
                dump("x1", x1, x1[:], [128, 1024])
        kb.barrier()
        live.remove(p4)
    mixs.close()
    live.remove(mixs)
    ck(4)

    with ExitStack() as p5:
        live.append(p5)
        Wup = kb.sb(p5, [128, 8, 5632], BF, "Wup")
        Wdn = kb.sb(p5, [128, 22, 1024], BF, "Wdn")
        with Staging() as stg:
            load_w(stg, Wup, lambda c0, n: Wup[:, :, c0:c0 + n], D["w_up"].rearrange("(c p) n -> p c n", p=128), 8, 5632)
            load_w(stg, Wdn, lambda c0, n: Wdn[:, :, c0:c0 + n], D["w_dn"].rearrange("(c p) n -> p c n", p=128), 22, 1024)
        convw = cload(p5, "convw")
        convb = cload(p5, "convb")
        nlw = cload(p5, "nlw")
        uh = kb.sb(p5, [128, 44, 2], F32, "uh")
        kb.V(lambda e: e.memset(uh[:], 0.0), [uh], [])
        uhc = [T(uh.t) for _ in range(44)]
        for t_ in uhc:
            t_.w = uh.w
        PW = 256
        hT2s = Rot([kb.sb(p5, [128, 8, PW], BF, "hT2") for _ in range(2)])
        uts = Rot([kb.sb(p5, [128, PW + 2], F32, "ut") for _ in range(4)])
        cvs = Rot([kb.sb(p5, [128, PW], F32, "cv") for _ in range(4)])
        cvg = Rot([kb.sb(p5, [128, PW], F32, "cvg") for _ in range(3)])
        ATs = Rot([kb.sb(p5, [128, 22, PW], BF, "AT") for _ in range(1)])
        x2s = Rot([kb.sb(p5, [128, 1024], F32, "x2") for _ in range(1)])
        ys5 = Rot([kb.sb(p5, [128, 1024], F32, "y5") for _ in range(2)])
        out_blk = out_d.rearrange("(b p) f -> b p f", p=128)
        groups = [[0]] + [[1 + 2 * i, 2 + 2 * i] for i in range(8)]
        for grp in groups:
            N = 128 * len(grp)
            hT2 = hT2s.next()
            for j, o in enumerate(grp):
                norm_T(x1_blk[o], nfw, dst=(hT2, hT2[:, :, j * 128:(j + 1) * 128]))
            at = ATs.next()
            atc = [T(at.t) for _ in range(22)]
            for t_ in atc:
                t_.w, t_.r = at.w, dict(at.r)

            def conv_chunk(ch):
                pu = PSS.next()
                for c in range(8):
                    kb.mm(pu, pu[:, 0:N], Wup, Wup[:, c, ch * 128:(ch + 1) * 128], hT2, hT2[:, c, 0:N], c == 0, c == 7)
                ut = uts.next()
                cv = cvs.next()
                kb.A(lambda e: e.copy(out=ut[:, 2:2 + N], in_=pu[:, 0:N]), [ut], [pu])
                kb.A(lambda e: e.activation(out=cv[:, 0:N], in_=pu[:, 0:N], func=AF.Identity,
                                            scale=convw[:, 2, ch:ch + 1], bias=convb[:, ch:ch + 1]),
                     [cv], [pu, convw, convb])
                kb.G(lambda e: e.tensor_copy(out=ut[:, 0:2], in_=uh[:, ch, :]), [ut], [uhc[ch]])
                kb.G(lambda e: e.tensor_copy(out=uh[:, ch, :], in_=ut[:, N:N + 2]), [uhc[ch]], [ut])
                kb.V(lambda e: e.scalar_tensor_tensor(out=cv[:, 0:N], in0=ut[:, 1:1 + N], scalar=convw[:, 1, ch:ch + 1],
                                                      in1=cv[:, 0:N], op0=ALU.mult, op1=ALU.add), [cv], [ut, convw, cv])
                kb.V(lambda e: e.scalar_tensor_tensor(out=cv[:, 0:N], in0=ut[:, 0:N], scalar=convw[:, 0, ch:ch + 1],
                                                      in1=cv[:, 0:N], op0=ALU.mult, op1=ALU.add), [cv], [ut, convw, cv])
                return cv

            for c22 in range(22):
                cg_ = conv_chunk(c22)
                sgt = cvg.next()
                kb.A(lambda e: e.activation(out=sgt[:, 0:N], in_=cg_[:, 0:N], func=AF.Silu), [sgt], [cg_])
                cvv = conv_chunk(22 + c22)
                kb.V(lambda e: e.tensor_tensor(out=at[:, c22, 0:N], in0=sgt[:, 0:N], in1=cvv[:, 0:N], op=ALU.mult),
                     [atc[c22]], [sgt, cvv])
            def fold_at():
                for t_ in atc:
                    deps = list(t_.r.items()) + ([t_.w] if t_.w is not None else [])
                    for sem_, v_ in deps:
                        if at.r.get(sem_, 0) < v_:
                            at.r[sem_] = v_
            if grp == [0]:
                fold_at()
                continue
            for j, o in enumerate(grp):
                xt = xts.next()
                kb.dma(SP, xt[:], x1_blk[o], outs=[xt])
                x2 = x2s.next()
                for n in range(2):
                    pd = PSA.next()
                    for c in range(22):
                        kb.mm(pd, pd[:, :], atc[c], at[:, c, j * 128:(j + 1) * 128], Wdn, Wdn[:, c, n * 512:(n + 1) * 512],
                              c == 0, c == 21)
                    kb.V(lambda e: e.tensor_tensor(out=x2[:, n * 512:(n + 1) * 512], in0=pd[:],
                                                   in1=xt[:, n * 512:(n + 1) * 512], op=ALU.add), [x2], [pd, xt])
                st = stat.next()
                kb.V(lambda e: e.scalar_tensor_tensor(out=sq_junk[:], in0=x2[:], scalar=1.0, in1=x2[:], op0=ALU.mult,
                                                      op1=ALU.mult, accum_out=st[:, 0:1]), [sq_junk, st], [x2])
                kb.V(lambda e: e.tensor_scalar(out=st[:, 1:2], in0=st[:, 0:1], scalar1=1.0 / 1024, scalar2=EPS,
                                               op0=ALU.mult, op1=ALU.add), [st], [st])
                kb.A(lambda e: e.activation(out=st[:, 3:4], in_=st[:, 1:2], func=AF.Sqrt), [st], [st])
                kb.V(lambda e: e.reciprocal(out=st[:, 2:3], in_=st[:, 3:4]), [st], [st])
                y5 = ys5.next()
                kb.V(lambda e: e.scalar_tensor_tensor(out=y5[:], in0=x2[:], scalar=st[:, 2:3], in1=nlw[:], op0=ALU.mult,
                                                      op1=ALU.mult), [y5], [x2, st, nlw])
                kb.dma(SP, out_blk[o - 1], y5[:], ins=[y5])
            fold_at()
        kb.barrier()
        live.remove(p5)
    return dbg_out


def finish(kb, top, stacks, dbg_out):
    kb.barrier()
    for s in stacks:
        s.close()
    top.close()
    return kb.nc, dbg_out


def _consts(s):
    f = np.float32
    c = {}
    c["ident"] = np.eye(128, dtype=f)
    tloc = np.arange(4096)
    pos = np.where(tloc < 2048, tloc, 2048 * s + tloc - 2048).astype(np.float64)
    if s == 0:
        pos[:2048] = 0.0

    def rope_tab(theta, half):
        inv = np.power(np.float32(theta), -np.arange(half, dtype=np.float32) / half).astype(np.float32)
        ang = pos.astype(np.float32)[:, None] * inv[None, :]
        cs, sn = np.cos(ang).astype(f), np.sin(ang).astype(f)
        to = lambda a: np.ascontiguousarray(a.reshape(32, 128, half).transpose(1, 0, 2))
        return to(cs), to(sn)
    c["ropeN_c"], c["ropeN_s"] = rope_tab(500000.0, 8)
    c["ropeR_c"], c["ropeR_s"] = rope_tab(10000.0, 64)
    c["expand"] = (np.arange(4096)[None, :] // 64 == np.arange(64)[:, None]).astype(f)
    cst = np.arange(256) * 16
    sst = np.arange(64) * 64
    ov = np.clip(np.minimum(cst[:, None] + 32, sst[None, :] + 64) - np.maximum(cst[:, None], sst[None, :]), 0, None) / 32.0
    ov[255] = 0.0
    c["ov"] = np.ascontiguousarray(ov.reshape(2, 128, 64).transpose(1, 0, 2)).astype(f)
    lo_c = 0 if s == 1 else 128
    lo_j = 0 if s == 1 else 32
    tq = (OB0 * 128 + np.arange(NO * 128))
    cm = (16 * np.arange(256)[:, None] + 31 <= tq[None, :]) & (np.arange(256)[:, None] >= lo_c) & (np.arange(256)[:, None] < 255)
    c["cmpmask"] = np.ascontiguousarray(cm.reshape(2, 128, NO, 128).transpose(1, 0, 2, 3)).astype(f)
    cur = tq // 64
    jb = np.arange(64)[None, :]
    valid = (jb <= cur[:, None]) & (jb >= lo_j)
    forced = ((jb == lo_j) | (jb == cur[:, None]) | (jb == cur[:, None] - 1)) & valid
    add = np.where(forced, 1e4 + jb, np.where(valid, 0.0, -1e4 - jb))
    vm = (valid & ~forced)
    c["selvalid"] = np.ascontiguousarray(vm.reshape(NO, 128, 64).transpose(1, 0, 2)).astype(f)
    c["seladd"] = np.ascontiguousarray(add.reshape(NO, 128, 64).transpose(1, 0, 2)).astype(f)
    c["selkeep"] = np.ascontiguousarray(valid.reshape(NO, 128, 64).transpose(1, 0, 2)).astype(f)
    k_, q_ = np.arange(128)[:, None], np.arange(128)[None, :]
    c["lowmask"] = (k_ <= q_).astype(f)
    c["upmask"] = (k_ > q_).astype(f)
    c["pbias"] = np.full((128, 1), 0.0 if s == 1 else -30000.0, dtype=f)
    g = np.array(GAMMAS, dtype=np.float64)
    i_ = np.arange(128)
    dk = 128 ** -0.5
    dec = np.where(i_[None, :] >= i_[:, None], g[:, None, None] ** np.maximum(i_[None, :] - i_[:, None], 0)[None], 0.0)
    c["decT"] = np.ascontiguousarray((dec * dk).transpose(1, 0, 2)).astype(f)
    c["qdecrow"] = np.ascontiguousarray(np.broadcast_to((g[:, None] ** (i_[None, :] + 1.0))[None], (128, 4, 128))).astype(f)
    c["kdec"] = np.ascontiguousarray((g[None, :] ** (127.0 - i_[:, None])) * dk).astype(f)
    return c


def kernel(x, norm_mix_w, w_in, cmp_pe_k, cmp_w1_k, cmp_w2_k, cmp_pe_v, cmp_w1_v, cmp_w2_v,
           w_nsa_branch, ret_gn_w, w_ret_branch, w_mix_out, norm_ffn_w, w_ffn_up, ffn_conv_w,
           ffn_conv_b, w_ffn_down, norm_final_w, _build_kwargs=None, _return_all=False):
    f = np.float32
    A = lambda a: np.ascontiguousarray(np.asarray(a, dtype=f))
    x = A(x)
    shared = {
        "w_in": A(w_in)[0], "cmp_w1_k": A(cmp_w1_k)[0], "cmp_w2_k": A(cmp_w2_k)[0], "cmp_w1_v": A(cmp_w1_v)[0],
        "cmp_w2_v": A(cmp_w2_v)[0], "w_nsa": A(w_nsa_branch)[0], "w_ret": A(w_ret_branch)[0], "w_mix": A(w_mix_out)[0],
        "w_up": A(w_ffn_up)[0], "w_dn": A(w_ffn_down)[0],
        "nmw": np.ascontiguousarray(A(norm_mix_w)[0].reshape(8, 128).T),
        "nfw": np.ascontiguousarray(A(norm_ffn_w)[0].reshape(8, 128).T),
        "nlw": np.ascontiguousarray(np.broadcast_to(A(norm_final_w)[None, :], (128, 1024))),
        "gnw": np.ascontiguousarray(np.broadcast_to(A(ret_gn_w)[0].reshape(1, 1024), (128, 1024))),
        "convw": np.ascontiguousarray(A(ffn_conv_w)[0].reshape(3, 44, 128).transpose(2, 0, 1)),
        "convb": np.ascontiguousarray(A(ffn_conv_b)[0].reshape(44, 128).T),
        "peTk": np.ascontiguousarray(np.concatenate([A(cmp_pe_k)[0].T] * 2, axis=0)),
        "peTv": np.ascontiguousarray(np.concatenate([A(cmp_pe_v)[0].T] * 2, axis=0)),
    }
    consts = [_consts(0), _consts(1)]
    nc, dbg = build(**(_build_kwargs or {}))
    in_maps = []
    for core in range(8):
        b, s = core // 2, core % 2
        xl = np.zeros((4096, 1024), dtype=f)
        if s == 1:
            xl[:] = x[b]
        else:
            xl[2048:] = x[b, :2048]
        m = dict(shared)
        m["xl"] = xl
        cs = consts[s]
        for k_, v in cs.items():
            if not k_.startswith("_"):
                m[k_] = v
        in_maps.append(m)
    res = run_bass_kernel_spmd(nc, in_maps, core_ids=list(range(8)))
    if _return_all:
        return res
    out = np.zeros((4, 4096, 1024), dtype=f)
    for core in range(8):
        b, s = core // 2, core % 2
        out[b, 2048 * s:2048 * (s + 1)] = res.results[core]["out"]
    return out
```
